# Optimizing a Trainium2 kernel written in Bass

```python
import math, functools
import jax, jax.numpy as jnp
from jax import lax
import numpy as np

D_MODEL = 1024
BATCH = 32
SEQ = 2048
DEPTH = 2

GRID_W = 64
CTX_LEN = 256
HEAD_DIM = 64
D_MIX = D_MODEL
N_GROUPS = 4
GROUP_W = D_MIX // N_GROUPS
N_HEADS_G = GROUP_W // HEAD_DIM
CHUNK = 128
ROPE_PAIRS = HEAD_DIM // 4
ROPE_BASE = 10000.0
RWKV_W_LORA = 32
RWKV_A_LORA = 32
RWKV_G_LORA = 64
RWKV_SHIFT_W = 3 * GROUP_W + RWKV_W_LORA + RWKV_A_LORA + RWKV_G_LORA
RWKV_SPLIT = (GROUP_W, 2 * GROUP_W, 3 * GROUP_W, 3 * GROUP_W + RWKV_W_LORA, 3 * GROUP_W + RWKV_W_LORA + RWKV_A_LORA)
HYENA_EMB = 33
HYENA_BANDS = (HYENA_EMB - 1) // 2
HYENA_FFN = 64
HYENA_ORDER = 2
HYENA_FAST_DECAY = 0.3
HYENA_SLOW_DECAY = 1.5
HYENA_TARGET = 1e-2
D_FF = 2816
DEEPNORM_ALPHA = (2 * DEPTH) ** 0.25
DEEPNORM_BETA = (8 * DEPTH) ** -0.25
LN_EPS = 1e-5
GN_EPS = 1e-5
SPLIT_SIZES = (GROUP_W, GROUP_W, GROUP_W, GROUP_W, 4 * N_HEADS_G,
               GROUP_W, GROUP_W, GROUP_W,
               GROUP_W, GROUP_W, GROUP_W, GROUP_W,
               GROUP_W, GROUP_W, GROUP_W, RWKV_W_LORA, RWKV_A_LORA, RWKV_G_LORA)
D_PROJ = sum(SPLIT_SIZES)
F32 = jnp.float32

kernel_name = 'hybrid_mlstm_hyena_retnet_rwkv7_prefix_dit'


def layer_norm(x, g=None, b=None):
    xf = x.astype(F32)
    mu = jnp.mean(xf, -1, keepdims=True)
    var = jnp.mean(jnp.square(xf - mu), -1, keepdims=True)
    y = (xf - mu) * lax.rsqrt(var + LN_EPS)
    if g is not None:
        y = y * g + b
    return y.astype(x.dtype)


def modulate(u, shift, scale):
    return u * (1 + scale) + shift


def dwconv3(u, w, b):
    up = jnp.pad(u, ((0, 0), (1, 1), (0, 0)))
    return up[:, :-2] * w[0] + up[:, 1:-1] * w[1] + up[:, 2:] * w[2] + b


def token_shift(u, mu):
    up = jnp.pad(u, ((0, 0), (1, 1), (0, 0)))
    return u + mu[0] * (up[:, :-2] - u) + mu[1] * (up[:, 2:] - u)


def to_heads(u):
    B, L, W = u.shape
    return u.reshape(B, L, W // HEAD_DIM, HEAD_DIM).transpose(0, 2, 1, 3)


def from_heads(h):
    B, H, L, d = h.shape
    return h.transpose(0, 2, 1, 3).reshape(B, L, H * d)


def head_group_norm(y, g, b):
    B, L, W = y.shape
    yf = y.astype(F32).reshape(B, L, W // HEAD_DIM, HEAD_DIM)
    mu = jnp.mean(yf, -1, keepdims=True)
    var = jnp.mean(jnp.square(yf - mu), -1, keepdims=True)
    yf = ((yf - mu) * lax.rsqrt(var + GN_EPS)).reshape(B, L, W)
    return yf * g + b


def split_columns(proj):
    return jnp.split(proj, np.cumsum(SPLIT_SIZES)[:-1].tolist(), axis=-1)


def rope_2d(x, ang_row, ang_col):
    def rot(xh, a):
        c, s = jnp.cos(a), jnp.sin(a)
        x1, x2 = xh[..., :ROPE_PAIRS], xh[..., ROPE_PAIRS:]
        return jnp.concatenate([x1 * c - x2 * s, x2 * c + x1 * s], -1)
    half = HEAD_DIM // 2
    return jnp.concatenate([rot(x[..., :half], ang_row), rot(x[..., half:], ang_col)], -1)


def to_chunks(t):
    B, H, L = t.shape[:3]
    t = t.reshape((B, H, L // CHUNK, CHUNK) + t.shape[3:])
    return jnp.moveaxis(t, 2, 0)


def from_chunks(t):
    t = jnp.moveaxis(t, 0, 2)
    B, H, N, T = t.shape[:4]
    return t.reshape((B, H, N * T) + t.shape[4:])


def mlstm_chunk_scan(q, k, v, ig, lf, state):
    causal = jnp.tril(jnp.ones((CHUNK, CHUNK), dtype=bool))

    def step(carry, inp):
        c_mem, n_mem, m_mem = carry
        qc, kc, vc, igc, lfc = inp
        b = jnp.cumsum(lfc, axis=-1)
        dlog = jnp.where(causal, b[..., :, None] - b[..., None, :] + igc[..., None, :], -jnp.inf)
        inter = b + m_mem[..., None]
        m_t = jnp.maximum(inter, jnp.max(dlog, -1))
        s = jnp.einsum('bhid,bhjd->bhij', qc, kc) * jnp.exp(dlog - m_t[..., None])
        w_inter = jnp.exp(inter - m_t)
        num = jnp.einsum('bhij,bhjd->bhid', s, vc) + w_inter[..., None] * jnp.einsum('bhid,bhde->bhie', qc, c_mem)
        den = jnp.sum(s, -1) + w_inter * jnp.einsum('bhid,bhd->bhi', qc, n_mem)
        h = num / jnp.maximum(jnp.abs(den), jnp.exp(-m_t))[..., None]
        b_last = b[..., -1]
        g_log = b_last[..., None] - b + igc
        m_new = jnp.maximum(b_last + m_mem, jnp.max(g_log, -1))
        wk = jnp.exp(g_log - m_new[..., None])
        decay = jnp.exp(b_last + m_mem - m_new)
        c_new = decay[..., None, None] * c_mem + jnp.einsum('bhjd,bhje->bhde', kc * wk[..., None], vc)
        n_new = decay[..., None] * n_mem + jnp.einsum('bhjd,bhj->bhd', kc, wk)
        return (c_new, n_new, m_new), h

    final, h = lax.scan(step, state, tuple(to_chunks(t) for t in (q, k, v, ig, lf)))
    return from_chunks(h), final


def retention_chunk_scan(q, k, v, state, log_gamma):
    idx = jnp.arange(CHUNK, dtype=F32)
    diff = idx[:, None] - idx[None, :]
    causal = diff >= 0
    decay_mask = jnp.where(causal, jnp.exp(jnp.where(causal, diff, 0.0) * log_gamma[:, None, None]), 0.0)
    q_dec = jnp.exp((idx + 1.0) * log_gamma[:, None])
    k_dec = jnp.exp((CHUNK - 1.0 - idx) * log_gamma[:, None])
    chunk_dec = jnp.exp(CHUNK * log_gamma)

    def step(s_mem, inp):
        qc, kc, vc = inp
        s = jnp.einsum('bhid,bhjd->bhij', qc, kc) * decay_mask
        o = jnp.einsum('bhij,bhjd->bhid', s, vc) + jnp.einsum('bhid,bhde->bhie', qc * q_dec[..., None], s_mem)
        s_new = s_mem * chunk_dec[:, None, None] + jnp.einsum('bhjd,bhje->bhde', kc * k_dec[..., None], vc)
        return s_new, o

    final, o = lax.scan(step, state, tuple(to_chunks(t) for t in (q, k, v)))
    return from_chunks(o), final


def rwkv_scan(r, w, k, v, a, b, state):
    def step(s_mem, inp):
        rt, wt, kt, vt, at, bt = inp
        sa = jnp.einsum('bhvk,bhk->bhv', s_mem, at)
        s_new = s_mem * wt[:, :, None, :] + sa[..., None] * bt[:, :, None, :] + vt[..., None] * kt[:, :, None, :]
        return s_new, jnp.einsum('bhvk,bhk->bhv', s_new, rt)

    xs = tuple(jnp.moveaxis(t, 2, 0) for t in (r, w, k, v, a, b))
    final, y = lax.scan(step, state, xs)
    return jnp.moveaxis(y, 0, 2), final


def bidirectional_scan(scan_fw, scan_bw, ctx_fw, ctx_bw, lat_fw, lat_bw, init):
    flip = lambda ts: tuple(jnp.flip(t, axis=2) for t in ts)
    hc_f, st_f = scan_fw(*ctx_fw, init)
    hc_b, st_b = scan_bw(*flip(ctx_bw), init)
    h_f, _ = scan_fw(*lat_fw, st_f)
    h_b, _ = scan_bw(*flip(lat_bw), st_b)
    return h_f + jnp.flip(h_b, axis=2), hc_f + jnp.flip(hc_b, axis=2)


def mlstm_mixer(parts, parts_c, conv_w, conv_b, gate_b, ng, nb):
    def prep(q, k, v, o, gates):
        B, L, _ = q.shape
        qk = jax.nn.silu(dwconv3(jnp.concatenate([q, k], -1), conv_w, conv_b)).astype(F32)
        qh = to_heads(qk[..., :GROUP_W]) * HEAD_DIM ** -0.5
        kh = to_heads(qk[..., GROUP_W:])
        vh = to_heads(v.astype(F32))
        g = (gates.reshape(B, L, 4, N_HEADS_G) + gate_b).astype(F32).transpose(2, 0, 3, 1)
        fw = (qh, kh, vh, g[0], jax.nn.log_sigmoid(g[1]))
        bw = (qh, kh, vh, g[2], jax.nn.log_sigmoid(g[3]))
        return fw, bw

    lat_fw, lat_bw = prep(*parts)
    ctx_fw, ctx_bw = prep(*parts_c)
    B = parts[0].shape[0]
    init = (jnp.zeros((B, N_HEADS_G, HEAD_DIM, HEAD_DIM), F32),
            jnp.zeros((B, N_HEADS_G, HEAD_DIM), F32),
            jnp.zeros((B, N_HEADS_G), F32))
    h, h_c = bidirectional_scan(mlstm_chunk_scan, mlstm_chunk_scan, ctx_fw, ctx_bw, lat_fw, lat_bw, init)
    out = lambda o, hh: jax.nn.sigmoid(o.astype(F32)) * head_group_norm(from_heads(hh), ng, nb)
    return out(parts[3], h), out(parts_c[3], h_c)


def hyena_kernels(L, w1, b1, w2, b2, w3, freq):
    w1, b1, w2, b2, w3, freq = (t.astype(F32) for t in (w1, b1, w2, b2, w3, freq))
    pos = jnp.arange(L, dtype=F32)
    t = jnp.linspace(0.0, 1.0, L, dtype=F32)[:, None]
    ang = 2.0 * math.pi * pos[:, None] / L
    bands = jnp.linspace(1e-4, HYENA_BANDS - 1, HYENA_BANDS, dtype=F32)[None, :]
    feats = jnp.concatenate([t, jnp.cos(bands * ang), -jnp.sin(bands * ang)], -1)
    h = jnp.sin(freq * (feats @ w1 + b1))
    h = jnp.sin(freq * (h @ w2 + b2))
    h = h @ w3
    max_decay = math.log(HYENA_TARGET) / HYENA_FAST_DECAY
    min_decay = math.log(HYENA_TARGET) / HYENA_SLOW_DECAY
    deltas = jnp.abs(jnp.linspace(min_decay, max_decay, GROUP_W, dtype=F32))
    window = jnp.exp(-t * deltas)
    h = h.reshape(L, 2, HYENA_ORDER, GROUP_W) * window[:, None, None, :]
    hf, hb = h[:, 0], h[:, 1]
    return jnp.concatenate([hf[:1] + hb[:1], hf[1:], jnp.zeros_like(hf[:1]), hb[:0:-1]], 0)


def fftconv(u, k_full):
    L = u.shape[1]
    uf = jnp.fft.rfft(u, n=2 * L, axis=1)
    kf = jnp.fft.rfft(k_full, n=2 * L, axis=0)
    return jnp.fft.irfft(uf * kf[None], n=2 * L, axis=1)[:, :L]


def hyena_mixer(parts, parts_c, conv_w, conv_b, w1, b1, w2, b2, w3, freq, d_bias, ng, nb):
    def run(v, x1, x2):
        L = v.shape[1]
        k_full = hyena_kernels(L, w1, b1, w2, b2, w3, freq)
        z = dwconv3(jnp.concatenate([v, x1, x2], -1), conv_w, conv_b).astype(F32)
        v, x1, x2 = jnp.split(z, [GROUP_W, 2 * GROUP_W], axis=-1)
        z2 = x1 * (fftconv(v, k_full[:, 0]) + v * d_bias[0])
        y = x2 * (fftconv(z2, k_full[:, 1]) + z2 * d_bias[1])
        return head_group_norm(y, ng, nb)

    y = run(*parts)
    y_c = run(*parts_c) if parts_c is not None else None
    return y, y_c


def retention_mixer(parts, parts_c, ang, ng, nb):
    lg_f = jnp.log(1.0 - 2.0 ** (-5.0 - jnp.arange(N_HEADS_G, dtype=F32)))
    lg_b = lg_f[::-1]

    def prep(q, k, v, rotate):
        qh = to_heads(q.astype(F32)) * HEAD_DIM ** -0.5
        kh = to_heads(k.astype(F32))
        if rotate:
            qh, kh = rope_2d(qh, *ang), rope_2d(kh, *ang)
        return (qh, kh, to_heads(v.astype(F32)))

    lat = prep(*parts[:3], True)
    ctx = prep(*parts_c[:3], False)
    B = parts[0].shape[0]
    init = jnp.zeros((B, N_HEADS_G, HEAD_DIM, HEAD_DIM), F32)
    h, h_c = bidirectional_scan(functools.partial(retention_chunk_scan, log_gamma=lg_f),
                                functools.partial(retention_chunk_scan, log_gamma=lg_b),
                                ctx, ctx, lat, lat, init)
    out = lambda g, hh: jax.nn.silu(g.astype(F32)) * head_group_norm(from_heads(hh), ng, nb)
    return out(parts[3], h), out(parts_c[3], h_c)


def rwkv_mixer(parts, parts_c, mu, w0, w2, a0, a2, g2, kk_scale, ka, rk, ng, nb):
    def prep(seq_parts):
        z = token_shift(jnp.concatenate(seq_parts, -1).astype(F32), mu)
        r, k, v, wlo, alo, glo = jnp.split(z, list(RWKV_SPLIT), axis=-1)
        g = jax.nn.sigmoid(glo) @ g2
        rh, kh, vh = to_heads(r), to_heads(k), to_heads(v)
        kk = to_heads(k * kk_scale)
        kk = kk * lax.rsqrt(jnp.sum(kk * kk, -1, keepdims=True) + 1e-12)
        dirs = []
        for d in range(2):
            w_pre = w0[d] + jnp.tanh(wlo) @ w2[d]
            decay = jnp.exp(-jnp.exp(-jax.nn.softplus(-w_pre) - 0.5))
            a = jax.nn.sigmoid(a0[d] + alo @ a2[d])
            k_d = k * (1 + (a - 1) * ka)
            dirs.append((rh, to_heads(decay), to_heads(k_d), vh, -kk, kk * to_heads(a)))
        bonus = jnp.sum(rh * kh * rk[None, :, None, :], -1, keepdims=True) * vh
        return dirs[0], dirs[1], bonus, g

    lat_fw, lat_bw, bonus, g = prep(parts)
    ctx_fw, ctx_bw, bonus_c, g_c = prep(parts_c)
    B = parts[0].shape[0]
    init = jnp.zeros((B, N_HEADS_G, HEAD_DIM, HEAD_DIM), F32)
    h, h_c = bidirectional_scan(rwkv_scan, rwkv_scan, ctx_fw, ctx_bw, lat_fw, lat_bw, init)
    out = lambda hh, bon, gg: gg * (head_group_norm(from_heads(hh), ng, nb) + from_heads(bon))
    return out(h, bonus, g), out(h_c, bonus_c, g_c)


def token_mixer(u, uc, p, ang, need_ctx):
    parts = split_columns(u @ p['w_in'])
    parts_c = split_columns(uc @ p['w_in'])
    ng, nb = p['out_norm_g'], p['out_norm_b']
    gs = lambda i, t: t[i * GROUP_W:(i + 1) * GROUP_W]
    ya, ya_c = mlstm_mixer(parts[0:5], parts_c[0:5], p['mlstm_conv_w'], p['mlstm_conv_b'], p['mlstm_gate_b'],
                           gs(0, ng), gs(0, nb))
    yb, yb_c = hyena_mixer(parts[5:8], parts_c[5:8] if need_ctx else None, p['hyena_conv_w'], p['hyena_conv_b'],
                           p['hyena_w1'], p['hyena_b1'], p['hyena_w2'], p['hyena_b2'], p['hyena_w3'],
                           p['hyena_freq'], p['hyena_d'], gs(1, ng), gs(1, nb))
    yr, yr_c = retention_mixer(parts[8:12], parts_c[8:12], ang, gs(2, ng), gs(2, nb))
    yw, yw_c = rwkv_mixer(parts[12:18], parts_c[12:18], p['rwkv_mu'], p['rwkv_w0'], p['rwkv_w2'], p['rwkv_a0'],
                          p['rwkv_a2'], p['rwkv_g2'], p['rwkv_kk'], p['rwkv_ka'], p['rwkv_rk'], gs(3, ng), gs(3, nb))
    y = jnp.concatenate([ya, yb, yr, yw], -1).astype(u.dtype) @ p['w_out']
    if not need_ctx:
        return y, None
    y_c = jnp.concatenate([ya_c, yb_c, yr_c, yw_c], -1).astype(uc.dtype) @ p['w_out']
    return y, y_c


def conv_ffn(u, w_up, conv_w, conv_b, w_down):
    h = u @ w_up
    a, b = jnp.split(h, [D_FF], axis=-1)
    return (jax.nn.silu(dwconv3(a, conv_w, conv_b)) * b) @ w_down


def setup_inputs(seed: int = 0) -> dict:
    key = jax.random.key(seed)
    ks = iter(jax.random.split(key, 48))

    def nrm(shape, std):
        return std * jax.random.normal(next(ks), shape, F32)

    def conv_init(width):
        return nrm((DEPTH, 3, width), 0.3) + jnp.array([0.0, 1.0, 0.0], F32)[None, :, None]

    head_lin = jnp.linspace(3.0, 6.0, N_HEADS_G, dtype=F32)
    f_mask = jnp.array([0.0, 1.0, 0.0, 1.0], F32)[:, None]
    ratio = jnp.arange(GROUP_W, dtype=F32) / (GROUP_W - 1)
    decay_speed = -7.0 + 5.0 * ratio ** 1.35
    return {
        'x': nrm((BATCH, SEQ, D_MODEL), 1.0),
        'c': nrm((BATCH, D_MODEL), 1.0),
        'ctx': nrm((BATCH, CTX_LEN, D_MODEL), 1.0),
        'c_ctx': nrm((D_MODEL,), 1.0),
        'ada_w': nrm((DEPTH, D_MODEL, 6 * D_MODEL), 0.5 * D_MODEL ** -0.5),
        'ada_b': nrm((DEPTH, 6 * D_MODEL), 0.01),
        'w_in': nrm((DEPTH, D_MODEL, D_PROJ), D_MODEL ** -0.5),
        'mlstm_conv_w': conv_init(2 * GROUP_W),
        'mlstm_conv_b': nrm((DEPTH, 2 * GROUP_W), 0.01),
        'mlstm_gate_b': nrm((DEPTH, 4, N_HEADS_G), 0.1) + f_mask * head_lin[None, :],
        'hyena_conv_w': conv_init(3 * GROUP_W),
        'hyena_conv_b': nrm((DEPTH, 3 * GROUP_W), 0.01),
        'hyena_w1': nrm((DEPTH, HYENA_EMB, HYENA_FFN), HYENA_EMB ** -0.5),
        'hyena_b1': nrm((DEPTH, HYENA_FFN), 0.01),
        'hyena_w2': nrm((DEPTH, HYENA_FFN, HYENA_FFN), HYENA_FFN ** -0.5),
        'hyena_b2': nrm((DEPTH, HYENA_FFN), 0.01),
        'hyena_w3': nrm((DEPTH, HYENA_FFN, 2 * HYENA_ORDER * GROUP_W), HYENA_FFN ** -0.5),
        'hyena_freq': 1.0 + nrm((DEPTH, HYENA_FFN), 0.01),
        'hyena_d': nrm((DEPTH, HYENA_ORDER, GROUP_W), 0.5),
        'rwkv_mu': jax.random.uniform(next(ks), (DEPTH, 2, RWKV_SHIFT_W), F32, 0.0, 0.5),
        'rwkv_w0': decay_speed + 0.5 + nrm((DEPTH, 2, GROUP_W), 0.1),
        'rwkv_w2': nrm((DEPTH, 2, RWKV_W_LORA, GROUP_W), 0.1 * RWKV_W_LORA ** -0.5),
        'rwkv_a0': nrm((DEPTH, 2, GROUP_W), 0.1),
        'rwkv_a2': nrm((DEPTH, 2, RWKV_A_LORA, GROUP_W), 0.1 * RWKV_A_LORA ** -0.5),
        'rwkv_g2': nrm((DEPTH, RWKV_G_LORA, GROUP_W), RWKV_G_LORA ** -0.5),
        'rwkv_kk': 0.85 + nrm((DEPTH, GROUP_W), 0.02),
        'rwkv_ka': 1.0 + nrm((DEPTH, GROUP_W), 0.02),
        'rwkv_rk': nrm((DEPTH, N_HEADS_G, HEAD_DIM), 0.1),
        'out_norm_g': 1.0 + nrm((DEPTH, D_MIX), 0.02),
        'out_norm_b': nrm((DEPTH, D_MIX), 0.01),
        'w_out': nrm((DEPTH, D_MIX, D_MODEL), DEEPNORM_BETA * D_MIX ** -0.5),
        'ln1_g': 1.0 + nrm((DEPTH, D_MODEL), 0.02),
        'ln1_b': nrm((DEPTH, D_MODEL), 0.01),
        'ffn_w_up': nrm((DEPTH, D_MODEL, 2 * D_FF), D_MODEL ** -0.5),
        'ffn_conv_w': conv_init(D_FF),
        'ffn_conv_b': nrm((DEPTH, D_FF), 0.01),
        'ffn_w_down': nrm((DEPTH, D_FF, D_MODEL), DEEPNORM_BETA * D_FF ** -0.5),
        'ln2_g': 1.0 + nrm((DEPTH, D_MODEL), 0.02),
        'ln2_b': nrm((DEPTH, D_MODEL), 0.01),
    }


def reference(x, c, ctx, c_ctx, ada_w, ada_b, w_in, mlstm_conv_w, mlstm_conv_b, mlstm_gate_b,
              hyena_conv_w, hyena_conv_b, hyena_w1, hyena_b1, hyena_w2, hyena_b2, hyena_w3, hyena_freq, hyena_d,
              rwkv_mu, rwkv_w0, rwkv_w2, rwkv_a0, rwkv_a2, rwkv_g2, rwkv_kk, rwkv_ka, rwkv_rk,
              out_norm_g, out_norm_b, w_out, ln1_g, ln1_b, ffn_w_up, ffn_conv_w, ffn_conv_b, ffn_w_down,
              ln2_g, ln2_b):
    L = x.shape[1]
    ROWS = L // GRID_W
    freqs = ROPE_BASE ** (-jnp.arange(ROPE_PAIRS, dtype=F32) / ROPE_PAIRS)
    row = jnp.repeat(jnp.arange(ROWS, dtype=F32), GRID_W)
    col = jnp.tile(jnp.arange(GRID_W, dtype=F32), ROWS)
    ang = (row[:, None] * freqs, col[:, None] * freqs)
    xc = ctx
    silu_c = jax.nn.silu(c)
    silu_cc = jax.nn.silu(c_ctx)
    for l in range(DEPTH):
        last = l == DEPTH - 1
        mod = jnp.split((silu_c @ ada_w[l] + ada_b[l])[:, None, :], 6, axis=-1)
        modc = jnp.split(silu_cc @ ada_w[l] + ada_b[l], 6, axis=-1)
        p = {'w_in': w_in[l], 'mlstm_conv_w': mlstm_conv_w[l], 'mlstm_conv_b': mlstm_conv_b[l],
             'mlstm_gate_b': mlstm_gate_b[l], 'hyena_conv_w': hyena_conv_w[l], 'hyena_conv_b': hyena_conv_b[l],
             'hyena_w1': hyena_w1[l], 'hyena_b1': hyena_b1[l], 'hyena_w2': hyena_w2[l], 'hyena_b2': hyena_b2[l],
             'hyena_w3': hyena_w3[l], 'hyena_freq': hyena_freq[l], 'hyena_d': hyena_d[l],
             'rwkv_mu': rwkv_mu[l], 'rwkv_w0': rwkv_w0[l], 'rwkv_w2': rwkv_w2[l], 'rwkv_a0': rwkv_a0[l],
             'rwkv_a2': rwkv_a2[l], 'rwkv_g2': rwkv_g2[l], 'rwkv_kk': rwkv_kk[l], 'rwkv_ka': rwkv_ka[l],
             'rwkv_rk': rwkv_rk[l], 'out_norm_g': out_norm_g[l], 'out_norm_b': out_norm_b[l], 'w_out': w_out[l]}
        u = modulate(layer_norm(x), mod[0], mod[1])
        uc = modulate(layer_norm(xc), modc[0], modc[1])
        y, y_c = token_mixer(u, uc, p, ang, not last)
        x = layer_norm(DEEPNORM_ALPHA * x + mod[2] * y, ln1_g[l], ln1_b[l])
        u = modulate(layer_norm(x), mod[3], mod[4])
        f = conv_ffn(u, ffn_w_up[l], ffn_conv_w[l], ffn_conv_b[l], ffn_w_down[l])
        x = layer_norm(DEEPNORM_ALPHA * x + mod[5] * f, ln2_g[l], ln2_b[l])
        if not last:
            xc = layer_norm(DEEPNORM_ALPHA * xc + modc[2] * y_c, ln1_g[l], ln1_b[l])
            uc = modulate(layer_norm(xc), modc[3], modc[4])
            fc = conv_ffn(uc, ffn_w_up[l], ffn_conv_w[l], ffn_conv_b[l], ffn_w_down[l])
            xc = layer_norm(DEEPNORM_ALPHA * xc + modc[5] * fc, ln2_g[l], ln2_b[l])
    return x
```

```python
import math
import os
from contextlib import ExitStack
import numpy as np
import ml_dtypes
import concourse.bass as bass
import concourse.mybir as mybir
from concourse.bass_utils import run_bass_kernel_spmd

F32 = mybir.dt.float32
BF16 = mybir.dt.bfloat16
ALU = mybir.AluOpType
AF = mybir.ActivationFunctionType
AX = mybir.AxisListType

ENGS = ("pe", "act", "dve", "pool", "sp")

D = 1024
SEQ = 2048
CTX = 256
NT = 18
TOK = NT * 128
DPROJ = 3728
DFF = 2816
GW = 256
ALPHA = 4 ** 0.25
NROWS = 2307


def rowbase(t):
    return 1 + 128 * t if t < 2 else 2 + 128 * t


class Prog:
    N_DMA_SEMS = 24

    def __init__(self, nc):
        self.nc = nc
        self.q = {e: [] for e in ENGS}
        self.cnt = {e: 0 for e in ENGS}
        self.waited = {}
        self.lastw = {}
        self.reads = {}
        self.dma_uses = [0] * self.N_DMA_SEMS
        self.dma_rr = 0
        self.n_inst = 0
        self.epoch = {e: 0 for e in ENGS}

    @staticmethod
    def key(x):
        if isinstance(x, tuple):
            return (Prog.key(x[0]),) + tuple(x[1:])
        if isinstance(x, str):
            return x
        t = getattr(x, "tensor", x)
        return t.name

    def _need(self, eng, deps, waits):
        for (src, val) in deps:
            if self.waited.get((eng, src), 0) < val:
                self.waited[(eng, src)] = val
                waits.append((src, val))

    def _deps(self, eng, r, w):
        waits = []
        deps = []
        for k in r:
            k = self.key(k)
            if k in self.lastw:
                deps.append(self.lastw[k])
        for k in w:
            k = self.key(k)
            if k in self.lastw:
                deps.append(self.lastw[k])
            for s, v in self.reads.get(k, {}).items():
                deps.append((s, v))
        self._need(eng, deps, waits)
        return waits

    def _commit(self, src, val, r, w):
        for k in r:
            k = self.key(k)
            d = self.reads.setdefault(k, {})
            if d.get(src, 0) < val:
                d[src] = val
        for k in w:
            k = self.key(k)
            self.lastw[k] = (src, val)
            self.reads[k] = {}

    def op(self, eng, fn, r=(), w=()):
        waits = self._deps(eng, r, w)
        self.cnt[eng] += 1
        val = self.cnt[eng]
        src = ("eng", eng, self.epoch[eng])
        self.q[eng].append(("op", fn, waits, src))
        self._commit(src, val, r, w)
        self.n_inst += 1

    def dma(self, out, in_, r=(), w=(), eng=None, **kw):
        if eng is None:
            eng = "sp"
        waits = self._deps(eng, r, w)
        s = self.dma_rr
        self.dma_rr = (self.dma_rr + 1) % self.N_DMA_SEMS
        src = ("dma", s)
        prev = self.dma_uses[s] * 16
        if prev and self.waited.get((eng, src), 0) < prev:
            self.waited[(eng, src)] = prev
            waits.append((src, prev))
        self.dma_uses[s] += 1
        val = self.dma_uses[s] * 16
        self.q[eng].append(("dma", (out, in_, kw), waits, s))
        self._commit(src, val, r, w)
        self.n_inst += 1

    def barrier(self):
        for e in ENGS:
            waits = []
            deps = [(("eng", o, self.epoch[o]), self.cnt[o]) for o in ENGS if o != e and self.cnt[o] > 0]
            deps += [(("dma", s), self.dma_uses[s] * 16)
                     for s in range(self.N_DMA_SEMS) if self.dma_uses[s]]
            self._need(e, deps, waits)
            if waits:
                self.q[e].append(("wait", None, waits, None))
        self.lastw.clear()
        self.reads.clear()
        for e in ENGS:
            if self.cnt[e] > 1000000000:
                self.epoch[e] += 1
                self.cnt[e] = 0

    def emit(self, stack):
        nc = self.nc
        self.barrier()
        esem = {}
        for e in ENGS:
            for ep in range(self.epoch[e] + 1):
                esem[(e, ep)] = stack.enter_context(nc.semaphore("s_%s_%d" % (e, ep)))
        dsem = [stack.enter_context(nc.semaphore("d%d" % i)) for i in range(self.N_DMA_SEMS)]

        def semof(src):
            return dsem[src[1]] if src[0] == "dma" else esem[(src[1], src[2])]

        block = stack.enter_context(nc.Block())

        def run(e, engobj):
            for kind, fn, waits, s in self.q[e]:
                for (src, val) in waits:
                    engobj.wait_ge(semof(src), val)
                if kind == "op":
                    fn(engobj).then_inc(semof(s), 1)
                elif kind == "dma":
                    out, in_, kw = fn
                    engobj.dma_start(out=out, in_=in_, **kw).then_inc(dsem[s], 16)

        @block.tensor
        def _(eng):
            run("pe", eng)

        @block.scalar
        def _(eng):
            run("act", eng)

        @block.vector
        def _(eng):
            run("dve", eng)

        @block.gpsimd
        def _(eng):
            run("pool", eng)

        @block.sync
        def _(eng):
            run("sp", eng)


class Rot:
    def __init__(self, items):
        self.items = list(items)
        self.i = 0

    def next(self):
        x = self.items[self.i % len(self.items)]
        self.i += 1
        return x


def make_consts():
    c = {}
    c["ident"] = np.eye(128, dtype=np.float32)
    j = np.arange(128)[:, None]
    i = np.arange(128)[None, :]
    c["tri"] = np.stack([(j <= i), (j >= i), (j < i), (j > i)]).astype(np.float32)
    c["ones"] = np.ones((128, 128), np.float32)
    c["tri4"] = np.ascontiguousarray(np.broadcast_to(c["tri"].transpose(1, 0, 2)[:, :, None, :], (128, 4, 4, 128))).astype(np.float32)
    lg_f = np.log(1.0 - 2.0 ** (-5.0 - np.arange(4, dtype=np.float64)))
    lg = np.stack([lg_f, lg_f[::-1]])
    jj = np.arange(128, dtype=np.float64)
    rtmask = np.zeros((128, 2, 4, 128), np.float64)
    qdec = np.zeros((128, 8)); kdec = np.zeros((128, 8)); cdec = np.zeros((128, 8))
    for d in range(2):
        for h in range(4):
            g = lg[d, h]
            diff = (jj[None, :] - jj[:, None]) if d == 0 else (jj[:, None] - jj[None, :])
            rtmask[:, d, h, :] = np.where(diff >= 0, np.exp(np.maximum(diff, 0) * g), 0.0)
            pos = jj if d == 0 else 127 - jj
            qdec[:, d * 4 + h] = np.exp((pos + 1.0) * g)
            kdec[:, d * 4 + h] = np.exp((127.0 - pos) * g)
            cdec[:, d * 4 + h] = np.exp(128.0 * g)
    c["rtmask"] = rtmask.astype(np.float32)
    c["rtdec"] = np.concatenate([qdec, kdec, cdec], 1).astype(np.float32)
    freqs = 10000.0 ** (-np.arange(16, dtype=np.float32) / 16)
    row = np.repeat(np.arange(32, dtype=np.float32), 64); col = np.tile(np.arange(64, dtype=np.float32), 32)
    ar = (row[:, None] * freqs).astype(np.float32); ac = (col[:, None] * freqs).astype(np.float32)
    cos64 = np.concatenate([np.cos(ar), np.cos(ar), np.cos(ac), np.cos(ac)], 1)
    sin64 = np.concatenate([-np.sin(ar), np.sin(ar), -np.sin(ac), np.sin(ac)], 1)
    c["rope"] = np.stack([np.tile(cos64, (1, 4)), np.tile(sin64, (1, 4))], 1).astype(np.float32)
    def lo(n, rows_odd):
        same = (j // (2 * n)) == (i // (2 * n))
        return same & (((j // n) % 2) == (1 if rows_odd else 0)) & (((i // n) % 2) == (0 if rows_odd else 1))
    bd = (j // 16) == (i // 16)
    rwm = np.zeros((128, 2, 8, 128), np.float32)
    for d in range(2):
        strictT = (j < i) if d == 0 else (j > i)
        strictN = (i < j) if d == 0 else (i > j)
        rwm[:, d, 0, :] = bd & strictT
        rwm[:, d, 1, :] = bd & strictN
        for li, n in enumerate((16, 32, 64)):
            rwm[:, d, 2 + li, :] = lo(n, d == 1)
            rwm[:, d, 5 + li, :] = lo(n, d == 0)
    c["rwm"] = rwm.astype(ml_dtypes.bfloat16)
    for kind, L in (("lat", SEQ), ("ctx", CTX)):
        N = 2 * L
        nft = L // 128
        t = np.arange(L, dtype=np.int64)[:, None]
        f = np.arange(L, dtype=np.int64)[None, :]
        ang = 2.0 * np.pi * ((t * f) % N).astype(np.float64) / N
        sgn = (-1.0) ** np.arange(L)
        Fre = np.cos(ang); Fim = -np.sin(ang); Fim[:, 0] = sgn
        F = np.stack([Fre, Fim], 0).reshape(2, nft, 128, nft, 128).transpose(3, 2, 0, 1, 4)
        c["hF_" + kind] = np.ascontiguousarray(F).astype(ml_dtypes.bfloat16)
        Bre = (2.0 / N) * np.cos(ang.T); Bre[0, :] = 1.0 / N
        Bim = -(2.0 / N) * np.sin(ang.T); Bim[0, :] = sgn / N
        B = np.stack([Bre, Bim], 0).reshape(2, nft, 128, nft, 128).transpose(3, 2, 0, 1, 4)
        c["hB_" + kind] = np.ascontiguousarray(B).astype(ml_dtypes.bfloat16)
        pos = np.arange(L, dtype=np.float32)
        tl = np.linspace(0.0, 1.0, L, dtype=np.float32)[:, None]
        a2 = (2.0 * math.pi * pos[:, None] / L).astype(np.float32)
        bands = np.linspace(1e-4, 15.0, 16, dtype=np.float32)[None, :]
        feats = np.concatenate([tl, np.cos(bands * a2), -np.sin(bands * a2)], -1).astype(np.float32)
        c["hfeat_" + kind] = np.ascontiguousarray(feats.T)
        max_decay = math.log(1e-2) / 0.3; min_decay = math.log(1e-2) / 1.5
        deltas = np.abs(np.linspace(min_decay, max_decay, GW, dtype=np.float32))
        c["hwin_" + kind] = np.exp(-tl * deltas).astype(np.float32)
    return c


CONST_SPECS = None


class Builder:
    def __init__(self, n_seq=4, layers=(0, 1), dbg=False, stages=None):
        self.n_seq = n_seq
        self.layers = layers
        self.dbg = dbg
        self.stages = stages
        self.uid = 0

    def nm(self, base):
        self.uid += 1
        return "%s_%d" % (base, self.uid)

    def build(self, consts):
        nc = bass.Bass("TRN2", target_bir_lowering=False)
        self.nc = nc
        self.P = Prog(nc)
        n_seq = self.n_seq
        I = {}

        def inp(name, shape, dt=F32):
            I[name] = nc.dram_tensor(name, list(shape), dt, kind="ExternalInput").ap()

        inp("x", [n_seq, SEQ, D]); inp("c", [n_seq, D]); inp("ctx", [n_seq, CTX, D]); inp("c_ctx", [D])
        inp("ada_w", [2, D, 6 * D]); inp("ada_b", [2, 6 * D]); inp("w_in", [2, D, DPROJ])
        inp("mlstm_conv_w", [2, 3, 512]); inp("mlstm_conv_b", [2, 512]); inp("mlstm_gate_b", [2, 4, 4])
        inp("hyena_conv_w", [2, 3, 768]); inp("hyena_conv_b", [2, 768]); inp("hyena_w1", [2, 33, 64])
        inp("hyena_b1", [2, 64]); inp("hyena_w2", [2, 64, 64]); inp("hyena_b2", [2, 64]); inp("hyena_w3", [2, 64, 1024])
        inp("hyena_freq", [2, 64]); inp("hyena_d", [2, 2, 256])
        inp("rwkv_mu", [2, 2, 896]); inp("rwkv_w0", [2, 2, 256]); inp("rwkv_w2", [2, 2, 32, 256])
        inp("rwkv_a0", [2, 2, 256]); inp("rwkv_a2", [2, 2, 32, 256]); inp("rwkv_g2", [2, 64, 256])
        inp("rwkv_kk", [2, 256]); inp("rwkv_ka", [2, 256]); inp("rwkv_rk", [2, 4, 64])
        inp("out_norm_g", [2, D]); inp("out_norm_b", [2, D]); inp("w_out", [2, D, D])
        inp("ln1_g", [2, D]); inp("ln1_b", [2, D]); inp("ffn_w_up", [2, D, 2 * DFF])
        inp("ffn_conv_w", [2, 3, DFF]); inp("ffn_conv_b", [2, DFF]); inp("ffn_w_down", [2, DFF, D])
        inp("ln2_g", [2, D]); inp("ln2_b", [2, D])
        for k, v in consts.items():
            inp("k_" + k, v.shape, BF16 if v.dtype == ml_dtypes.bfloat16 else F32)
        if self.dbg:
            inp("mix_in", [TOK, D])
        self.I = I
        self.out = nc.dram_tensor("out", [n_seq, SEQ, D], F32, kind="ExternalOutput").ap()

        def scr(name, shape, dt=F32):
            kind = "ExternalOutput" if (self.dbg and name in ("proj", "mix", "modv0", "xdump")) else None
            if kind:
                return nc.dram_tensor(name, list(shape), dt, kind=kind).ap()
            return nc.dram_tensor(name, list(shape), dt).ap()

        S = {}
        for l in (0, 1):
            S["wb_in%d" % l] = scr("wb_in%d" % l, [D, DPROJ], BF16)
            S["wb_out%d" % l] = scr("wb_out%d" % l, [D, D], BF16)
            S["wb_up%d" % l] = scr("wb_up%d" % l, [D, 2 * DFF], BF16)
            S["wb_dn%d" % l] = scr("wb_dn%d" % l, [DFF, D], BF16)
            S["modv%d" % l] = scr("modv%d" % l, [5, 6 * D])
        for l in (0, 1):
            S["HT%d_lat" % l] = scr("HT%d_lat" % l, [16, 128, 3, 512])
            S["HT%d_ctx" % l] = scr("HT%d_ctx" % l, [2, 128, 3, 512])
        S["proj"] = scr("proj", [NROWS, DPROJ])
        S["mix"] = scr("mix", [TOK, D])
        S["gT"] = scr("gT", [DFF, TOK], BF16)
        if self.dbg:
            S["xdump"] = scr("xdump", [TOK, D])
        self.S = S

        with ExitStack() as top:
            self.top = top
            self.alloc_globals(top)
            self.stage_weights()
            self.stage_mod()
            if self.on("hyena"):
                self.stage_hyena()
            for s in range(n_seq):
                self.run_sequence(s)
            self.P.emit(top)
        return nc

    def sbt(self, st, base, shape, dt=F32):
        return st.enter_context(self.nc.sbuf_tensor(self.nm(base), list(shape), dt))

    def pst(self, st, base, shape, dt=F32):
        return st.enter_context(self.nc.psum_tensor(self.nm(base), list(shape), dt))

    def bank(self, st, base, dt=F32):
        return self.pst(st, base, [128, 512 if dt == F32 else 1024], dt)

    def alloc_globals(self, st):
        P, I = self.P, self.I
        self.xres = [self.sbt(st, "xres", [128, D]) for _ in range(NT)]
        self.identf = self.sbt(st, "identf", [128, 128])
        self.identb = self.sbt(st, "identb", [128, 128], BF16)
        self.tri = self.sbt(st, "tri", [128, 4, 128])
        self.ones = self.sbt(st, "ones", [128, 128])
        self.epsc = self.sbt(st, "epsc", [128, 1])
        self.zero = self.sbt(st, "zero", [128, 512])
        self.modp = self.sbt(st, "modp", [128, 2, 5, 4, 8])
        P.dma(self.identf[:], I["k_ident"], w=[self.identf])
        P.dma(self.tri[:], I["k_tri"].rearrange("m j i -> j m i"), w=[self.tri])
        P.dma(self.ones[:], I["k_ones"], w=[self.ones])
        P.op("dve", lambda e: e.tensor_copy(out=self.identb[:], in_=self.identf[:]), r=[self.identf], w=[self.identb])
        P.op("pool", lambda e: e.memset(self.epsc[:], 1e-5), w=[self.epsc])
        P.op("pool", lambda e: e.memset(self.modp[:], 0.0), w=[self.modp])
        P.op("pool", lambda e: e.memset(self.zero[:], 0.0), w=[self.zero])
        for row in (0, 257, 2306):
            for c0 in range(0, DPROJ, 512):
                n = min(512, DPROJ - c0)
                P.dma(self.S["proj"][row:row + 1, c0:c0 + n], self.zero[0:1, 0:n], r=[self.zero])

    def stage_weights(self):
        P, I, S = self.P, self.I, self.S
        with ExitStack() as st:
            stf = Rot([self.sbt(st, "wstf", [128, 2 * DFF]) for _ in range(2)])
            stb = Rot([self.sbt(st, "wstb", [128, 2 * DFF], BF16) for _ in range(2)])
            engs = Rot(["act", "dve", "pool"])
            for l in self.layers:
                for (src, dst, K, N) in ((I["w_in"][l], S["wb_in%d" % l], D, DPROJ),
                                         (I["w_out"][l], S["wb_out%d" % l], D, D),
                                         (I["ffn_w_up"][l], S["wb_up%d" % l], D, 2 * DFF),
                                         (I["ffn_w_down"][l], S["wb_dn%d" % l], DFF, D)):
                    for kk in range(K // 128):
                        a = stf.next(); b = stb.next()
                        P.dma(a[:, 0:N], src[kk * 128:(kk + 1) * 128, :], w=[a])
                        e = engs.next()
                        if e == "act":
                            P.op("act", lambda eng, a=a, b=b, N=N: eng.copy(out=b[:, 0:N], in_=a[:, 0:N]), r=[a], w=[b])
                        else:
                            P.op(e, lambda eng, a=a, b=b, N=N: eng.tensor_copy(out=b[:, 0:N], in_=a[:, 0:N]), r=[a], w=[b])
                        P.dma(dst[kk * 128:(kk + 1) * 128, :], b[:, 0:N], r=[b], eng="act")
            P.barrier()

    def stage_mod(self):
        P, I, S, nc = self.P, self.I, self.S, self.nc
        n_seq = self.n_seq
        with ExitStack() as st:
            cT = self.sbt(st, "cT", [128, 8, 5])
            aw = Rot([self.sbt(st, "aw", [128, 8, 512]) for _ in range(2)])
            ab = self.sbt(st, "ab", [5, 6 * D])
            msb = self.sbt(st, "msb", [5, 6 * D])
            ps = Rot([self.pst(st, "mps", [128, 512]) for _ in range(2)])
            P.op("pool", lambda e: e.memset(cT[:], 0.0), w=[cT])
            for r in range(n_seq):
                P.dma(cT[:, :, r], I["c"][r].rearrange("(k p) -> p k", p=128), w=[cT], allow_slow_non_contiguous=True)
            P.dma(cT[:, :, 4], I["c_ctx"].rearrange("(k p) -> p k", p=128), w=[cT], allow_slow_non_contiguous=True)
            P.op("act", lambda e: e.activation(out=cT[:], in_=cT[:], func=AF.Silu), r=[cT], w=[cT])
            for l in self.layers:
                P.dma(ab[:], I["ada_b"][l].partition_broadcast(5), w=[ab])
                for cg in range(12):
                    a = aw.next(); p = ps.next()
                    P.dma(a[:], I["ada_w"][l][:, cg * 512:(cg + 1) * 512].rearrange("(k p) n -> p k n", p=128), w=[a])

                    def mm(e, a=a, p=p):
                        for k in range(8):
                            i = e.matmul(p[0:5, :], lhsT=cT[:, k, :], rhs=a[:, k, :], start=(k == 0), stop=(k == 7))
                        return i
                    P.op("pe", mm, r=[cT, a], w=[p])
                    P.op("dve", lambda e, p=p, cg=cg: e.tensor_tensor(out=msb[:, cg * 512:(cg + 1) * 512], in0=p[0:5, :],
                                                                      in1=ab[:, cg * 512:(cg + 1) * 512], op=ALU.add),
                         r=[p, ab], w=[msb])
                P.dma(S["modv%d" % l], msb[:], r=[msb])
                P.barrier()
                for r in range(5):
                    for jj, j in enumerate((0, 1, 3, 4)):
                        P.dma(self.modp[:, l, r, jj, :], S["modv%d" % l][r, j * D:(j + 1) * D].rearrange("(k p) -> p k", p=128),
                              w=[self.modp], allow_slow_non_contiguous=True)
            for jj in (1, 3):
                P.op("dve", lambda e, jj=jj: e.tensor_scalar_add(out=self.modp[:, :, :, jj, :], in0=self.modp[:, :, :, jj, :], scalar1=1.0),
                     r=[self.modp], w=[self.modp])
            P.barrier()

    def run_sequence(self, s):
        P, I = self.P, self.I
        for t in range(NT):
            src = I["ctx"][s, t * 128:(t + 1) * 128, :] if t < 2 else I["x"][s, (t - 2) * 128:(t - 1) * 128, :]
            P.dma(self.xres[t][:], src, w=[self.xres[t]])
        for l in self.layers:
            self.run_layer(s, l)
        for t in range(2, NT):
            P.dma(self.out[s, (t - 2) * 128:(t - 1) * 128, :], self.xres[t][:], r=[self.xres[t]])
        if self.dbg:
            for t in range(NT):
                P.dma(self.S["xdump"][t * 128:(t + 1) * 128, :], self.xres[t][:], r=[self.xres[t]])
        P.barrier()

    def on(self, name):
        return self.stages is None or name in self.stages

    def run_layer(self, s, l):
        P = self.P
        with ExitStack() as st:
            uT = self.sbt(st, "uT", [128, 8, TOK], BF16)
            self.ln_mod(s, l, uT, 0)
            if self.on("proj"):
                self.in_proj(l, uT)
            P.barrier()
        if self.on("ret"):
            self.mix_retention(s, l)
        if self.on("mlstm"):
            self.mix_mlstm(s, l)
        if self.on("hyena"):
            self.mix_hyena(s, l)
        if self.on("rwkv"):
            self.mix_rwkv(s, l)
        if self.dbg and self.on("mixin"):
            for t in range(NT):
                P.dma(self.S["mix"][t * 128:(t + 1) * 128, :], self.I["mix_in"][t * 128:(t + 1) * 128, :])
            P.barrier()
        if self.on("post"):
            self.post_mix(s, l)
        if self.on("ffn") or self.on("ffnup"):
            with ExitStack() as st:
                uT = self.sbt(st, "uT", [128, 8, TOK], BF16)
                self.ln_mod(s, l, uT, 1)
                self.ffn_up(l, uT)
                P.barrier()
        if self.on("ffn") or self.on("ffndown"):
            self.ffn_down(s, l)
        P.barrier()

    def bc(self, st, src, n, name="bc"):
        t = self.sbt(st, name, [128, n])
        self.P.dma(t[:], src.partition_broadcast(128), w=[t])
        return t

    def resid_ln(self, t, py, mbc, g, b, tmp, stats, mv, rstd):
        P = self.P
        x = self.xres[t]
        for h in range(2):
            P.op("dve", lambda e, h=h: e.tensor_tensor(out=tmp[:, h * 512:(h + 1) * 512], in0=py[h][:], in1=mbc[:, h * 512:(h + 1) * 512], op=ALU.mult),
                 r=[py[h], mbc], w=[tmp])
        P.op("dve", lambda e: e.scalar_tensor_tensor(out=tmp[:], in0=x[:], scalar=ALPHA, in1=tmp[:], op0=ALU.mult, op1=ALU.add), r=[x, tmp], w=[tmp])
        self.ln_stats(tmp, stats, mv, rstd)
        P.op("dve", lambda e: e.tensor_scalar(out=tmp[:], in0=tmp[:], scalar1=mv[:, 0:1], scalar2=rstd[:, 0:1], op0=ALU.subtract, op1=ALU.mult),
             r=[tmp, mv, rstd], w=[tmp])
        P.op("pool", lambda e: e.tensor_tensor(out=tmp[:], in0=tmp[:], in1=g[:], op=ALU.mult), r=[tmp, g], w=[tmp])
        P.op("dve", lambda e: e.tensor_tensor(out=x[:], in0=tmp[:], in1=b[:], op=ALU.add), r=[tmp, b], w=[x])

    def post_mix(self, s, l):
        P, S, I = self.P, self.S, self.I
        with ExitStack() as st:
            wo = self.sbt(st, "wo", [128, 8, D], BF16)
            P.dma(wo[:], S["wb_out%d" % l].rearrange("(k p) n -> p k n", p=128), w=[wo])
            g = self.bc(st, I["ln1_g"][l], D); b = self.bc(st, I["ln1_b"][l], D)
            m_s = self.bc(st, S["modv%d" % l][s, 2 * D:3 * D], D); m_c = self.bc(st, S["modv%d" % l][4, 2 * D:3 * D], D)
            mx = Rot([self.sbt(st, "mx", [128, D]) for _ in range(3)])
            mxb = Rot([self.sbt(st, "mxb", [128, D], BF16) for _ in range(2)])
            mT = Rot([self.sbt(st, "mT", [128, 8, 128], BF16) for _ in range(2)])
            tmp = self.sbt(st, "tmp", [128, D])
            stats = self.sbt(st, "stats", [128, 2, 6]); mv = self.sbt(st, "mv", [128, 2]); rstd = self.sbt(st, "rstd", [128, 1])
            pt = Rot([self.pst(st, "ptr", [128, 8, 128], BF16) for _ in range(2)])
            py = Rot([[self.pst(st, "py", [128, 512]) for _ in range(2)] for _ in range(2)])
            for t in range(NT):
                a = mx.next(); ab = mxb.next(); m = mT.next(); p = pt.next(); y = py.next()
                P.dma(a[:], S["mix"][t * 128:(t + 1) * 128, :], w=[a])
                P.op("act", lambda e, a=a, ab=ab: e.copy(out=ab[:], in_=a[:]), r=[a], w=[ab])

                def tr(e, ab=ab, p=p):
                    for k in range(8):
                        i = e.transpose(out=p[:, k, :], in_=ab[:, k * 128:(k + 1) * 128], identity=self.identb[:])
                    return i
                P.op("pe", tr, r=[ab, self.identb], w=[p])
                P.op("act", lambda e, p=p, m=m: e.copy(out=m[:], in_=p[:]), r=[p], w=[m])
                for h in range(2):
                    def mm(e, m=m, y=y, h=h):
                        for k in range(8):
                            i = e.matmul(y[h][:], lhsT=m[:, k, :], rhs=wo[:, k, h * 512:(h + 1) * 512], start=(k == 0), stop=(k == 7))
                        return i
                    P.op("pe", mm, r=[m, wo], w=[y[h]])
                self.resid_ln(t, y, m_c if t < 2 else m_s, g, b, tmp, stats, mv, rstd)
            P.barrier()

    def ffn_up(self, l, uT):
        P, S, I = self.P, self.S, self.I
        NJ = DFF // 128
        chunks = [(0, 256), (256, 768), (768, 1280), (1280, 1792), (1792, 2304)]
        with ExitStack() as st:
            fcw = self.sbt(st, "fcw", [128, 3, NJ]); fcb = self.sbt(st, "fcb", [128, NJ])
            for tap in range(3):
                P.dma(fcw[:, tap, :], I["ffn_conv_w"][l, tap].rearrange("(j p) -> p j", p=128), w=[fcw], allow_slow_non_contiguous=True)
            P.dma(fcb[:], I["ffn_conv_b"][l].rearrange("(j p) -> p j", p=128), w=[fcb], allow_slow_non_contiguous=True)
            wa = Rot([self.sbt(st, "wa", [128, 8, 128], BF16) for _ in range(2)])
            wb = Rot([self.sbt(st, "wb", [128, 8, 128], BF16) for _ in range(2)])
            aS = Rot([self.sbt(st, "aS", [128, NROWS]) for _ in range(2)])
            bS = Rot([self.sbt(st, "bS", [128, NROWS]) for _ in range(2)])
            cv = Rot([self.sbt(st, "cv", [128, NROWS]) for _ in range(2)])
            gS = Rot([self.sbt(st, "gS", [128, NROWS], BF16) for _ in range(2)])
            pa = Rot([self.pst(st, "pa", [128, 512]) for _ in range(4)])
            pb = Rot([self.pst(st, "pb", [128, 512]) for _ in range(4)])
            for a in aS.items + bS.items:
                P.op("pool", lambda e, a=a: e.memset(a[:], 0.0), w=[a])
            for j in range(NJ):
                w1 = wa.next(); w2 = wb.next(); a = aS.next(); b = bS.next(); c = cv.next(); g = gS.next()
                P.dma(w1[:], S["wb_up%d" % l][:, j * 128:(j + 1) * 128].rearrange("(k p) n -> p k n", p=128), w=[w1])
                P.dma(w2[:], S["wb_up%d" % l][:, DFF + j * 128:DFF + (j + 1) * 128].rearrange("(k p) n -> p k n", p=128), w=[w2])
                for (c0, c1) in chunks:
                    n = c1 - c0
                    o0 = c0 + 1 if c0 < 256 else c0 + 2
                    p1 = pa.next(); p2 = pb.next()

                    def mm(e, w=w1, p=p1, c0=c0, c1=c1, n=n):
                        for k in range(8):
                            i = e.matmul(p[:, 0:n], lhsT=w[:, k, :], rhs=uT[:, k, c0:c1], start=(k == 0), stop=(k == 7))
                        return i
                    P.op("pe", mm, r=[w1] + [(uT, t) for t in range(NT)], w=[p1])

                    def mm2(e, w=w2, p=p2, c0=c0, c1=c1, n=n):
                        for k in range(8):
                            i = e.matmul(p[:, 0:n], lhsT=w[:, k, :], rhs=uT[:, k, c0:c1], start=(k == 0), stop=(k == 7))
                        return i
                    P.op("pe", mm2, r=[w2] + [(uT, t) for t in range(NT)], w=[p2])
                    P.op("act", lambda e, p=p1, a=a, o0=o0, n=n: e.copy(out=a[:, o0:o0 + n], in_=p[:, 0:n]), r=[p1], w=[a])
                    P.op("dve", lambda e, p=p2, b=b, o0=o0, n=n: e.tensor_copy(out=b[:, o0:o0 + n], in_=p[:, 0:n]), r=[p2], w=[b])
                W = NROWS - 2
                P.op("dve", lambda e, a=a, c=c, j=j: e.tensor_scalar(out=c[:, 1:1 + W], in0=a[:, 0:W], scalar1=fcw[:, 0, j:j + 1], scalar2=fcb[:, j:j + 1], op0=ALU.mult, op1=ALU.add),
                     r=[a, fcw, fcb], w=[c])
                P.op("dve", lambda e, a=a, c=c, j=j: e.scalar_tensor_tensor(out=c[:, 1:1 + W], in0=a[:, 1:1 + W], scalar=fcw[:, 1, j:j + 1], in1=c[:, 1:1 + W], op0=ALU.mult, op1=ALU.add),
                     r=[a, fcw, c], w=[c])
                P.op("dve", lambda e, a=a, c=c, j=j: e.scalar_tensor_tensor(out=c[:, 1:1 + W], in0=a[:, 2:2 + W], scalar=fcw[:, 2, j:j + 1], in1=c[:, 1:1 + W], op0=ALU.mult, op1=ALU.add),
                     r=[a, fcw, c], w=[c])
                P.op("act", lambda e, c=c: e.activation(out=c[:, 1:1 + W], in_=c[:, 1:1 + W], func=AF.Silu), r=[c], w=[c])
                P.op("pool", lambda e, c=c, b=b, g=g: e.tensor_tensor(out=g[:, 1:1 + W], in0=c[:, 1:1 + W], in1=b[:, 1:1 + W], op=ALU.mult), r=[c, b], w=[g])
                P.dma(S["gT"][j * 128:(j + 1) * 128, 0:256], g[:, 1:257], r=[g], eng="act")
                P.dma(S["gT"][j * 128:(j + 1) * 128, 256:TOK], g[:, 258:258 + SEQ], r=[g], eng="act")

    def ffn_down(self, s, l):
        P, S, I = self.P, self.S, self.I
        NJ = DFF // 128
        with ExitStack() as st:
            wd = self.sbt(st, "wd", [128, NJ, D], BF16)
            for j in range(NJ):
                P.dma(wd[:, j, :], S["wb_dn%d" % l][j * 128:(j + 1) * 128, :], w=[wd])
            g = self.bc(st, I["ln2_g"][l], D); b = self.bc(st, I["ln2_b"][l], D)
            m_s = self.bc(st, S["modv%d" % l][s, 5 * D:6 * D], D); m_c = self.bc(st, S["modv%d" % l][4, 5 * D:6 * D], D)
            gt = Rot([self.sbt(st, "gt", [128, NJ, 128], BF16) for _ in range(3)])
            tmp = self.sbt(st, "tmp", [128, D])
            stats = self.sbt(st, "stats", [128, 2, 6]); mv = self.sbt(st, "mv", [128, 2]); rstd = self.sbt(st, "rstd", [128, 1])
            py = Rot([[self.pst(st, "py", [128, 512]) for _ in range(2)] for _ in range(2)])
            for t in range(NT):
                gg = gt.next(); y = py.next()
                for j0 in (0, 6, 12, 18):
                    j1 = min(NJ, j0 + 6)
                    P.dma(gg[:, j0:j1, :], S["gT"][j0 * 128:j1 * 128, t * 128:(t + 1) * 128].rearrange("(j p) n -> p j n", p=128), w=[gg])
                for h in range(2):
                    def mm(e, gg=gg, y=y, h=h):
                        for j in range(NJ):
                            i = e.matmul(y[h][:], lhsT=gg[:, j, :], rhs=wd[:, j, h * 512:(h + 1) * 512], start=(j == 0), stop=(j == NJ - 1))
                        return i
                    P.op("pe", mm, r=[gg, wd], w=[y[h]])
                self.resid_ln(t, y, m_c if t < 2 else m_s, g, b, tmp, stats, mv, rstd)
            P.barrier()


    def finish_heads(self, st_tiles, hacc, gate, gfunc, ng, nb, t, col, extra=None):
        P = self.P
        s1, s2, gm, grs, sq, gout = st_tiles
        hv = hacc[:].rearrange("p (h f) -> p h f", f=64)
        gv = gout[:].rearrange("p (h f) -> p h f", f=64)
        bcl = lambda a: a.unsqueeze(2).to_broadcast([128, 4, 64])
        P.op("dve", lambda e: e.tensor_reduce(out=s1[:], in_=hv, axis=AX.X, op=ALU.add), r=[hacc], w=[s1])
        P.op("pool", lambda e: e.tensor_tensor(out=sq[:], in0=hacc[:], in1=hacc[:], op=ALU.mult), r=[hacc], w=[sq])
        P.op("dve", lambda e: e.tensor_reduce(out=s2[:], in_=sq[:].rearrange("p (h f) -> p h f", f=64), axis=AX.X, op=ALU.add), r=[sq], w=[s2])
        P.op("dve", lambda e: e.tensor_scalar_mul(out=gm[:], in0=s1[:], scalar1=1.0 / 64), r=[s1], w=[gm])
        P.op("dve", lambda e: e.tensor_tensor(out=s1[:], in0=gm[:], in1=gm[:], op=ALU.mult), r=[gm, s1], w=[s1])
        P.op("dve", lambda e: e.scalar_tensor_tensor(out=grs[:], in0=s2[:], scalar=1.0 / 64, in1=s1[:], op0=ALU.mult, op1=ALU.subtract), r=[s2, s1], w=[grs])
        P.op("act", lambda e: e.activation(out=grs[:], in_=grs[:], func=AF.Sqrt, bias=self.epsc[:, 0:1], scale=1.0), r=[grs, self.epsc], w=[grs])
        P.op("dve", lambda e: e.reciprocal(out=grs[:], in_=grs[:]), r=[grs], w=[grs])
        P.op("dve", lambda e: e.tensor_tensor(out=gv, in0=hv, in1=bcl(gm[:]), op=ALU.subtract), r=[hacc, gm], w=[gout])
        P.op("dve", lambda e: e.tensor_tensor(out=gv, in0=gv, in1=bcl(grs[:]), op=ALU.mult), r=[gout, grs], w=[gout])
        P.op("pool", lambda e: e.tensor_tensor(out=gout[:], in0=gout[:], in1=ng[:], op=ALU.mult), r=[gout, ng], w=[gout])
        P.op("pool", lambda e: e.tensor_tensor(out=gout[:], in0=gout[:], in1=nb[:], op=ALU.add), r=[gout, nb], w=[gout])
        if extra is not None:
            P.op("pool", lambda e: e.tensor_tensor(out=gout[:], in0=gout[:], in1=extra[:], op=ALU.add), r=[gout, extra], w=[gout])
        if gfunc is not None:
            P.op("act", lambda e: e.activation(out=gate, in_=gate, func=gfunc), r=[gate], w=[gate])
        if gate is not None:
            P.op("dve", lambda e: e.tensor_tensor(out=gout[:], in0=gout[:], in1=gate, op=ALU.mult), r=[gout, gate], w=[gout])
        P.dma(self.S["mix"][t * 128:(t + 1) * 128, col:col + 256], gout[:], r=[gout], eng="act")

    def head_tiles(self, st):
        return (self.sbt(st, "gs1", [128, 4]), self.sbt(st, "gs2", [128, 4]), self.sbt(st, "ggm", [128, 4]), self.sbt(st, "grs", [128, 4]),
                self.sbt(st, "gsq", [128, 256]), self.sbt(st, "gout", [128, 256]))

    @staticmethod
    def order(d):
        return list(range(NT)) if d == 0 else [1, 0] + list(range(NT - 1, 1, -1))


    def wrap_sin(self, h, tmp, L):
        P = self.P
        for _ in range(2):
            P.op("dve", lambda e: e.tensor_scalar(out=tmp[:], in0=h[:], scalar1=-math.pi, scalar2=2.0 * math.pi, op0=ALU.is_lt, op1=ALU.mult), r=[h], w=[tmp])
            P.op("dve", lambda e: e.tensor_tensor(out=h[:], in0=h[:], in1=tmp[:], op=ALU.add), r=[h, tmp], w=[h])
            P.op("dve", lambda e: e.tensor_scalar(out=tmp[:], in0=h[:], scalar1=math.pi, scalar2=-2.0 * math.pi, op0=ALU.is_gt, op1=ALU.mult), r=[h], w=[tmp])
            P.op("dve", lambda e: e.tensor_tensor(out=h[:], in0=h[:], in1=tmp[:], op=ALU.add), r=[h, tmp], w=[h])
        P.op("act", lambda e: e.activation(out=h[:], in_=h[:], func=AF.Sin), r=[h], w=[h])

    def stage_hyena(self):
        for l in self.layers:
            for kind, L in (("lat", SEQ), ("ctx", CTX)):
                self.stage_hyena_one(l, kind, L)

    def stage_hyena_one(self, l, kind, L):
        P, S, I = self.P, self.S, self.I
        if True:
            if True:
                nft = L // 128
                with ExitStack() as st:
                    w1 = self.sbt(st, "hw1", [33, 64]); w2 = self.sbt(st, "hw2", [64, 64]); w3 = self.sbt(st, "hw3", [64, 1024])
                    b1 = self.sbt(st, "hb1", [64, 1]); b2 = self.sbt(st, "hb2", [64, 1]); fr = self.sbt(st, "hfr", [64, 1])
                    fT = self.sbt(st, "hfT", [33, L]); h1 = self.sbt(st, "hh1", [64, L]); h2 = self.sbt(st, "hh2", [64, L]); tmp = self.sbt(st, "htmp", [64, L])
                    P.dma(w1[:], I["hyena_w1"][l], w=[w1]); P.dma(w2[:], I["hyena_w2"][l], w=[w2]); P.dma(w3[:], I["hyena_w3"][l], w=[w3])
                    P.dma(b1[:], I["hyena_b1"][l].rearrange("(p o) -> p o", o=1), w=[b1])
                    P.dma(b2[:], I["hyena_b2"][l].rearrange("(p o) -> p o", o=1), w=[b2])
                    P.dma(fr[:], I["hyena_freq"][l].rearrange("(p o) -> p o", o=1), w=[fr])
                    P.dma(fT[:], I["k_hfeat_" + kind], w=[fT])
                    pb = Rot([self.bank(st, "hps") for _ in range(4)])
                    for (w, src, bb, dst) in ((w1, fT, b1, h1), (w2, h1, b2, h2)):
                        for c0 in range(0, L, 512):
                            n = min(512, L - c0)
                            p = pb.next()
                            P.op("pe", lambda e, p=p, w=w, src=src, c0=c0, n=n: e.matmul(p[0:64, 0:n], lhsT=w[:], rhs=src[:, c0:c0 + n], start=True, stop=True), r=[w, src], w=[p])
                            P.op("dve", lambda e, p=p, dst=dst, bb=bb, c0=c0, n=n: e.tensor_scalar(out=dst[:, c0:c0 + n], in0=p[0:64, 0:n], scalar1=bb[:, 0:1], scalar2=fr[:, 0:1], op0=ALU.add, op1=ALU.mult),
                                 r=[p, bb, fr], w=[dst])
                        self.wrap_sin(dst, tmp, L)
                    Pm = self.sbt(st, "hPm", [128, nft, 512], BF16); Mm = self.sbt(st, "hMm", [128, nft, 512], BF16)
                    win = Rot([self.sbt(st, "hwin", [128, 256]) for _ in range(2)])
                    hf = self.sbt(st, "hhf", [128, 512]); hb = self.sbt(st, "hhb", [128, 512])
                    for tt in range(nft):
                        wn = win.next(); pA = pb.next(); pB = pb.next()
                        P.dma(wn[:], I["k_hwin_" + kind][tt * 128:(tt + 1) * 128, :], w=[wn])
                        P.op("pe", lambda e, pA=pA, tt=tt: e.matmul(pA[:], lhsT=h2[:, tt * 128:(tt + 1) * 128], rhs=w3[:, 0:512], start=True, stop=True), r=[h2, w3], w=[pA])
                        P.op("pe", lambda e, pB=pB, tt=tt: e.matmul(pB[:], lhsT=h2[:, tt * 128:(tt + 1) * 128], rhs=w3[:, 512:1024], start=True, stop=True), r=[h2, w3], w=[pB])
                        for o in range(2):
                            P.op("dve", lambda e, pA=pA, wn=wn, o=o: e.tensor_tensor(out=hf[:, o * 256:(o + 1) * 256], in0=pA[:, o * 256:(o + 1) * 256], in1=wn[:], op=ALU.mult), r=[pA, wn], w=[hf])
                            P.op("dve", lambda e, pB=pB, wn=wn, o=o: e.tensor_tensor(out=hb[:, o * 256:(o + 1) * 256], in0=pB[:, o * 256:(o + 1) * 256], in1=wn[:], op=ALU.mult), r=[pB, wn], w=[hb])
                        P.op("pool", lambda e, tt=tt: e.tensor_tensor(out=Pm[:, tt, :], in0=hf[:], in1=hb[:], op=ALU.add), r=[hf, hb], w=[Pm])
                        P.op("pool", lambda e, tt=tt: e.tensor_tensor(out=Mm[:, tt, :], in0=hf[:], in1=hb[:], op=ALU.subtract), r=[hf, hb], w=[Mm])
                    bas = Rot([self.sbt(st, "hbas", [128, 2, nft, 128], BF16) for _ in range(2)])
                    tab = Rot([self.sbt(st, "htab", [128, 3, 512]) for _ in range(2)])
                    for ft in range(nft):
                        ba = bas.next(); tb = tab.next(); pK = pb.next(); pI = pb.next()
                        P.dma(ba[:], I["k_hF_" + kind][ft], w=[ba])

                        def mk(e, ba=ba, p=pK, ri=0, src=Pm):
                            for ch in range(nft):
                                i = e.matmul(p[:], lhsT=ba[:, ri, ch, :], rhs=src[:, ch, :], start=(ch == 0), stop=(ch == nft - 1))
                            return i
                        P.op("pe", mk, r=[ba, Pm], w=[pK])
                        P.op("pe", lambda e, ba=ba, pI=pI, mk=mk: mk(e, ba=ba, p=pI, ri=1, src=Mm), r=[ba, Mm], w=[pI])
                        P.op("act", lambda e, tb=tb, pK=pK: e.copy(out=tb[:, 0, :], in_=pK[:]), r=[pK], w=[tb])
                        P.op("dve", lambda e, tb=tb, pK=pK: e.tensor_copy(out=tb[:, 2, :], in_=pK[:]), r=[pK], w=[tb])
                        P.op("act", lambda e, tb=tb, pI=pI: e.copy(out=tb[:, 1, :], in_=pI[:]), r=[pI], w=[tb])
                        if ft == 0:
                            pN = pb.next()
                            P.op("pe", lambda e, ba=ba, pN=pN, mk=mk: mk(e, ba=ba, p=pN, ri=1, src=Pm), r=[ba, Pm], w=[pN])
                            P.op("pool", lambda e, tb=tb: e.memset(tb[0:1, 1, :], 0.0), r=[tb], w=[tb])
                            P.op("dve", lambda e, tb=tb, pN=pN: e.tensor_copy(out=tb[0:1, 2, :], in_=pN[0:1, :]), r=[pN, tb], w=[tb])
                        P.dma(S["HT%d_%s" % (l, kind)][ft], tb[:], r=[tb], eng="act")
                    P.barrier()

    def mix_hyena(self, s, l):
        for kind, tiles in (("lat", list(range(2, NT))), ("ctx", [0, 1])):
            self.mix_hyena_one(s, l, kind, tiles)

    def mix_hyena_one(self, s, l, kind, tiles):
        P, S, I = self.P, self.S, self.I
        C0 = 1040
        if True:
            nft = len(tiles)
            with ExitStack() as st:
                store = self.sbt(st, "hst", [128, nft, 768])
                Vb = self.sbt(st, "hVb", [128, nft, 256], BF16)
                with ExitStack() as st2:
                    cw = self.sbt(st2, "hcw", [128, 3, 768]); cb = self.bc(st2, I["hyena_conv_b"][l], 768)
                    for tap in range(3):
                        P.dma(cw[:, tap, :], I["hyena_conv_w"][l, tap].partition_broadcast(128), w=[cw])
                    q3 = Rot([self.sbt(st2, "hq3", [128, 3, 768]) for _ in range(2)])
                    tmp = self.sbt(st2, "hctmp", [128, 768])
                    for i, t in enumerate(tiles):
                        q = q3.next(); rb = rowbase(t)
                        for tap in range(3):
                            P.dma(q[:, tap, :], S["proj"][rb - 1 + tap:rb - 1 + tap + 128, C0:C0 + 768], w=[q])
                        sti = store[:, i, :]; K = (store, i)
                        P.op("dve", lambda e, q=q, sti=sti: e.tensor_tensor(out=sti, in0=q[:, 0, :], in1=cw[:, 0, :], op=ALU.mult), r=[q, cw], w=[K])
                        P.op("pool", lambda e, q=q: e.tensor_tensor(out=tmp[:], in0=q[:, 1, :], in1=cw[:, 1, :], op=ALU.mult), r=[q, cw], w=[tmp])
                        P.op("dve", lambda e, sti=sti: e.tensor_tensor(out=sti, in0=sti, in1=tmp[:], op=ALU.add), r=[K, tmp], w=[K])
                        P.op("pool", lambda e, q=q: e.tensor_tensor(out=tmp[:], in0=q[:, 2, :], in1=cw[:, 2, :], op=ALU.mult), r=[q, cw], w=[tmp])
                        P.op("dve", lambda e, sti=sti: e.tensor_tensor(out=sti, in0=sti, in1=tmp[:], op=ALU.add), r=[K, tmp], w=[K])
                        P.op("pool", lambda e, sti=sti: e.tensor_tensor(out=sti, in0=sti, in1=cb[:], op=ALU.add), r=[K, cb], w=[K])
                        P.op("act", lambda e, i=i: e.copy(out=Vb[:, i, :], in_=store[:, i, 0:256]), r=[K], w=[(Vb, i)])
                    P.barrier()
                ng = self.bc(st, I["out_norm_g"][l, 256:512], 256); nb = self.bc(st, I["out_norm_b"][l, 256:512], 256)
                dd = self.sbt(st, "hdd", [128, 2, 256])
                for o in range(2):
                    P.dma(dd[:, o, :], I["hyena_d"][l, o].partition_broadcast(128), w=[dd])
                Ys = self.sbt(st, "hYs", [128, nft, 2, 256], BF16)
                bas = Rot([self.sbt(st, "hbas", [128, 2, nft, 128], BF16) for _ in range(4)])
                tab = Rot([self.sbt(st, "htab", [128, 3, 256]) for _ in range(4)])
                t1 = self.sbt(st, "ht1", [128, 256]); t2 = self.sbt(st, "ht2", [128, 256]); t3 = self.sbt(st, "ht3", [128, 256]); t4 = self.sbt(st, "ht4", [128, 256])
                yy = Rot([self.sbt(st, "hyy", [128, 256]) for _ in range(2)])
                ht = self.head_tiles(st)
                pV = Rot([self.bank(st, "hpV") for _ in range(2)]); pY = Rot([self.bank(st, "hpY") for _ in range(2)])
                allV = [(Vb, i) for i in range(nft)]; allY = [(Ys, i) for i in range(nft)]
                for o in range(2):
                    for ft in range(nft):
                        ba = bas.next(); tb = tab.next(); p = pV.next()
                        P.dma(ba[:], I["k_hF_" + kind][ft], w=[ba])
                        P.dma(tb[:], S["HT%d_%s" % (l, kind)][ft][:, :, o * 256:(o + 1) * 256], w=[tb])

                        def fw(e, ba=ba, p=p):
                            for ri in range(2):
                                for ch in range(nft):
                                    i = e.matmul(p[:, ri * 256:(ri + 1) * 256], lhsT=ba[:, ri, ch, :], rhs=Vb[:, ch, :], start=(ch == 0), stop=(ch == nft - 1))
                            return i
                        P.op("pe", fw, r=[ba] + allV, w=[p])
                        P.op("dve", lambda e, p=p, tb=tb: e.tensor_tensor(out=t1[:], in0=p[:, 0:256], in1=tb[:, 0, :], op=ALU.mult), r=[p, tb], w=[t1])
                        P.op("dve", lambda e, p=p, tb=tb: e.tensor_tensor(out=t2[:], in0=p[:, 256:512], in1=tb[:, 1, :], op=ALU.mult), r=[p, tb], w=[t2])
                        P.op("pool", lambda e, ft=ft: e.tensor_tensor(out=Ys[:, ft, 0, :], in0=t1[:], in1=t2[:], op=ALU.subtract), r=[t1, t2], w=[(Ys, ft)])
                        P.op("dve", lambda e, p=p, tb=tb: e.tensor_tensor(out=t3[:], in0=p[:, 0:256], in1=tb[:, 1, :], op=ALU.mult), r=[p, tb], w=[t3])
                        P.op("dve", lambda e, p=p, tb=tb: e.tensor_tensor(out=t4[:], in0=p[:, 256:512], in1=tb[:, 2, :], op=ALU.mult), r=[p, tb], w=[t4])
                        P.op("pool", lambda e, ft=ft: e.tensor_tensor(out=Ys[:, ft, 1, :], in0=t3[:], in1=t4[:], op=ALU.add), r=[t3, t4], w=[(Ys, ft)])
                    for i, t in enumerate(tiles):
                        ba = bas.next(); p = pY.next(); K = (store, i)
                        P.dma(ba[:], I["k_hB_" + kind][i], w=[ba])

                        def iv(e, ba=ba, p=p):
                            n = 0
                            for fc in range(nft):
                                for ri in range(2):
                                    ins = e.matmul(p[:, 0:256], lhsT=ba[:, ri, fc, :], rhs=Ys[:, fc, ri, :], start=(n == 0), stop=(n == 2 * nft - 1))
                                    n += 1
                            return ins
                        P.op("pe", iv, r=[ba] + allY, w=[p])
                        P.op("pool", lambda e, i=i, o=o: e.tensor_tensor(out=t1[:], in0=store[:, i, 0:256], in1=dd[:, o, :], op=ALU.mult), r=[K, dd], w=[t1])
                        P.op("dve", lambda e, p=p: e.tensor_tensor(out=t1[:], in0=p[:, 0:256], in1=t1[:], op=ALU.add), r=[p, t1], w=[t1])
                        if o == 0:
                            P.op("dve", lambda e, i=i: e.tensor_tensor(out=store[:, i, 0:256], in0=store[:, i, 256:512], in1=t1[:], op=ALU.mult), r=[K, t1], w=[K])
                            P.op("act", lambda e, i=i: e.copy(out=Vb[:, i, :], in_=store[:, i, 0:256]), r=[K], w=[(Vb, i)])
                        else:
                            y_ = yy.next()
                            P.op("dve", lambda e, i=i, y_=y_: e.tensor_tensor(out=y_[:], in0=store[:, i, 512:768], in1=t1[:], op=ALU.mult), r=[K, t1], w=[y_])
                            self.finish_heads(ht, y_, None, None, ng, nb, t, 256)
                P.barrier()

    def TT(self, eng, out, a, b, op, r, w):
        self.P.op(eng, lambda e: e.tensor_tensor(out=out, in0=a, in1=b, op=op), r=r, w=w)

    def TS(self, eng, out, a, s1, s2, op0, op1, r, w):
        self.P.op(eng, lambda e: e.tensor_scalar(out=out, in0=a, scalar1=s1, scalar2=s2, op0=op0, op1=op1), r=r, w=w)

    def TSM(self, eng, out, a, s1, r, w):
        self.P.op(eng, lambda e: e.tensor_scalar_mul(out=out, in0=a, scalar1=s1), r=r, w=w)

    def STT(self, out, a, sc, b, op0, op1, r, w):
        self.P.op("dve", lambda e: e.scalar_tensor_tensor(out=out, in0=a, scalar=sc, in1=b, op0=op0, op1=op1), r=r, w=w)

    def ACT(self, out, a, func, r, w, bias=None, scale=1.0):
        if bias is None:
            self.P.op("act", lambda e: e.activation(out=out, in_=a, func=func, scale=scale), r=r, w=w)
        else:
            self.P.op("act", lambda e: e.activation(out=out, in_=a, func=func, bias=bias, scale=scale), r=r, w=w)

    def CP(self, eng, out, a, r, w):
        if eng == "act":
            self.P.op("act", lambda e: e.copy(out=out, in_=a), r=r, w=w)
        else:
            self.P.op(eng, lambda e: e.tensor_copy(out=out, in_=a), r=r, w=w)

    def MM(self, specs, r, w):
        specs = list(specs)

        def f(e):
            for (o, l_, r_, st_, sp_) in specs:
                i = e.matmul(o, lhsT=l_, rhs=r_, start=st_, stop=sp_)
            return i
        self.P.op("pe", f, r=r, w=w)

    def mix_rwkv(self, s, l):
        P, S, I = self.P, self.S, self.I
        C0 = 2832
        EM05 = math.exp(-0.5)
        with ExitStack() as st:
            ng = self.bc(st, I["out_norm_g"][l, 768:1024], 256); nb = self.bc(st, I["out_norm_b"][l, 768:1024], 256)
            mu = self.sbt(st, "rmu", [128, 2, 896]); w0 = self.sbt(st, "rw0", [128, 2, 256]); a0 = self.sbt(st, "ra0", [128, 2, 256])
            for j in range(2):
                P.dma(mu[:, j, :], I["rwkv_mu"][l, j].partition_broadcast(128), w=[mu])
                P.dma(w0[:, j, :], I["rwkv_w0"][l, j].partition_broadcast(128), w=[w0])
                P.dma(a0[:, j, :], I["rwkv_a0"][l, j].partition_broadcast(128), w=[a0])
            kks = self.bc(st, I["rwkv_kk"][l], 256); ka = self.bc(st, I["rwkv_ka"][l], 256)
            rk = self.bc(st, I["rwkv_rk"][l].rearrange("a b -> (a b)"), 256)
            Wl = self.sbt(st, "rWl", [128, 1280])
            P.op("pool", lambda e: e.memset(Wl[:], 0.0), w=[Wl])
            for j in range(2):
                P.dma(Wl[0:32, j * 256:(j + 1) * 256], I["rwkv_w2"][l, j], w=[Wl])
                P.dma(Wl[32:64, 512 + j * 256:512 + (j + 1) * 256], I["rwkv_a2"][l, j], w=[Wl])
            P.dma(Wl[64:128, 1024:1280], I["rwkv_g2"][l], w=[Wl])
            cmask = self.sbt(st, "rcm", [128, 2, 256])
            for d in range(2):
                self.CP("dve", cmask[:, d, 0:128], self.tri[:, 2 + d, :], [self.tri], [cmask])
                self.CP("dve", cmask[:, d, 128:256], self.tri[:, d, :], [self.tri], [cmask])
            eps12 = self.sbt(st, "reps", [128, 1])
            P.op("pool", lambda e: e.memset(eps12[:], 1e-12), w=[eps12])
            hacc = [self.sbt(st, "hacc", [128, 256]) for _ in range(NT)]
            ht = self.head_tiles(st)
            q3 = Rot([self.sbt(st, "rq3", [128, 3, 896]) for _ in range(1)])
            z = self.sbt(st, "rz", [128, 896])
            Lt = self.sbt(st, "rLt", [128, 128]); LT = self.sbt(st, "rLT", [128, 128])
            F = {n: self.sbt(st, "r" + n, [128, 256]) for n in ("logw", "a", "kk", "kd", "t1", "t2", "cum", "gam", "game", "ginv")}
            ss = self.sbt(st, "rss", [128, 4]); s4 = self.sbt(st, "rs4", [128, 4])
            X4 = Rot([self.sbt(st, "rX4", [128, 4, 256], BF16) for _ in range(3)])
            Vb = Rot([self.sbt(st, "rVb", [128, 256], BF16) for _ in range(3)])
            TTt = Rot([self.sbt(st, "rTT", [128, 8, 128], BF16) for _ in range(2)])
            ZP = Rot([self.sbt(st, "rZP", [128, 4, 2, 128], BF16) for _ in range(2)])
            for zp in ZP.items:
                P.op("pool", lambda e, zp=zp: e.memset(zp[:], 0.0), w=[zp])
            GC = [Rot([self.sbt(st, "rgc", [128, 1]) for _ in range(3)]) for _ in range(2)]
            FG = Rot([self.sbt(st, "rFg", [128, 256]) for _ in range(3)]); FB = Rot([self.sbt(st, "rFb", [128, 256]) for _ in range(3)])
            rwm = self.sbt(st, "rwm", [128, 2, 8, 128], BF16)
            P.dma(rwm[:], I["k_rwm"], w=[rwm])
            t4 = lambda nm_, n_: Rot([self.sbt(st, nm_, [128, 4, 128], BF16) for _ in range(n_)])
            RX = t4("rX", 3); RXT = t4("rXT", 3); RQ = t4("rQ", 3); RQT = t4("rQT", 3); RE = t4("rE", 2); RE2 = t4("rE2", 2)
            AdN4 = self.sbt(st, "rAdN", [128, 4, 128], BF16); AdT4 = self.sbt(st, "rAdT", [128, 4, 128], BF16)
            LoN4 = [self.sbt(st, "rLoN", [128, 4, 128], BF16) for _ in range(3)]; LoT4 = [self.sbt(st, "rLoT", [128, 4, 128], BF16) for _ in range(3)]
            RbTr = t4("rRb", 2); PTr = t4("rPT", 2)
            SKr = Rot([self.sbt(st, "rSK", [128, 2, 4, 128], BF16) for _ in range(2)])
            Wb = [self.sbt(st, "rWb", [128, 128], BF16) for _ in range(2)]
            Ub = [self.sbt(st, "rUb", [128, 128], BF16) for _ in range(2)]
            Hf = [self.sbt(st, "rHf", [128, 128]) for _ in range(2)]
            Hb = [self.sbt(st, "rHb", [128, 128], BF16) for _ in range(2)]
            htmp = self.sbt(st, "rht", [128, 128])
            pT = self.pst(st, "rpT", [128, 8, 128], BF16)
            pg = Rot([self.bank(st, "rpg") for _ in range(1)])
            pi = Rot([self.bank(st, "rpi") for _ in range(4)])
            pq = Rot([self.bank(st, "rpq") for _ in range(2)])
            for d in range(2):
                for pr in range(2):
                    P.op("pool", lambda e, pr=pr: e.memset(Hf[pr][:], 0.0), w=[Hf[pr]])
                    P.op("pool", lambda e, pr=pr: e.memset(Hb[pr][:], 0.0), w=[Hb[pr]])
                def prep(t):
                    q = q3.next(); rb = rowbase(t)
                    Fg = FG.next(); Fbon = FB.next(); gcol = [GC[0].next(), GC[1].next()]
                    for tap in range(3):
                        P.dma(q[:, tap, :], S["proj"][rb - 1 + tap:rb - 1 + tap + 128, C0:C0 + 896], w=[q])
                    self.TT("dve", q[:, 0, :], q[:, 0, :], q[:, 1, :], ALU.subtract, [q], [q])
                    self.TT("pool", q[:, 0, :], q[:, 0, :], mu[:, 0, :], ALU.mult, [q, mu], [q])
                    self.TT("pool", q[:, 2, :], q[:, 2, :], q[:, 1, :], ALU.subtract, [q], [q])
                    self.TT("pool", q[:, 2, :], q[:, 2, :], mu[:, 1, :], ALU.mult, [q, mu], [q])
                    self.TT("dve", z[:], q[:, 1, :], q[:, 0, :], ALU.add, [q], [z])
                    self.TT("dve", z[:], z[:], q[:, 2, :], ALU.add, [z, q], [z])
                    r_ = z[:, 0:256]; k_ = z[:, 256:512]; v_ = z[:, 512:768]
                    self.ACT(Lt[:, 0:32], z[:, 768:800], AF.Tanh, [z], [Lt])
                    self.CP("dve", Lt[:, 32:64], z[:, 800:832], [z], [Lt])
                    self.ACT(Lt[:, 64:128], z[:, 832:896], AF.Sigmoid, [z], [Lt])
                    p0 = pg.next()
                    P.op("pe", lambda e, p0=p0: e.transpose(out=p0[:, 0:128], in_=Lt[:], identity=self.identf[:]), r=[Lt, self.identf], w=[p0])
                    self.CP("dve", LT[:], p0[:, 0:128], [p0], [LT])
                    p1 = pg.next()
                    self.MM([(p1[:, 0:256], LT[:], Wl[:, d * 256:(d + 1) * 256], True, True),
                             (p1[:, 256:512], LT[:], Wl[:, 512 + d * 256:512 + (d + 1) * 256], True, True)], [LT, Wl], [p1])
                    self.TT("dve", F["logw"][:], p1[:, 0:256], w0[:, d, :], ALU.add, [p1, w0], [F["logw"]])
                    self.ACT(F["logw"][:], F["logw"][:], AF.Sigmoid, [F["logw"]], [F["logw"]])
                    self.TSM("dve", F["logw"][:], F["logw"][:], -EM05, [F["logw"]], [F["logw"]])
                    self.TT("dve", F["a"][:], p1[:, 256:512], a0[:, d, :], ALU.add, [p1, a0], [F["a"]])
                    self.ACT(F["a"][:], F["a"][:], AF.Sigmoid, [F["a"]], [F["a"]])
                    if d == 1:
                        p2 = pg.next()
                        self.MM([(p2[:, 0:256], LT[:], Wl[:, 1024:1280], True, True)], [LT, Wl], [p2])
                        self.CP("act", Fg[:], p2[:, 0:256], [p2], [Fg])
                        self.TT("pool", F["t1"][:], r_, k_, ALU.mult, [z], [F["t1"]])
                        self.TT("pool", F["t1"][:], F["t1"][:], rk[:], ALU.mult, [F["t1"], rk], [F["t1"]])
                        P.op("dve", lambda e: e.tensor_reduce(out=s4[:], in_=F["t1"][:].rearrange("p (h f) -> p h f", f=64), axis=AX.X, op=ALU.add), r=[F["t1"]], w=[s4])
                        for h in range(4):
                            self.TSM("pool", Fbon[:, h * 64:(h + 1) * 64], z[:, 512 + h * 64:512 + (h + 1) * 64], s4[:, h:h + 1], [z, s4], [Fbon])
                    self.TT("pool", F["kk"][:], k_, kks[:], ALU.mult, [z, kks], [F["kk"]])
                    self.TT("pool", F["t2"][:], F["kk"][:], F["kk"][:], ALU.mult, [F["kk"]], [F["t2"]])
                    P.op("dve", lambda e: e.tensor_reduce(out=ss[:], in_=F["t2"][:].rearrange("p (h f) -> p h f", f=64), axis=AX.X, op=ALU.add), r=[F["t2"]], w=[ss])
                    self.ACT(ss[:], ss[:], AF.Sqrt, [ss, eps12], [ss], bias=eps12[:, 0:1])
                    P.op("dve", lambda e: e.reciprocal(out=ss[:], in_=ss[:]), r=[ss], w=[ss])
                    for h in range(4):
                        self.TSM("pool", F["kk"][:, h * 64:(h + 1) * 64], F["kk"][:, h * 64:(h + 1) * 64], ss[:, h:h + 1], [F["kk"], ss], [F["kk"]])
                    self.STT(F["t2"][:], F["a"][:], -1.0, ka[:], ALU.add, ALU.mult, [F["a"], ka], [F["t2"]])
                    self.STT(F["kd"][:], F["t2"][:], 1.0, k_, ALU.add, ALU.mult, [F["t2"], z], [F["kd"]])
                    p3 = pg.next()
                    self.MM([(p3[:, 0:256], self.tri[:, d, :], F["logw"][:], True, True)], [self.tri, F["logw"]], [p3])
                    self.CP("act", F["cum"][:], p3[:, 0:256], [p3], [F["cum"]])
                    self.ACT(F["gam"][:], F["cum"][:], AF.Exp, [F["cum"]], [F["gam"]])
                    self.ACT(F["ginv"][:], F["cum"][:], AF.Exp, [F["cum"]], [F["ginv"]], scale=-1.0)
                    self.TT("dve", F["game"][:], F["cum"][:], F["logw"][:], ALU.subtract, [F["cum"], F["logw"]], [F["game"]])
                    self.ACT(F["game"][:], F["game"][:], AF.Exp, [F["game"]], [F["game"]])
                    x4 = X4.next(); vb = Vb.next()
                    self.STT(x4[:, 0, :], F["kk"][:], -1.0, F["game"][:], ALU.mult, ALU.mult, [F["kk"], F["game"]], [x4])
                    self.TT("pool", x4[:, 1, :], r_, F["gam"][:], ALU.mult, [z, F["gam"]], [x4])
                    self.TT("dve", F["t2"][:], F["kk"][:], F["a"][:], ALU.mult, [F["kk"], F["a"]], [F["t2"]])
                    self.TT("dve", x4[:, 2, :], F["t2"][:], F["ginv"][:], ALU.mult, [F["t2"], F["ginv"]], [x4])
                    self.TT("pool", x4[:, 3, :], F["kd"][:], F["ginv"][:], ALU.mult, [F["kd"], F["ginv"]], [x4])
                    self.CP("act", vb[:], v_, [z], [vb])
                    for pr in range(2):
                        p4 = pg.next()
                        self.MM([(p4[:, 0:1], F["logw"][:, pr * 128:(pr + 1) * 128], self.ones[:, 0:1], True, True)], [F["logw"], self.ones], [p4])
                        self.ACT(gcol[pr][:], p4[:, 0:1], AF.Exp, [p4], [gcol[pr]])
                    return dict(x4=x4, vb=vb, Fg=Fg, Fbon=Fbon, gcol=gcol)

                def inv(t, H):
                    x4 = H["x4"]
                    tt = TTt.next(); zp = ZP.next()
                    RbT = RbTr.next(); SK = SKr.next()
                    H["tt"] = tt; H["RbT"] = RbT; H["SK"] = SK
                    def tr(e, x4=x4):
                        for var in range(4):
                            for pr in range(2):
                                i = e.transpose(out=pT[:, var * 2 + pr, :], in_=x4[:, var, pr * 128:(pr + 1) * 128], identity=self.identb[:])
                        return i
                    P.op("pe", tr, r=[x4, self.identb], w=[pT])
                    self.CP("dve", tt[:], pT[:], [pT], [tt])
                    for var in range(2):
                        self.CP("dve", zp[0:64, 0:4:2, var, :], pT[0:64, var * 2:var * 2 + 2, :], [pT], [zp])
                        self.CP("dve", zp[64:128, 1:4:2, var, :], pT[64:128, var * 2:var * 2 + 2, :], [pT], [zp])
                    bv = lambda bk: bk[:].rearrange("p (a b) -> p a b", b=128)
                    bc4 = lambda ap_: ap_.unsqueeze(1).to_broadcast([128, 4, 128])
                    hc = lambda h: slice(h * 128, (h + 1) * 128)
                    pA = pi.next(); pN = pi.next()
                    self.MM([(pA[:, hc(h)], tt[:, 4 + h // 2, :], zp[:, h, 0, :], True, True) for h in range(4)], [tt, zp], [pA])
                    self.MM([(pN[:, hc(h)], zp[:, h, 0, :], tt[:, 4 + h // 2, :], True, True) for h in range(4)], [tt, zp], [pN])
                    self.TT("dve", AdT4[:], bv(pA), bc4(rwm[:, d, 0, :]), ALU.mult, [pA, rwm], [AdT4])
                    self.TT("dve", AdN4[:], bv(pN), bc4(rwm[:, d, 1, :]), ALU.mult, [pN, rwm], [AdN4])
                    for li in range(3):
                        self.TT("dve", LoT4[li][:], bv(pA), bc4(rwm[:, d, 2 + li, :]), ALU.mult, [pA, rwm], [LoT4[li]])
                        self.TT("dve", LoN4[li][:], bv(pN), bc4(rwm[:, d, 5 + li, :]), ALU.mult, [pN, rwm], [LoN4[li]])
                    pB = pi.next()
                    self.MM([(pB[:, hc(h)], tt[:, 4 + h // 2, :], zp[:, h, 1, :], True, True) for h in range(4)], [tt, zp], [pB])
                    self.TT("dve", RbT[:], bv(pB), bc4(cmask[:, d, 128:256]), ALU.mult, [pB, cmask], [RbT])
                    pC = pi.next()
                    self.MM([(pC[:, hc(h)], tt[:, 6 + h // 2, :], zp[:, h, 0, :], True, True) for h in range(4)], [tt, zp], [pC])
                    self.TT("dve", SK[:, 0], bv(pC), bc4(cmask[:, d, 0:128]), ALU.mult, [pC, cmask], [SK])
                    pD = pi.next()
                    self.MM([(pD[:, hc(h)], tt[:, 6 + h // 2, :], zp[:, h, 1, :], True, True) for h in range(4)], [tt, zp], [pD])
                    self.TT("dve", SK[:, 1], bv(pD), bc4(cmask[:, d, 128:256]), ALU.mult, [pD, cmask], [SK])
                    X = AdN4; XT = AdT4
                    Q = RQ.next(); QT = RQT.next()
                    self.TT("dve", Q[:], AdN4[:], bc4(self.identb[:]), ALU.add, [AdN4, self.identb], [Q])
                    self.TT("dve", QT[:], AdT4[:], bc4(self.identb[:]), ALU.add, [AdT4, self.identb], [QT])
                    for k in range(4):
                        if k < 3:
                            pXT = pi.next(); pX = pi.next()
                            self.MM([(pXT[:, hc(h)], X[:, h, :], XT[:, h, :], True, True) for h in range(4)], [X, XT], [pXT])
                            self.MM([(pX[:, hc(h)], XT[:, h, :], X[:, h, :], True, True) for h in range(4)], [X, XT], [pX])
                        if k >= 1:
                            pQT = pi.next(); pQ = pi.next()
                            self.MM([(pQT[:, hc(h)], X[:, h, :], QT[:, h, :], True, True) for h in range(4)], [X, QT], [pQT])
                            self.MM([(pQ[:, hc(h)], XT[:, h, :], Q[:, h, :], True, True) for h in range(4)], [XT, Q], [pQ])
                        if k < 3:
                            nXT = RXT.next(); nX = RX.next()
                            self.CP("act", nXT[:], bv(pXT), [pXT], [nXT])
                            self.CP("act", nX[:], bv(pX), [pX], [nX])
                        if k >= 1:
                            nQT = RQT.next(); nQ = RQ.next()
                            self.TT("dve", nQT[:], bv(pQT), QT[:], ALU.add, [pQT, QT], [nQT])
                            self.TT("dve", nQ[:], bv(pQ), Q[:], ALU.add, [pQ, Q], [nQ])
                            Q = nQ; QT = nQT
                        if k < 3:
                            X = nX; XT = nXT
                    for li in range(3):
                        last = li == 2
                        pE2 = pi.next()
                        self.MM([(pE2[:, hc(h)], LoN4[li][:, h, :], QT[:, h, :], True, True) for h in range(4)], [LoN4[li], QT], [pE2])
                        e2 = RE2.next()
                        self.CP("act", e2[:], bv(pE2), [pE2], [e2])
                        if not last:
                            pE = pi.next()
                            self.MM([(pE[:, hc(h)], LoT4[li][:, h, :], Q[:, h, :], True, True) for h in range(4)], [LoT4[li], Q], [pE])
                            e1 = RE.next()
                            self.CP("act", e1[:], bv(pE), [pE], [e1])
                        pD2 = pi.next()
                        self.MM([(pD2[:, hc(h)], Q[:, h, :], e2[:, h, :], True, True) for h in range(4)], [Q, e2], [pD2])
                        nQT = PTr.next() if last else RQT.next()
                        self.TT("dve", nQT[:], bv(pD2), QT[:], ALU.add, [pD2, QT], [nQT])
                        if not last:
                            pDD = pi.next()
                            self.MM([(pDD[:, hc(h)], QT[:, h, :], e1[:, h, :], True, True) for h in range(4)], [QT, e1], [pDD])
                            nQ = RQ.next()
                            self.TT("dve", nQ[:], bv(pDD), Q[:], ALU.add, [pDD, Q], [nQ])
                            Q = nQ
                        QT = nQT
                    H["QT"] = QT

                def seq(t, H):
                    x4 = H["x4"]; vb = H["vb"]; tt = H["tt"]; Fg = H["Fg"]; Fbon = H["Fbon"]; gcol = H["gcol"]
                    RbT = H["RbT"]; SK = H["SK"]; QT = H["QT"]
                    for pr in range(2):
                        pqx = pq.next()
                        hs = [pr * 2, pr * 2 + 1]
                        sp = [(pqx[:, 0:128], tt[:, 0 + pr, :], Hb[pr][:], True, False)]
                        for hh, h in enumerate(hs):
                            sp.append((pqx[:, hh * 64:(hh + 1) * 64], SK[:, 0, h, :], vb[:, h * 64:(h + 1) * 64], False, hh == 1))
                        self.MM(sp, [tt, Hb[pr], vb, SK], [pqx])
                        self.CP("act", Wb[pr][:], pqx[:, 0:128], [pqx], [Wb[pr]])
                        sp = []
                        for hh, h in enumerate(hs):
                            sp.append((pqx[:, 128 + hh * 64:128 + (hh + 1) * 64], QT[:, h, :], Wb[pr][:, hh * 64:(hh + 1) * 64], True, True))
                        self.MM(sp, [Wb[pr], QT], [pqx])
                        self.CP("dve", Ub[pr][:], pqx[:, 128:256], [pqx], [Ub[pr]])
                        sp = [(pqx[:, 256:384], tt[:, 2 + pr, :], Hb[pr][:], True, False)]
                        for hh, h in enumerate(hs):
                            cs = slice(256 + hh * 64, 256 + (hh + 1) * 64)
                            sp.append((pqx[:, cs], RbT[:, h, :], Ub[pr][:, hh * 64:(hh + 1) * 64], False, False))
                            sp.append((pqx[:, cs], SK[:, 1, h, :], vb[:, h * 64:(h + 1) * 64], False, hh == 1))
                        self.MM(sp, [tt, Hb[pr], Ub[pr], vb, RbT, SK], [pqx])
                        dst = hacc[t][:, pr * 128:(pr + 1) * 128]
                        if d == 0:
                            self.CP("act", dst, pqx[:, 256:384], [pqx], [hacc[t]])
                        else:
                            self.TT("dve", dst, pqx[:, 256:384], dst, ALU.add, [pqx, hacc[t]], [hacc[t]])
                        self.MM([(pqx[:, 384:512], x4[:, 2, pr * 128:(pr + 1) * 128], Ub[pr][:], True, False),
                                 (pqx[:, 384:512], x4[:, 3, pr * 128:(pr + 1) * 128], vb[:, pr * 128:(pr + 1) * 128], False, True)],
                                [x4, Ub[pr], vb], [pqx])
                        for hh in range(2):
                            blk = slice(hh * 64, (hh + 1) * 64); cb_ = slice(384 + hh * 64, 384 + (hh + 1) * 64)
                            self.TT("dve", htmp[blk, blk], pqx[blk, cb_], Hf[pr][blk, blk], ALU.add, [pqx, Hf[pr]], [htmp])
                            self.TSM("dve", Hf[pr][blk, blk], htmp[blk, blk], gcol[pr][blk, 0:1], [htmp, gcol[pr]], [Hf[pr]])
                        self.CP("act", Hb[pr][:], Hf[pr][:], [Hf[pr]], [Hb[pr]])
                    if d == 1:
                        self.finish_heads(ht, hacc[t], Fg[:], None, ng, nb, t, 768, extra=Fbon)
                order_ = self.order(d)
                n_ = len(order_)
                Hs = {}
                for step in range(n_ + 2):
                    if step < n_:
                        Hs[step] = prep(order_[step])
                    if 0 <= step - 1 < n_:
                        inv(order_[step - 1], Hs[step - 1])
                    if 0 <= step - 2 < n_:
                        seq(order_[step - 2], Hs.pop(step - 2))
            P.barrier()

    def mix_mlstm(self, s, l):
        P, S, I = self.P, self.S, self.I
        with ExitStack() as st:
            ng = self.bc(st, I["out_norm_g"][l, 0:256], 256); nb = self.bc(st, I["out_norm_b"][l, 0:256], 256)
            cw = self.sbt(st, "cw", [128, 3, 512]); cb = self.bc(st, I["mlstm_conv_b"][l], 512)
            for tap in range(3):
                P.dma(cw[:, tap, :], I["mlstm_conv_w"][l, tap].partition_broadcast(128), w=[cw])
            gb = self.bc(st, I["mlstm_gate_b"][l].rearrange("a b -> (a b)"), 16)
            tri4 = self.sbt(st, "tri4", [128, 4, 4, 128])
            P.dma(tri4[:], I["k_tri4"], w=[tri4])
            hacc = [self.sbt(st, "hacc", [128, 256]) for _ in range(NT)]
            ht = self.head_tiles(st)
            q3 = Rot([self.sbt(st, "q3", [128, 3, 512]) for _ in range(2)])
            rest = Rot([self.sbt(st, "rest", [128, 528]) for _ in range(2)])
            acc = self.sbt(st, "acc", [128, 512]); acc2 = self.sbt(st, "acc2", [128, 512])
            qkb = Rot([self.sbt(st, "qkb", [128, 512], BF16) for _ in range(2)])
            va = Rot([self.sbt(st, "va", [128, 4, 65], BF16) for _ in range(2)])
            TT = Rot([self.sbt(st, "TT", [128, 4, 128], BF16) for _ in range(2)])
            AM = Rot([self.sbt(st, "AM", [128, 4, 128], BF16) for _ in range(2)])
            TQ = Rot([self.sbt(st, "TQ", [128, 4, 128], BF16) for _ in range(2)])
            for tq_ in TQ.items:
                P.op("pool", lambda e, tq_=tq_: e.memset(tq_[:], 0.0), w=[tq_])
            G = self.sbt(st, "G", [128, 16]); lf = self.sbt(st, "lf", [128, 4])
            EE = Rot([self.sbt(st, "ee", [128, 4]) for _ in range(2)]); SC8 = Rot([self.sbt(st, "sc8", [128, 8]) for _ in range(2)]); ZS = Rot([self.sbt(st, "zs", [128, 4]) for _ in range(2)])
            Z = self.sbt(st, "Z", [128, 4]); rZ = self.sbt(st, "rZ", [128, 4]); wk = self.sbt(st, "wk", [128, 4]); wi = self.sbt(st, "wi", [128, 4])
            thr = self.sbt(st, "thr", [128, 4]); Em = self.sbt(st, "Em", [128, 4]); dm = self.sbt(st, "dm", [128, 4]); hsc = self.sbt(st, "hsc", [128, 256])
            CN = [self.sbt(st, "CN", [128, 130]) for _ in range(2)]
            CNw = [self.sbt(st, "CNw", [128, 130]) for _ in range(2)]
            CNb = [Rot([self.sbt(st, "CNb", [128, 130], BF16) for _ in range(2)]) for _ in range(2)]
            ptt = Rot([self.pst(st, "ptt", [128, 8, 128], BF16) for _ in range(1)])
            psc = Rot([self.bank(st, "psc") for _ in range(2)])
            pO = Rot([self.bank(st, "pO") for _ in range(2)])
            pSt = Rot([self.bank(st, "pSt") for _ in range(1)])
            pg = Rot([self.bank(st, "pg") for _ in range(2)])
            for d in range(2):
                for pr in range(2):
                    P.op("pool", lambda e, pr=pr: e.memset(CN[pr][:], 0.0), w=[CN[pr]])
                    P.op("pool", lambda e, pr=pr: e.memset(CNw[pr][:], 0.0), w=[CNw[pr]])
                P.op("pool", lambda e: e.memset(Em[:], 1.0), w=[Em])
                def prep(t):
                    q = q3.next(); rs = rest.next(); qk = qkb.next()
                    sc8 = SC8.next(); ee = EE.next(); zs = ZS.next()
                    rb = rowbase(t)
                    for tap in range(3):
                        P.dma(q[:, tap, :], S["proj"][rb - 1 + tap:rb - 1 + tap + 128, 0:512], w=[q])
                    P.dma(rs[:], S["proj"][rb:rb + 128, 512:1040], w=[rs])
                    P.op("dve", lambda e, q=q: e.tensor_tensor(out=acc[:], in0=q[:, 0, :], in1=cw[:, 0, :], op=ALU.mult), r=[q, cw], w=[acc])
                    P.op("pool", lambda e, q=q: e.tensor_tensor(out=acc2[:], in0=q[:, 1, :], in1=cw[:, 1, :], op=ALU.mult), r=[q, cw], w=[acc2])
                    P.op("dve", lambda e: e.tensor_tensor(out=acc[:], in0=acc[:], in1=acc2[:], op=ALU.add), r=[acc, acc2], w=[acc])
                    P.op("pool", lambda e, q=q: e.tensor_tensor(out=acc2[:], in0=q[:, 2, :], in1=cw[:, 2, :], op=ALU.mult), r=[q, cw], w=[acc2])
                    P.op("dve", lambda e: e.tensor_tensor(out=acc[:], in0=acc[:], in1=acc2[:], op=ALU.add), r=[acc, acc2], w=[acc])
                    P.op("dve", lambda e: e.tensor_tensor(out=acc[:], in0=acc[:], in1=cb[:], op=ALU.add), r=[acc, cb], w=[acc])
                    P.op("act", lambda e: e.activation(out=acc[:], in_=acc[:], func=AF.Silu), r=[acc], w=[acc])
                    P.op("act", lambda e, qk=qk: e.mul(out=qk[:, 0:256], in_=acc[:, 0:256], mul=0.125), r=[acc], w=[qk])
                    P.op("dve", lambda e, qk=qk: e.tensor_copy(out=qk[:, 256:512], in_=acc[:, 256:512]), r=[acc], w=[qk])
                    P.op("dve", lambda e, rs=rs: e.tensor_tensor(out=G[:], in0=rs[:, 512:528], in1=gb[:], op=ALU.add), r=[rs, gb], w=[G])
                    igs = G[:, d * 8:d * 8 + 4]; fps = G[:, d * 8 + 4:d * 8 + 8]
                    P.op("act", lambda e, fps=fps: e.activation(out=lf[:], in_=fps, func=AF.Exp, scale=-1.0), r=[G], w=[lf])
                    P.op("act", lambda e: e.activation(out=lf[:], in_=lf[:], func=AF.Ln, bias=self.ones[:, 0:1], scale=1.0), r=[lf, self.ones], w=[lf])
                    P.op("dve", lambda e: e.tensor_scalar_mul(out=lf[:], in0=lf[:], scalar1=-1.0), r=[lf], w=[lf])
                    g1 = pg.next()

                    def gm(e, g1=g1, d=d):
                        e.matmul(g1[:, 0:4], lhsT=self.tri[:, d, :], rhs=lf[:], start=True, stop=True)
                        return e.matmul(g1[:, 4:8], lhsT=self.ones[:], rhs=lf[:], start=True, stop=True)
                    P.op("pe", gm, r=[lf, self.tri, self.ones], w=[g1])
                    P.op("dve", lambda e, g1=g1: e.tensor_copy(out=sc8[:], in_=g1[:, 0:8]), r=[g1], w=[sc8])
                    P.op("dve", lambda e, igs=igs: e.tensor_tensor(out=ee[:], in0=igs, in1=sc8[:, 0:4], op=ALU.subtract), r=[G, sc8], w=[ee])
                    P.op("act", lambda e: e.activation(out=ee[:], in_=ee[:], func=AF.Exp), r=[ee], w=[ee])
                    g2 = pg.next()
                    P.op("pe", lambda e, g2=g2: e.matmul(g2[:, 0:4], lhsT=self.ones[:], rhs=ee[:], start=True, stop=True), r=[ee, self.ones], w=[g2])
                    P.op("act", lambda e, g2=g2: e.copy(out=zs[:], in_=g2[:, 0:4]), r=[g2], w=[zs])
                    return dict(rs=rs, qk=qk, sc8=sc8, ee=ee, zs=zs)

                def chain(t, H):
                    rs = H["rs"]; qk = H["qk"]; sc8 = H["sc8"]; ee = H["ee"]; zs = H["zs"]
                    v = va.next(); tt = TT.next(); am = AM.next()
                    P.op("dve", lambda e: e.tensor_tensor(out=Z[:], in0=zs[:], in1=Em[:], op=ALU.add), r=[zs, Em], w=[Z])
                    P.op("dve", lambda e: e.reciprocal(out=rZ[:], in_=Z[:]), r=[Z], w=[rZ])
                    P.op("dve", lambda e: e.tensor_tensor(out=wk[:], in0=ee[:], in1=rZ[:], op=ALU.mult), r=[ee, rZ], w=[wk])
                    P.op("dve", lambda e: e.tensor_tensor(out=wi[:], in0=Em[:], in1=rZ[:], op=ALU.mult), r=[Em, rZ], w=[wi])
                    P.op("act", lambda e: e.activation(out=thr[:], in_=sc8[:, 0:4], func=AF.Exp, scale=-1.0), r=[sc8], w=[thr])
                    P.op("dve", lambda e: e.tensor_tensor(out=thr[:], in0=thr[:], in1=rZ[:], op=ALU.mult), r=[thr, rZ], w=[thr])
                    P.op("act", lambda e: e.activation(out=Em[:], in_=sc8[:, 4:8], func=AF.Exp), r=[sc8, wi], w=[Em])
                    P.op("dve", lambda e: e.tensor_tensor(out=Em[:], in0=Em[:], in1=Z[:], op=ALU.mult), r=[Em, Z], w=[Em])
                    for h in range(4):
                        P.op("pool", lambda e, rs=rs, v=v, h=h: e.tensor_scalar_mul(out=v[:, h, 0:64], in0=rs[:, h * 64:(h + 1) * 64], scalar1=wk[:, h:h + 1]), r=[rs, wk], w=[v])
                    P.op("pool", lambda e, v=v: e.tensor_copy(out=v[:, :, 64], in_=wk[:]), r=[wk], w=[v])
                    ptb = ptt.next(); pt = ptb[:, 0:4, :]

                    def tr(e, qk=qk, pt=pt):
                        for k in range(4):
                            i = e.transpose(out=pt[:, k, :], in_=qk[:, k * 128:(k + 1) * 128], identity=self.identb[:])
                        return i
                    P.op("pe", tr, r=[qk, self.identb], w=[ptb])
                    P.op("dve", lambda e, pt=pt, tt=tt: e.tensor_copy(out=tt[:], in_=pt), r=[ptb], w=[tt])
                    tq = TQ.next()
                    P.op("dve", lambda e, pt=pt, tq=tq: e.tensor_copy(out=tq[0:64, 0:4:2, :], in_=pt[0:64, 0:2, :]), r=[ptb], w=[tq])
                    P.op("dve", lambda e, pt=pt, tq=tq: e.tensor_copy(out=tq[64:128, 1:4:2, :], in_=pt[64:128, 0:2, :]), r=[ptb], w=[tq])
                    psb = psc.next(); ps = psb[:].rearrange("p (a b) -> p a b", b=128)

                    def sc(e, tt=tt, ps=ps, tq=tq):
                        for h in range(4):
                            i = e.matmul(ps[:, h, :], lhsT=tt[:, 2 + h // 2, :], rhs=tq[:, h, :], start=True, stop=True)
                        return i
                    P.op("pe", sc, r=[tt, tq], w=[psb])
                    P.op("dve", lambda e, ps=ps, am=am, d=d: e.tensor_tensor(out=am[:], in0=ps, in1=tri4[:, d], op=ALU.mult), r=[psb, tri4], w=[am])
                    po = pO.next()
                    for pr in range(2):
                        pS = pSt.next(); cb_ = CNb[pr].next()
                        c0 = pr * 130
                        for hh in range(2):
                            h = pr * 2 + hh
                            blk = slice(hh * 64, (hh + 1) * 64); cblk = slice(hh * 65, (hh + 1) * 65)
                            P.op("dve", lambda e, pr=pr, blk=blk, cblk=cblk, h=h: e.tensor_scalar_mul(out=CNw[pr][blk, cblk], in0=CN[pr][blk, cblk], scalar1=wi[blk, h:h + 1]),
                                 r=[CN[pr], wi], w=[CNw[pr]])
                        P.op("act", lambda e, pr=pr, cb_=cb_: e.copy(out=cb_[:], in_=CNw[pr][:]), r=[CNw[pr]], w=[cb_])

                        def mo(e, tt=tt, am=am, v=v, po=po, pr=pr, cb_=cb_, c0=c0):
                            e.matmul(po[:, c0:c0 + 130], lhsT=tt[:, pr, :], rhs=cb_[:], start=True, stop=False)
                            for hh in range(2):
                                h = pr * 2 + hh
                                i = e.matmul(po[:, c0 + hh * 65:c0 + (hh + 1) * 65], lhsT=am[:, h, :], rhs=v[:, h, :], start=False, stop=(hh == 1))
                            return i
                        P.op("pe", mo, r=[tt, am, v, cb_], w=[(po, pr)])
                        P.op("pe", lambda e, qk=qk, v=v, pS=pS, pr=pr: e.matmul(pS[:, 0:130], lhsT=qk[:, 256 + pr * 128:256 + (pr + 1) * 128],
                                                                                 rhs=v[:, pr * 2:pr * 2 + 2, :].rearrange("p a b -> p (a b)"), start=True, stop=True), r=[qk, v], w=[pS])
                        for hh in range(2):
                            blk = slice(hh * 64, (hh + 1) * 64); cblk = slice(hh * 65, (hh + 1) * 65)
                            P.op("dve", lambda e, pS=pS, pr=pr, blk=blk, cblk=cblk: e.tensor_tensor(out=CN[pr][blk, cblk], in0=CNw[pr][blk, cblk], in1=pS[blk, cblk], op=ALU.add),
                                 r=[pS, CNw[pr]], w=[CN[pr]])
                    pov = po[:, 0:260].rearrange("p (a b) -> p a b", b=65)
                    pok = [(po, 0), (po, 1)]
                    P.op("act", lambda e, pov=pov: e.activation(out=dm[:], in_=pov[:, :, 64], func=AF.Abs), r=pok, w=[dm])
                    P.op("dve", lambda e: e.tensor_tensor(out=dm[:], in0=dm[:], in1=thr[:], op=ALU.max), r=[dm, thr], w=[dm])
                    P.op("dve", lambda e: e.reciprocal(out=dm[:], in_=dm[:]), r=[dm], w=[dm])
                    dmb = dm[:].unsqueeze(2).to_broadcast([128, 4, 64])
                    hv = hacc[t][:].rearrange("p (h f) -> p h f", f=64)
                    if d == 0:
                        P.op("dve", lambda e, pov=pov, hv=hv, dmb=dmb: e.tensor_tensor(out=hv, in0=pov[:, :, 0:64], in1=dmb, op=ALU.mult), r=pok + [dm], w=[hacc[t]])
                    else:
                        P.op("dve", lambda e, pov=pov, dmb=dmb: e.tensor_tensor(out=hsc[:].rearrange("p (h f) -> p h f", f=64), in0=pov[:, :, 0:64], in1=dmb, op=ALU.mult), r=pok + [dm], w=[hsc])
                        P.op("pool", lambda e, t=t: e.tensor_tensor(out=hacc[t][:], in0=hacc[t][:], in1=hsc[:], op=ALU.add), r=[hacc[t], hsc], w=[hacc[t]])
                    if d == 1:
                        gate = rs[:, 256:512]
                        self.finish_heads(ht, hacc[t], gate, AF.Sigmoid, ng, nb, t, 0)
                order_ = self.order(d)
                hn_ = prep(order_[0])
                for i_, t in enumerate(order_):
                    hc_ = hn_
                    if i_ + 1 < len(order_):
                        hn_ = prep(order_[i_ + 1])
                    chain(t, hc_)
            P.barrier()

    def mix_retention(self, s, l):
        P, S, I = self.P, self.S, self.I
        C0 = 1808
        with ExitStack() as st:
            ng = self.bc(st, I["out_norm_g"][l, 512:768], 256); nb = self.bc(st, I["out_norm_b"][l, 512:768], 256)
            mask = self.sbt(st, "rtmask", [128, 2, 4, 128]); dec = self.sbt(st, "rtdec", [128, 24])
            P.dma(mask[:], I["k_rtmask"], w=[mask]); P.dma(dec[:], I["k_rtdec"], w=[dec])
            hacc = [self.sbt(st, "hacc", [128, 256]) for _ in range(NT)]
            ht = self.head_tiles(st)
            raw = Rot([self.sbt(st, "raw", [128, 1024]) for _ in range(2)])
            rope = Rot([self.sbt(st, "rope", [128, 2, 256]) for _ in range(2)])
            rt1 = self.sbt(st, "rt1", [128, 256]); rt2 = self.sbt(st, "rt2", [128, 256])
            qkb = Rot([self.sbt(st, "qkb", [128, 512], BF16) for _ in range(2)])
            vb = Rot([self.sbt(st, "vb", [128, 256], BF16) for _ in range(2)])
            vd = Rot([self.sbt(st, "vd", [128, 256], BF16) for _ in range(2)])
            TT = Rot([self.sbt(st, "TT", [128, 4, 128], BF16) for _ in range(2)])
            AM = Rot([self.sbt(st, "AM", [128, 4, 128], BF16) for _ in range(2)])
            TQ = Rot([self.sbt(st, "TQ", [128, 4, 128], BF16) for _ in range(2)])
            for tq_ in TQ.items:
                P.op("pool", lambda e, tq_=tq_: e.memset(tq_[:], 0.0), w=[tq_])
            o1 = Rot([self.sbt(st, "o1", [128, 128]) for _ in range(2)])
            Sf = [self.sbt(st, "Sf", [128, 128]) for _ in range(2)]
            Sb = [Rot([self.sbt(st, "Sb", [128, 128], BF16) for _ in range(2)]) for _ in range(2)]
            ptt = Rot([self.pst(st, "ptt", [128, 8, 128], BF16) for _ in range(1)])
            psc = Rot([self.bank(st, "psc") for _ in range(2)])
            pO1 = Rot([self.bank(st, "pO1") for _ in range(2)])
            pO2 = Rot([self.bank(st, "pO2") for _ in range(2)])
            pSt = Rot([self.bank(st, "pSt") for _ in range(1)])
            for d in range(2):
                sbc = []
                for pr in range(2):
                    P.op("pool", lambda e, pr=pr: e.memset(Sf[pr][:], 0.0), w=[Sf[pr]])
                    b0 = Sb[pr].next()
                    P.op("pool", lambda e, b0=b0: e.memset(b0[:], 0.0), w=[b0])
                    sbc.append(b0)
                def prep(t):
                    rw = raw.next(); qk = qkb.next(); v = vb.next(); vdd = vd.next()
                    P.dma(rw[:], S["proj"][rowbase(t):rowbase(t) + 128, C0:C0 + 1024], w=[rw])
                    if t >= 2:
                        rp = rope.next()
                        P.dma(rp[:], I["k_rope"][(t - 2) * 128:(t - 1) * 128], w=[rp])
                        for qi in range(2):
                            src = rw[:, qi * 256:(qi + 1) * 256]
                            sv = src.rearrange("p (a two f) -> p a two f", two=2, f=16)
                            snv = rp[:, 1, :].rearrange("p (a two f) -> p a two f", two=2, f=16)
                            t1v = rt1[:].rearrange("p (a two f) -> p a two f", two=2, f=16)
                            P.op("pool", lambda e, sv=sv, snv=snv, t1v=t1v: e.tensor_tensor(out=t1v[:, :, 0, :], in0=sv[:, :, 1, :], in1=snv[:, :, 0, :], op=ALU.mult), r=[rw, rp], w=[rt1])
                            P.op("pool", lambda e, sv=sv, snv=snv, t1v=t1v: e.tensor_tensor(out=t1v[:, :, 1, :], in0=sv[:, :, 0, :], in1=snv[:, :, 1, :], op=ALU.mult), r=[rw, rp], w=[rt1])
                            P.op("dve", lambda e, src=src, rp=rp: e.tensor_tensor(out=rt2[:], in0=src, in1=rp[:, 0, :], op=ALU.mult), r=[rw, rp], w=[rt2])
                            P.op("dve", lambda e: e.tensor_tensor(out=rt2[:], in0=rt2[:], in1=rt1[:], op=ALU.add), r=[rt1, rt2], w=[rt2])
                            P.op("act", lambda e, qi=qi, qk=qk: e.mul(out=qk[:, qi * 256:(qi + 1) * 256], in_=rt2[:], mul=(0.125 if qi == 0 else 1.0)), r=[rt2], w=[qk])
                    else:
                        P.op("act", lambda e, rw=rw, qk=qk: e.mul(out=qk[:, 0:256], in_=rw[:, 0:256], mul=0.125), r=[rw], w=[qk])
                        P.op("act", lambda e, rw=rw, qk=qk: e.copy(out=qk[:, 256:512], in_=rw[:, 256:512]), r=[rw], w=[qk])
                    P.op("dve", lambda e, rw=rw, v=v: e.tensor_copy(out=v[:], in_=rw[:, 512:768]), r=[rw], w=[v])
                    for h in range(4):
                        P.op("pool", lambda e, rw=rw, vdd=vdd, h=h, d=d: e.tensor_scalar_mul(out=vdd[:, h * 64:(h + 1) * 64], in0=rw[:, 512 + h * 64:512 + (h + 1) * 64],
                                                                                       scalar1=dec[:, 8 + d * 4 + h:8 + d * 4 + h + 1]), r=[rw, dec], w=[vdd])
                    return dict(rw=rw, qk=qk, v=v, vdd=vdd)

                def chain(t, H):
                    rw = H["rw"]; qk = H["qk"]; v = H["v"]; vdd = H["vdd"]
                    tt = TT.next(); am = AM.next()
                    ptb = ptt.next(); pt = ptb[:, 0:4, :]

                    def tr(e, qk=qk, pt=pt):
                        for k in range(4):
                            i = e.transpose(out=pt[:, k, :], in_=qk[:, k * 128:(k + 1) * 128], identity=self.identb[:])
                        return i
                    P.op("pe", tr, r=[qk, self.identb], w=[ptb])
                    P.op("dve", lambda e, pt=pt, tt=tt: e.tensor_copy(out=tt[:], in_=pt), r=[ptb], w=[tt])
                    tq = TQ.next()
                    P.op("dve", lambda e, pt=pt, tq=tq: e.tensor_copy(out=tq[0:64, 0:4:2, :], in_=pt[0:64, 0:2, :]), r=[ptb], w=[tq])
                    P.op("dve", lambda e, pt=pt, tq=tq: e.tensor_copy(out=tq[64:128, 1:4:2, :], in_=pt[64:128, 0:2, :]), r=[ptb], w=[tq])
                    psb = psc.next(); ps = psb[:].rearrange("p (a b) -> p a b", b=128)

                    def sc(e, tt=tt, ps=ps, tq=tq):
                        for h in range(4):
                            i = e.matmul(ps[:, h, :], lhsT=tt[:, 2 + h // 2, :], rhs=tq[:, h, :], start=True, stop=True)
                        return i
                    P.op("pe", sc, r=[tt, tq], w=[psb])
                    P.op("dve", lambda e, ps=ps, am=am, d=d: e.tensor_tensor(out=am[:], in0=ps, in1=mask[:, d], op=ALU.mult), r=[psb, mask], w=[am])
                    for pr in range(2):
                        p1 = pO1.next(); p2 = pO2.next(); pS = pSt.next(); oo = o1.next()
                        sb_old = sbc[pr]

                        def m1(e, am=am, v=v, p1=p1, pr=pr):
                            for hh in range(2):
                                h = pr * 2 + hh
                                i = e.matmul(p1[:, hh * 64:(hh + 1) * 64], lhsT=am[:, h, :], rhs=v[:, h * 64:(h + 1) * 64], start=True, stop=True)
                            return i
                        P.op("pe", m1, r=[am, v], w=[p1])
                        P.op("pe", lambda e, tt=tt, p2=p2, pr=pr, sb_old=sb_old: e.matmul(p2[:, 0:128], lhsT=tt[:, pr, :], rhs=sb_old[:], start=True, stop=True), r=[tt, sb_old], w=[p2])
                        P.op("pe", lambda e, qk=qk, vdd=vdd, pS=pS, pr=pr: e.matmul(pS[:, 0:128], lhsT=qk[:, 256 + pr * 128:256 + (pr + 1) * 128], rhs=vdd[:, pr * 128:(pr + 1) * 128], start=True, stop=True),
                             r=[qk, vdd], w=[pS])
                        P.op("act", lambda e, p1=p1, oo=oo: e.copy(out=oo[:], in_=p1[:, 0:128]), r=[p1], w=[oo])
                        for hh in range(2):
                            h = pr * 2 + hh
                            dst = hacc[t][:, h * 64:(h + 1) * 64]
                            P.op("dve", lambda e, p2=p2, oo=oo, hh=hh, h=h, d=d, dst=dst: e.scalar_tensor_tensor(out=(oo[:, hh * 64:(hh + 1) * 64] if d == 1 else dst), in0=p2[:, hh * 64:(hh + 1) * 64],
                                                                                                             scalar=dec[:, d * 4 + h:d * 4 + h + 1], in1=oo[:, hh * 64:(hh + 1) * 64], op0=ALU.mult, op1=ALU.add),
                                 r=[p2, oo, dec], w=[oo if d == 1 else hacc[t]])
                            if d == 1:
                                P.op("pool", lambda e, oo=oo, hh=hh, dst=dst: e.tensor_tensor(out=dst, in0=dst, in1=oo[:, hh * 64:(hh + 1) * 64], op=ALU.add), r=[oo, hacc[t]], w=[hacc[t]])
                            blk = slice(hh * 64, (hh + 1) * 64)
                            P.op("dve", lambda e, pS=pS, pr=pr, blk=blk, h=h, d=d: e.scalar_tensor_tensor(out=Sf[pr][blk, blk], in0=Sf[pr][blk, blk], scalar=dec[blk, 16 + d * 4 + h:16 + d * 4 + h + 1],
                                                                                                     in1=pS[blk, blk], op0=ALU.mult, op1=ALU.add), r=[pS, Sf[pr], dec], w=[Sf[pr]])
                        nb_ = Sb[pr].next()
                        P.op("act", lambda e, nb_=nb_, pr=pr: e.copy(out=nb_[:], in_=Sf[pr][:]), r=[Sf[pr]], w=[nb_])
                        sbc[pr] = nb_
                    if d == 1:
                        gate = rw[:, 768:1024]
                        self.finish_heads(ht, hacc[t], gate, AF.Silu, ng, nb, t, 512)
                for t in self.order(d):
                    chain(t, prep(t))
            P.barrier()

    def ln_mod(self, s, l, uT, kind):
        P = self.P
        with ExitStack() as st:
            stats = self.sbt(st, "stats", [128, 2, 6])
            mv = self.sbt(st, "mv", [128, 2])
            rstd = self.sbt(st, "rstd", [128, 1])
            utmp = self.sbt(st, "utmp", [128, 8, 128])
            xn = Rot([self.sbt(st, "xn", [128, D], BF16) for _ in range(2)])
            pt = Rot([self.pst(st, "ptr", [128, 8, 128], BF16) for _ in range(2)])
            for t in range(NT):
                r = 4 if t < 2 else s
                x = self.xres[t]
                a = xn.next(); p = pt.next()
                self.ln_stats(x, stats, mv, rstd)
                P.op("dve", lambda e, a=a, x=x: e.tensor_scalar(out=a[:], in0=x[:], scalar1=mv[:, 0:1], scalar2=rstd[:, 0:1],
                                                                op0=ALU.subtract, op1=ALU.mult), r=[x, mv, rstd], w=[a])

                def tr(e, a=a, p=p):
                    for k in range(8):
                        i = e.transpose(out=p[:, k, :], in_=a[:, k * 128:(k + 1) * 128], identity=self.identb[:])
                    return i
                P.op("pe", tr, r=[a, self.identb], w=[p])
                sc = self.modp[:, l, r, 2 * kind + 1, :].unsqueeze(2).to_broadcast([128, 8, 128])
                sh = self.modp[:, l, r, 2 * kind, :].unsqueeze(2).to_broadcast([128, 8, 128])
                dst = uT[:, :, t * 128:(t + 1) * 128]
                P.op("dve", lambda e, p=p, sc=sc: e.tensor_tensor(out=utmp[:], in0=p[:], in1=sc, op=ALU.mult), r=[p, self.modp], w=[utmp])
                P.op("dve", lambda e, sh=sh, dst=dst: e.tensor_tensor(out=dst, in0=utmp[:], in1=sh, op=ALU.add), r=[utmp, self.modp], w=[(uT, t)])
            P.barrier()

    def ln_stats(self, x, stats, mv, rstd, width=D):
        P = self.P
        ng = width // 512
        for g in range(ng):
            P.op("dve", lambda e, g=g: e.bn_stats(out=stats[:, g, :], in_=x[:, g * 512:(g + 1) * 512]), r=[x], w=[stats])
        P.op("dve", lambda e: e.bn_aggr(out=mv[:], in_=stats[:, 0:ng, :].rearrange("p g s -> p (g s)")), r=[stats], w=[mv])
        P.op("act", lambda e: e.activation(out=rstd[:], in_=mv[:, 1:2], func=AF.Sqrt, bias=self.epsc[:, 0:1], scale=1.0), r=[mv, self.epsc], w=[rstd])
        P.op("dve", lambda e: e.reciprocal(out=rstd[:], in_=rstd[:]), r=[rstd], w=[rstd])

    def in_proj(self, l, uT):
        P, S = self.P, self.S
        with ExitStack() as st:
            wch = Rot([self.sbt(st, "wch", [128, 8, 512], BF16) for _ in range(2)])
            stg = Rot([self.sbt(st, "stg", [128, 512]) for _ in range(4)])
            ps = Rot([self.pst(st, "pps", [128, 512]) for _ in range(4)])
            ev = Rot(["act", "act", "act", "dve"])
            for g in range(8):
                c0 = g * 512
                nco = min(512, DPROJ - c0)
                w = wch.next()
                P.dma(w[:, :, 0:nco], S["wb_in%d" % l][:, c0:c0 + nco].rearrange("(k p) n -> p k n", p=128), w=[w])
                for t in range(NT):
                    p = ps.next(); sg = stg.next()

                    def mm(e, p=p, w=w, t=t, nco=nco):
                        for k in range(8):
                            i = e.matmul(p[:, 0:nco], lhsT=uT[:, k, t * 128:(t + 1) * 128], rhs=w[:, k, 0:nco], start=(k == 0), stop=(k == 7))
                        return i
                    P.op("pe", mm, r=[w, (uT, t)], w=[p])
                    eve = ev.next()
                    if eve == "act":
                        P.op("act", lambda e, p=p, sg=sg, nco=nco: e.copy(out=sg[:, 0:nco], in_=p[:, 0:nco]), r=[p], w=[sg])
                    else:
                        P.op("dve", lambda e, p=p, sg=sg, nco=nco: e.tensor_copy(out=sg[:, 0:nco], in_=p[:, 0:nco]), r=[p], w=[sg])
                    P.dma(S["proj"][rowbase(t):rowbase(t) + 128, c0:c0 + nco], sg[:, 0:nco], r=[sg], eng="act")
            P.barrier()


_CACHE = {}


def kernel(**inputs):
    consts = make_consts()
    n_cores = 8
    if "nc" not in _CACHE:
        _CACHE["nc"] = Builder(n_seq=4).build(consts)
    nc = _CACHE["nc"]
    in_maps = []
    for cidx in range(n_cores):
        m = {}
        for k, v in inputs.items():
            v = np.asarray(v)
            if k in ("x", "c", "ctx"):
                m[k] = np.ascontiguousarray(v[cidx * 4:(cidx + 1) * 4])
            else:
                m[k] = np.ascontiguousarray(v)
        for k, v in consts.items():
            m["k_" + k] = v
        in_maps.append(m)
    res = run_bass_kernel_spmd(nc, in_maps, core_ids=list(range(n_cores)))
    return np.concatenate([r["out"] for r in res.results], axis=0).astype(np.float32)
```

```python
import math
import os
from contextlib import ExitStack
import numpy as np
import ml_dtypes
import concourse.bass as bass
import concourse.mybir as mybir
from concourse.bass_utils import run_bass_kernel_spmd

F32 = mybir.dt.float32
BF16 = mybir.dt.bfloat16
ALU = mybir.AluOpType
AF = mybir.ActivationFunctionType
AX = mybir.AxisListType

ENGS = ("pe", "act", "dve", "pool", "sp")

D = 1024
SEQ = 2048
CTX = 256
NT = 18
TOK = NT * 128
DPROJ = 3728
DFF = 2816
GW = 256
ALPHA = 4 ** 0.25
NROWS = 2307


def rowbase(t):
    return 1 + 128 * t if t < 2 else 2 + 128 * t


class Prog:
    N_DMA_SEMS = 24

    def __init__(self, nc):
        self.nc = nc
        self.q = {e: [] for e in ENGS}
        self.cnt = {e: 0 for e in ENGS}
        self.waited = {}
        self.lastw = {}
        self.reads = {}
        self.dma_uses = [0] * self.N_DMA_SEMS
        self.dma_rr = 0
        self.n_inst = 0
        self.epoch = {e: 0 for e in ENGS}

    @staticmethod
    def key(x):
        if isinstance(x, tuple):
            return (Prog.key(x[0]),) + tuple(x[1:])
        if isinstance(x, str):
            return x
        t = getattr(x, "tensor", x)
        return t.name

    def _need(self, eng, deps, waits):
        for (src, val) in deps:
            if self.waited.get((eng, src), 0) < val:
                self.waited[(eng, src)] = val
                waits.append((src, val))

    def _deps(self, eng, r, w):
        waits = []
        deps = []
        for k in r:
            k = self.key(k)
            if k in self.lastw:
                deps.append(self.lastw[k])
        for k in w:
            k = self.key(k)
            if k in self.lastw:
                deps.append(self.lastw[k])
            for s, v in self.reads.get(k, {}).items():
                deps.append((s, v))
        self._need(eng, deps, waits)
        return waits

    def _commit(self, src, val, r, w):
        for k in r:
            k = self.key(k)
            d = self.reads.setdefault(k, {})
            if d.get(src, 0) < val:
                d[src] = val
        for k in w:
            k = self.key(k)
            self.lastw[k] = (src, val)
            self.reads[k] = {}

    def op(self, eng, fn, r=(), w=()):
        waits = self._deps(eng, r, w)
        self.cnt[eng] += 1
        val = self.cnt[eng]
        src = ("eng", eng, self.epoch[eng])
        self.q[eng].append(("op", fn, waits, src))
        self._commit(src, val, r, w)
        self.n_inst += 1

    def dma(self, out, in_, r=(), w=(), eng=None, **kw):
        if eng is None:
            eng = "sp"
        waits = self._deps(eng, r, w)
        s = self.dma_rr
        self.dma_rr = (self.dma_rr + 1) % self.N_DMA_SEMS
        src = ("dma", s)
        prev = self.dma_uses[s] * 16
        if prev and self.waited.get((eng, src), 0) < prev:
            self.waited[(eng, src)] = prev
            waits.append((src, prev))
        self.dma_uses[s] += 1
        val = self.dma_uses[s] * 16
        self.q[eng].append(("dma", (out, in_, kw), waits, s))
        self._commit(src, val, r, w)
        self.n_inst += 1

    def barrier(self):
        for e in ENGS:
            waits = []
            deps = [(("eng", o, self.epoch[o]), self.cnt[o]) for o in ENGS if o != e and self.cnt[o] > 0]
            deps += [(("dma", s), self.dma_uses[s] * 16)
                     for s in range(self.N_DMA_SEMS) if self.dma_uses[s]]
            self._need(e, deps, waits)
            if waits:
                self.q[e].append(("wait", None, waits, None))
        self.lastw.clear()
        self.reads.clear()
        for e in ENGS:
            if self.cnt[e] > 1000000000:
                self.epoch[e] += 1
                self.cnt[e] = 0

    def emit(self, stack):
        nc = self.nc
        self.barrier()
        esem = {}
        for e in ENGS:
            for ep in range(self.epoch[e] + 1):
                esem[(e, ep)] = stack.enter_context(nc.semaphore("s_%s_%d" % (e, ep)))
        dsem = [stack.enter_context(nc.semaphore("d%d" % i)) for i in range(self.N_DMA_SEMS)]

        def semof(src):
            return dsem[src[1]] if src[0] == "dma" else esem[(src[1], src[2])]

        block = stack.enter_context(nc.Block())

        def run(e, engobj):
            for kind, fn, waits, s in self.q[e]:
                for (src, val) in waits:
                    engobj.wait_ge(semof(src), val)
                if kind == "op":
                    fn(engobj).then_inc(semof(s), 1)
                elif kind == "dma":
                    out, in_, kw = fn
                    engobj.dma_start(out=out, in_=in_, **kw).then_inc(dsem[s], 16)

        @block.tensor
        def _(eng):
            run("pe", eng)

        @block.scalar
        def _(eng):
            run("act", eng)

        @block.vector
        def _(eng):
            run("dve", eng)

        @block.gpsimd
        def _(eng):
            run("pool", eng)

        @block.sync
        def _(eng):
            run("sp", eng)


class Rot:
    def __init__(self, items):
        self.items = list(items)
        self.i = 0

    def next(self):
        x = self.items[self.i % len(self.items)]
        self.i += 1
        return x


def make_consts():
    c = {}
    c["ident"] = np.eye(128, dtype=np.float32)
    j = np.arange(128)[:, None]
    i = np.arange(128)[None, :]
    c["tri"] = np.stack([(j <= i), (j >= i), (j < i), (j > i)]).astype(np.float32)
    c["ones"] = np.ones((128, 128), np.float32)
    c["tri4"] = np.ascontiguousarray(np.broadcast_to(c["tri"].transpose(1, 0, 2)[:, :, None, :], (128, 4, 4, 128))).astype(np.float32)
    lg_f = np.log(1.0 - 2.0 ** (-5.0 - np.arange(4, dtype=np.float64)))
    lg = np.stack([lg_f, lg_f[::-1]])
    jj = np.arange(128, dtype=np.float64)
    rtmask = np.zeros((128, 2, 4, 128), np.float64)
    qdec = np.zeros((128, 8)); kdec = np.zeros((128, 8)); cdec = np.zeros((128, 8))
    for d in range(2):
        for h in range(4):
            g = lg[d, h]
            diff = (jj[None, :] - jj[:, None]) if d == 0 else (jj[:, None] - jj[None, :])
            rtmask[:, d, h, :] = np.where(diff >= 0, np.exp(np.maximum(diff, 0) * g), 0.0)
            pos = jj if d == 0 else 127 - jj
            qdec[:, d * 4 + h] = np.exp((pos + 1.0) * g)
            kdec[:, d * 4 + h] = np.exp((127.0 - pos) * g)
            cdec[:, d * 4 + h] = np.exp(128.0 * g)
    c["rtmask"] = rtmask.astype(np.float32)
    c["rtdec"] = np.concatenate([qdec, kdec, cdec], 1).astype(np.float32)
    freqs = 10000.0 ** (-np.arange(16, dtype=np.float32) / 16)
    row = np.repeat(np.arange(32, dtype=np.float32), 64); col = np.tile(np.arange(64, dtype=np.float32), 32)
    ar = (row[:, None] * freqs).astype(np.float32); ac = (col[:, None] * freqs).astype(np.float32)
    cos64 = np.concatenate([np.cos(ar), np.cos(ar), np.cos(ac), np.cos(ac)], 1)
    sin64 = np.concatenate([-np.sin(ar), np.sin(ar), -np.sin(ac), np.sin(ac)], 1)
    c["rope"] = np.stack([np.tile(cos64, (1, 4)), np.tile(sin64, (1, 4))], 1).astype(np.float32)
    def lo(n, rows_odd):
        same = (j // (2 * n)) == (i // (2 * n))
        return same & (((j // n) % 2) == (1 if rows_odd else 0)) & (((i // n) % 2) == (0 if rows_odd else 1))
    bd = (j // 16) == (i // 16)
    rwm = np.zeros((128, 2, 8, 128), np.float32)
    for d in range(2):
        strictT = (j < i) if d == 0 else (j > i)
        strictN = (i < j) if d == 0 else (i > j)
        rwm[:, d, 0, :] = bd & strictT
        rwm[:, d, 1, :] = bd & strictN
        for li, n in enumerate((16, 32, 64)):
            rwm[:, d, 2 + li, :] = lo(n, d == 1)
            rwm[:, d, 5 + li, :] = lo(n, d == 0)
    c["rwm"] = rwm.astype(ml_dtypes.bfloat16)
    for kind, L in (("lat", SEQ), ("ctx", CTX)):
        N = 2 * L
        nft = L // 128
        t = np.arange(L, dtype=np.int64)[:, None]
        f = np.arange(L, dtype=np.int64)[None, :]
        ang = 2.0 * np.pi * ((t * f) % N).astype(np.float64) / N
        sgn = (-1.0) ** np.arange(L)
        Fre = np.cos(ang); Fim = -np.sin(ang); Fim[:, 0] = sgn
        F = np.stack([Fre, Fim], 0).reshape(2, nft, 128, nft, 128).transpose(3, 2, 0, 1, 4)
        c["hF_" + kind] = np.ascontiguousarray(F).astype(ml_dtypes.bfloat16)
        Bre = (2.0 / N) * np.cos(ang.T); Bre[0, :] = 1.0 / N
        Bim = -(2.0 / N) * np.sin(ang.T); Bim[0, :] = sgn / N
        B = np.stack([Bre, Bim], 0).reshape(2, nft, 128, nft, 128).transpose(3, 2, 0, 1, 4)
        c["hB_" + kind] = np.ascontiguousarray(B).astype(ml_dtypes.bfloat16)
        pos = np.arange(L, dtype=np.float32)
        tl = np.linspace(0.0, 1.0, L, dtype=np.float32)[:, None]
        a2 = (2.0 * math.pi * pos[:, None] / L).astype(np.float32)
        bands = np.linspace(1e-4, 15.0, 16, dtype=np.float32)[None, :]
        feats = np.concatenate([tl, np.cos(bands * a2), -np.sin(bands * a2)], -1).astype(np.float32)
        c["hfeat_" + kind] = np.ascontiguousarray(feats.T)
        max_decay = math.log(1e-2) / 0.3; min_decay = math.log(1e-2) / 1.5
        deltas = np.abs(np.linspace(min_decay, max_decay, GW, dtype=np.float32))
        c["hwin_" + kind] = np.exp(-tl * deltas).astype(np.float32)
    return c


CONST_SPECS = None


class Builder:
    def __init__(self, n_seq=4, layers=(0, 1), dbg=False, stages=None):
        self.n_seq = n_seq
        self.layers = layers
        self.dbg = dbg
        self.stages = stages
        self.uid = 0
        self.last_layer = 1

    def nm(self, base):
        self.uid += 1
        return "%s_%d" % (base, self.uid)

    def build(self, consts):
        nc = bass.Bass("TRN2", target_bir_lowering=False)
        self.nc = nc
        self.P = Prog(nc)
        n_seq = self.n_seq
        I = {}

        def inp(name, shape, dt=F32):
            I[name] = nc.dram_tensor(name, list(shape), dt, kind="ExternalInput").ap()

        inp("x", [n_seq, SEQ, D]); inp("c", [n_seq, D]); inp("ctx", [n_seq, CTX, D]); inp("c_ctx", [D])
        inp("ada_w", [2, D, 6 * D]); inp("ada_b", [2, 6 * D]); inp("w_in", [2, D, DPROJ])
        inp("mlstm_conv_w", [2, 3, 512]); inp("mlstm_conv_b", [2, 512]); inp("mlstm_gate_b", [2, 4, 4])
        inp("hyena_conv_w", [2, 3, 768]); inp("hyena_conv_b", [2, 768]); inp("hyena_w1", [2, 33, 64])
        inp("hyena_b1", [2, 64]); inp("hyena_w2", [2, 64, 64]); inp("hyena_b2", [2, 64]); inp("hyena_w3", [2, 64, 1024])
        inp("hyena_freq", [2, 64]); inp("hyena_d", [2, 2, 256])
        inp("rwkv_mu", [2, 2, 896]); inp("rwkv_w0", [2, 2, 256]); inp("rwkv_w2", [2, 2, 32, 256])
        inp("rwkv_a0", [2, 2, 256]); inp("rwkv_a2", [2, 2, 32, 256]); inp("rwkv_g2", [2, 64, 256])
        inp("rwkv_kk", [2, 256]); inp("rwkv_ka", [2, 256]); inp("rwkv_rk", [2, 4, 64])
        inp("out_norm_g", [2, D]); inp("out_norm_b", [2, D]); inp("w_out", [2, D, D])
        inp("ln1_g", [2, D]); inp("ln1_b", [2, D]); inp("ffn_w_up", [2, D, 2 * DFF])
        inp("ffn_conv_w", [2, 3, DFF]); inp("ffn_conv_b", [2, DFF]); inp("ffn_w_down", [2, DFF, D])
        inp("ln2_g", [2, D]); inp("ln2_b", [2, D])
        for k, v in consts.items():
            inp("k_" + k, v.shape, BF16 if v.dtype == ml_dtypes.bfloat16 else F32)
        if self.dbg:
            inp("mix_in", [TOK, D])
        self.I = I
        self.out = nc.dram_tensor("out", [n_seq, SEQ, D], F32, kind="ExternalOutput").ap()

        def scr(name, shape, dt=F32):
            kind = "ExternalOutput" if (self.dbg and name in ("proj", "mix", "modv0", "xdump")) else None
            if kind:
                return nc.dram_tensor(name, list(shape), dt, kind=kind).ap()
            return nc.dram_tensor(name, list(shape), dt).ap()

        S = {}
        for l in (0, 1):
            S["wb_in%d" % l] = scr("wb_in%d" % l, [D, DPROJ], BF16)
            S["wb_out%d" % l] = scr("wb_out%d" % l, [D, D], BF16)
            S["wb_up%d" % l] = scr("wb_up%d" % l, [D, 2 * DFF], BF16)
            S["wb_dn%d" % l] = scr("wb_dn%d" % l, [DFF, D], BF16)
            S["modv%d" % l] = scr("modv%d" % l, [5, 6 * D])
        for l in (0, 1):
            S["HT%d_lat" % l] = scr("HT%d_lat" % l, [16, 128, 3, 512])
            S["HT%d_ctx" % l] = scr("HT%d_ctx" % l, [2, 128, 3, 512])
        S["proj"] = scr("proj", [NROWS, DPROJ])
        S["mix"] = scr("mix", [TOK, D])
        S["gT"] = scr("gT", [DFF, TOK], BF16)
        if self.dbg:
            S["xdump"] = scr("xdump", [TOK, D])
        self.S = S

        with ExitStack() as top:
            self.top = top
            self.alloc_globals(top)
            self.stage_weights()
            self.stage_mod()
            if self.on("hyena"):
                self.stage_hyena()
            for s in range(n_seq):
                self.run_sequence(s)
            self.P.emit(top)
        return nc

    def sbt(self, st, base, shape, dt=F32):
        return st.enter_context(self.nc.sbuf_tensor(self.nm(base), list(shape), dt))

    def pst(self, st, base, shape, dt=F32):
        return st.enter_context(self.nc.psum_tensor(self.nm(base), list(shape), dt))

    def bank(self, st, base, dt=F32):
        return self.pst(st, base, [128, 512 if dt == F32 else 1024], dt)

    def alloc_globals(self, st):
        P, I = self.P, self.I
        self.xres = [self.sbt(st, "xres", [128, D]) for _ in range(NT)]
        self.identf = self.sbt(st, "identf", [128, 128])
        self.identb = self.sbt(st, "identb", [128, 128], BF16)
        self.tri = self.sbt(st, "tri", [128, 4, 128])
        self.ones = self.sbt(st, "ones", [128, 128])
        self.epsc = self.sbt(st, "epsc", [128, 1])
        self.zero = self.sbt(st, "zero", [128, 512])
        self.modp = self.sbt(st, "modp", [128, 2, 5, 4, 8])
        P.dma(self.identf[:], I["k_ident"], w=[self.identf])
        P.dma(self.tri[:], I["k_tri"].rearrange("m j i -> j m i"), w=[self.tri])
        P.dma(self.ones[:], I["k_ones"], w=[self.ones])
        P.op("dve", lambda e: e.tensor_copy(out=self.identb[:], in_=self.identf[:]), r=[self.identf], w=[self.identb])
        P.op("pool", lambda e: e.memset(self.epsc[:], 1e-5), w=[self.epsc])
        P.op("pool", lambda e: e.memset(self.modp[:], 0.0), w=[self.modp])
        P.op("pool", lambda e: e.memset(self.zero[:], 0.0), w=[self.zero])
        for row in (0, 257, 2306):
            for c0 in range(0, DPROJ, 512):
                n = min(512, DPROJ - c0)
                P.dma(self.S["proj"][row:row + 1, c0:c0 + n], self.zero[0:1, 0:n], r=[self.zero])

    def stage_weights(self):
        P, I, S = self.P, self.I, self.S
        with ExitStack() as st:
            stf = Rot([self.sbt(st, "wstf", [128, 2 * DFF]) for _ in range(2)])
            stb = Rot([self.sbt(st, "wstb", [128, 2 * DFF], BF16) for _ in range(2)])
            engs = Rot(["act", "dve", "pool"])
            for l in self.layers:
                for (src, dst, K, N) in ((I["w_in"][l], S["wb_in%d" % l], D, DPROJ),
                                         (I["w_out"][l], S["wb_out%d" % l], D, D),
                                         (I["ffn_w_up"][l], S["wb_up%d" % l], D, 2 * DFF),
                                         (I["ffn_w_down"][l], S["wb_dn%d" % l], DFF, D)):
                    for kk in range(K // 128):
                        a = stf.next(); b = stb.next()
                        P.dma(a[:, 0:N], src[kk * 128:(kk + 1) * 128, :], w=[a])
                        e = engs.next()
                        if e == "act":
                            P.op("act", lambda eng, a=a, b=b, N=N: eng.copy(out=b[:, 0:N], in_=a[:, 0:N]), r=[a], w=[b])
                        else:
                            P.op(e, lambda eng, a=a, b=b, N=N: eng.tensor_copy(out=b[:, 0:N], in_=a[:, 0:N]), r=[a], w=[b])
                        P.dma(dst[kk * 128:(kk + 1) * 128, :], b[:, 0:N], r=[b], eng="act")
            P.barrier()

    def stage_mod(self):
        P, I, S, nc = self.P, self.I, self.S, self.nc
        n_seq = self.n_seq
        with ExitStack() as st:
            cT = self.sbt(st, "cT", [128, 8, 5])
            aw = Rot([self.sbt(st, "aw", [128, 8, 512]) for _ in range(2)])
            ab = self.sbt(st, "ab", [5, 6 * D])
            msb = self.sbt(st, "msb", [5, 6 * D])
            ps = Rot([self.pst(st, "mps", [128, 512]) for _ in range(2)])
            P.op("pool", lambda e: e.memset(cT[:], 0.0), w=[cT])
            for r in range(n_seq):
                P.dma(cT[:, :, r], I["c"][r].rearrange("(k p) -> p k", p=128), w=[cT], allow_slow_non_contiguous=True)
            P.dma(cT[:, :, 4], I["c_ctx"].rearrange("(k p) -> p k", p=128), w=[cT], allow_slow_non_contiguous=True)
            P.op("act", lambda e: e.activation(out=cT[:], in_=cT[:], func=AF.Silu), r=[cT], w=[cT])
            for l in self.layers:
                P.dma(ab[:], I["ada_b"][l].partition_broadcast(5), w=[ab])
                for cg in range(12):
                    a = aw.next(); p = ps.next()
                    P.dma(a[:], I["ada_w"][l][:, cg * 512:(cg + 1) * 512].rearrange("(k p) n -> p k n", p=128), w=[a])

                    def mm(e, a=a, p=p):
                        for k in range(8):
                            i = e.matmul(p[0:5, :], lhsT=cT[:, k, :], rhs=a[:, k, :], start=(k == 0), stop=(k == 7))
                        return i
                    P.op("pe", mm, r=[cT, a], w=[p])
                    P.op("dve", lambda e, p=p, cg=cg: e.tensor_tensor(out=msb[:, cg * 512:(cg + 1) * 512], in0=p[0:5, :],
                                                                      in1=ab[:, cg * 512:(cg + 1) * 512], op=ALU.add),
                         r=[p, ab], w=[msb])
                P.dma(S["modv%d" % l], msb[:], r=[msb])
                P.barrier()
                for r in range(5):
                    for jj, j in enumerate((0, 1, 3, 4)):
                        P.dma(self.modp[:, l, r, jj, :], S["modv%d" % l][r, j * D:(j + 1) * D].rearrange("(k p) -> p k", p=128),
                              w=[self.modp], allow_slow_non_contiguous=True)
            for jj in (1, 3):
                P.op("dve", lambda e, jj=jj: e.tensor_scalar_add(out=self.modp[:, :, :, jj, :], in0=self.modp[:, :, :, jj, :], scalar1=1.0),
                     r=[self.modp], w=[self.modp])
            P.barrier()

    def run_sequence(self, s):
        P, I = self.P, self.I
        for t in range(NT):
            src = I["ctx"][s, t * 128:(t + 1) * 128, :] if t < 2 else I["x"][s, (t - 2) * 128:(t - 1) * 128, :]
            P.dma(self.xres[t][:], src, w=[self.xres[t]])
        for l in self.layers:
            self.run_layer(s, l)
        for t in range(2, NT):
            P.dma(self.out[s, (t - 2) * 128:(t - 1) * 128, :], self.xres[t][:], r=[self.xres[t]])
        if self.dbg:
            for t in range(NT):
                P.dma(self.S["xdump"][t * 128:(t + 1) * 128, :], self.xres[t][:], r=[self.xres[t]])
        P.barrier()

    def on(self, name):
        return self.stages is None or name in self.stages

    def run_layer(self, s, l):
        P = self.P
        self.cur_layer = l
        with ExitStack() as st:
            uT = self.sbt(st, "uT", [128, 8, TOK], BF16)
            self.ln_mod(s, l, uT, 0)
            if self.on("proj"):
                self.in_proj(l, uT)
            P.barrier()
        if self.on("ret"):
            self.mix_retention(s, l)
        if self.on("mlstm"):
            self.mix_mlstm(s, l)
        if self.on("hyena"):
            self.mix_hyena(s, l)
        if self.on("rwkv"):
            self.mix_rwkv(s, l)
        if self.dbg and self.on("mixin"):
            for t in range(NT):
                P.dma(self.S["mix"][t * 128:(t + 1) * 128, :], self.I["mix_in"][t * 128:(t + 1) * 128, :])
            P.barrier()
        if self.on("post"):
            self.post_mix(s, l)
        if self.on("ffn") or self.on("ffnup"):
            with ExitStack() as st:
                uT = self.sbt(st, "uT", [128, 8, TOK], BF16)
                self.ln_mod(s, l, uT, 1)
                self.ffn_up(l, uT)
                P.barrier()
        if self.on("ffn") or self.on("ffndown"):
            self.ffn_down(s, l)
        P.barrier()

    def bc(self, st, src, n, name="bc"):
        t = self.sbt(st, name, [128, n])
        self.P.dma(t[:], src.partition_broadcast(128), w=[t])
        return t

    def resid_ln(self, t, py, mbc, g, b, tmp, stats, mv, rstd):
        P = self.P
        x = self.xres[t]
        for h in range(2):
            P.op("dve", lambda e, h=h: e.tensor_tensor(out=tmp[:, h * 512:(h + 1) * 512], in0=py[h][:], in1=mbc[:, h * 512:(h + 1) * 512], op=ALU.mult),
                 r=[py[h], mbc], w=[tmp])
        P.op("dve", lambda e: e.scalar_tensor_tensor(out=tmp[:], in0=x[:], scalar=ALPHA, in1=tmp[:], op0=ALU.mult, op1=ALU.add), r=[x, tmp], w=[tmp])
        self.ln_stats(tmp, stats, mv, rstd)
        P.op("dve", lambda e: e.tensor_scalar(out=tmp[:], in0=tmp[:], scalar1=mv[:, 0:1], scalar2=rstd[:, 0:1], op0=ALU.subtract, op1=ALU.mult),
             r=[tmp, mv, rstd], w=[tmp])
        P.op("pool", lambda e: e.tensor_tensor(out=tmp[:], in0=tmp[:], in1=g[:], op=ALU.mult), r=[tmp, g], w=[tmp])
        P.op("dve", lambda e: e.tensor_tensor(out=x[:], in0=tmp[:], in1=b[:], op=ALU.add), r=[tmp, b], w=[x])

    def post_mix(self, s, l):
        P, S, I = self.P, self.S, self.I
        with ExitStack() as st:
            wo = self.sbt(st, "wo", [128, 8, D], BF16)
            P.dma(wo[:], S["wb_out%d" % l].rearrange("(k p) n -> p k n", p=128), w=[wo])
            g = self.bc(st, I["ln1_g"][l], D); b = self.bc(st, I["ln1_b"][l], D)
            m_s = self.bc(st, S["modv%d" % l][s, 2 * D:3 * D], D); m_c = self.bc(st, S["modv%d" % l][4, 2 * D:3 * D], D)
            mx = Rot([self.sbt(st, "mx", [128, D]) for _ in range(3)])
            mxb = Rot([self.sbt(st, "mxb", [128, D], BF16) for _ in range(2)])
            mT = Rot([self.sbt(st, "mT", [128, 8, 128], BF16) for _ in range(2)])
            tmp = self.sbt(st, "tmp", [128, D])
            stats = self.sbt(st, "stats", [128, 2, 6]); mv = self.sbt(st, "mv", [128, 2]); rstd = self.sbt(st, "rstd", [128, 1])
            pt = Rot([self.pst(st, "ptr", [128, 8, 128], BF16) for _ in range(2)])
            py = Rot([[self.pst(st, "py", [128, 512]) for _ in range(2)] for _ in range(2)])
            for t in range(2 if l == self.last_layer else 0, NT):
                a = mx.next(); ab = mxb.next(); m = mT.next(); p = pt.next(); y = py.next()
                P.dma(a[:], S["mix"][t * 128:(t + 1) * 128, :], w=[a])
                P.op("act", lambda e, a=a, ab=ab: e.copy(out=ab[:], in_=a[:]), r=[a], w=[ab])

                def tr(e, ab=ab, p=p):
                    for k in range(8):
                        i = e.transpose(out=p[:, k, :], in_=ab[:, k * 128:(k + 1) * 128], identity=self.identb[:])
                    return i
                P.op("pe", tr, r=[ab, self.identb], w=[p])
                P.op("act", lambda e, p=p, m=m: e.copy(out=m[:], in_=p[:]), r=[p], w=[m])
                for h in range(2):
                    def mm(e, m=m, y=y, h=h):
                        for k in range(8):
                            i = e.matmul(y[h][:], lhsT=m[:, k, :], rhs=wo[:, k, h * 512:(h + 1) * 512], start=(k == 0), stop=(k == 7))
                        return i
                    P.op("pe", mm, r=[m, wo], w=[y[h]])
                self.resid_ln(t, y, m_c if t < 2 else m_s, g, b, tmp, stats, mv, rstd)
            P.barrier()

    def ffn_up(self, l, uT):
        P, S, I = self.P, self.S, self.I
        NJ = DFF // 128
        chunks = [(0, 256), (256, 768), (768, 1280), (1280, 1792), (1792, 2304)]
        if l == self.last_layer:
            chunks = chunks[1:]
        with ExitStack() as st:
            fcw = self.sbt(st, "fcw", [128, 3, NJ]); fcb = self.sbt(st, "fcb", [128, NJ])
            for tap in range(3):
                P.dma(fcw[:, tap, :], I["ffn_conv_w"][l, tap].rearrange("(j p) -> p j", p=128), w=[fcw], allow_slow_non_contiguous=True)
            P.dma(fcb[:], I["ffn_conv_b"][l].rearrange("(j p) -> p j", p=128), w=[fcb], allow_slow_non_contiguous=True)
            wa = Rot([self.sbt(st, "wa", [128, 8, 128], BF16) for _ in range(2)])
            wb = Rot([self.sbt(st, "wb", [128, 8, 128], BF16) for _ in range(2)])
            aS = Rot([self.sbt(st, "aS", [128, NROWS]) for _ in range(2)])
            bS = Rot([self.sbt(st, "bS", [128, NROWS]) for _ in range(2)])
            cv = Rot([self.sbt(st, "cv", [128, NROWS]) for _ in range(2)])
            gS = Rot([self.sbt(st, "gS", [128, NROWS], BF16) for _ in range(2)])
            pa = Rot([self.pst(st, "pa", [128, 512]) for _ in range(4)])
            pb = Rot([self.pst(st, "pb", [128, 512]) for _ in range(4)])
            for a in aS.items + bS.items:
                P.op("pool", lambda e, a=a: e.memset(a[:], 0.0), w=[a])
            for j in range(NJ):
                w1 = wa.next(); w2 = wb.next(); a = aS.next(); b = bS.next(); c = cv.next(); g = gS.next()
                P.dma(w1[:], S["wb_up%d" % l][:, j * 128:(j + 1) * 128].rearrange("(k p) n -> p k n", p=128), w=[w1])
                P.dma(w2[:], S["wb_up%d" % l][:, DFF + j * 128:DFF + (j + 1) * 128].rearrange("(k p) n -> p k n", p=128), w=[w2])
                for (c0, c1) in chunks:
                    n = c1 - c0
                    o0 = c0 + 1 if c0 < 256 else c0 + 2
                    p1 = pa.next(); p2 = pb.next()

                    def mm(e, w=w1, p=p1, c0=c0, c1=c1, n=n):
                        for k in range(8):
                            i = e.matmul(p[:, 0:n], lhsT=w[:, k, :], rhs=uT[:, k, c0:c1], start=(k == 0), stop=(k == 7))
                        return i
                    P.op("pe", mm, r=[w1] + [(uT, t) for t in range(2 if l == self.last_layer else 0, NT)], w=[p1])

                    def mm2(e, w=w2, p=p2, c0=c0, c1=c1, n=n):
                        for k in range(8):
                            i = e.matmul(p[:, 0:n], lhsT=w[:, k, :], rhs=uT[:, k, c0:c1], start=(k == 0), stop=(k == 7))
                        return i
                    P.op("pe", mm2, r=[w2] + [(uT, t) for t in range(2 if l == self.last_layer else 0, NT)], w=[p2])
                    P.op("act", lambda e, p=p1, a=a, o0=o0, n=n: e.copy(out=a[:, o0:o0 + n], in_=p[:, 0:n]), r=[p1], w=[a])
                    P.op("dve", lambda e, p=p2, b=b, o0=o0, n=n: e.tensor_copy(out=b[:, o0:o0 + n], in_=p[:, 0:n]), r=[p2], w=[b])
                W = NROWS - 2
                P.op("dve", lambda e, a=a, c=c, j=j: e.tensor_scalar(out=c[:, 1:1 + W], in0=a[:, 0:W], scalar1=fcw[:, 0, j:j + 1], scalar2=fcb[:, j:j + 1], op0=ALU.mult, op1=ALU.add),
                     r=[a, fcw, fcb], w=[c])
                P.op("dve", lambda e, a=a, c=c, j=j: e.scalar_tensor_tensor(out=c[:, 1:1 + W], in0=a[:, 1:1 + W], scalar=fcw[:, 1, j:j + 1], in1=c[:, 1:1 + W], op0=ALU.mult, op1=ALU.add),
                     r=[a, fcw, c], w=[c])
                P.op("dve", lambda e, a=a, c=c, j=j: e.scalar_tensor_tensor(out=c[:, 1:1 + W], in0=a[:, 2:2 + W], scalar=fcw[:, 2, j:j + 1], in1=c[:, 1:1 + W], op0=ALU.mult, op1=ALU.add),
                     r=[a, fcw, c], w=[c])
                P.op("act", lambda e, c=c: e.activation(out=c[:, 1:1 + W], in_=c[:, 1:1 + W], func=AF.Silu), r=[c], w=[c])
                P.op("pool", lambda e, c=c, b=b, g=g: e.tensor_tensor(out=g[:, 1:1 + W], in0=c[:, 1:1 + W], in1=b[:, 1:1 + W], op=ALU.mult), r=[c, b], w=[g])
                if l != self.last_layer:
                    P.dma(S["gT"][j * 128:(j + 1) * 128, 0:256], g[:, 1:257], r=[g], eng="act")
                P.dma(S["gT"][j * 128:(j + 1) * 128, 256:TOK], g[:, 258:258 + SEQ], r=[g], eng="act")

    def ffn_down(self, s, l):
        P, S, I = self.P, self.S, self.I
        NJ = DFF // 128
        with ExitStack() as st:
            wd = self.sbt(st, "wd", [128, NJ, D], BF16)
            for j in range(NJ):
                P.dma(wd[:, j, :], S["wb_dn%d" % l][j * 128:(j + 1) * 128, :], w=[wd])
            g = self.bc(st, I["ln2_g"][l], D); b = self.bc(st, I["ln2_b"][l], D)
            m_s = self.bc(st, S["modv%d" % l][s, 5 * D:6 * D], D); m_c = self.bc(st, S["modv%d" % l][4, 5 * D:6 * D], D)
            gt = Rot([self.sbt(st, "gt", [128, NJ, 128], BF16) for _ in range(3)])
            tmp = self.sbt(st, "tmp", [128, D])
            stats = self.sbt(st, "stats", [128, 2, 6]); mv = self.sbt(st, "mv", [128, 2]); rstd = self.sbt(st, "rstd", [128, 1])
            py = Rot([[self.pst(st, "py", [128, 512]) for _ in range(2)] for _ in range(2)])
            for t in range(2 if l == self.last_layer else 0, NT):
                gg = gt.next(); y = py.next()
                for j0 in (0, 6, 12, 18):
                    j1 = min(NJ, j0 + 6)
                    P.dma(gg[:, j0:j1, :], S["gT"][j0 * 128:j1 * 128, t * 128:(t + 1) * 128].rearrange("(j p) n -> p j n", p=128), w=[gg])
                for h in range(2):
                    def mm(e, gg=gg, y=y, h=h):
                        for j in range(NJ):
                            i = e.matmul(y[h][:], lhsT=gg[:, j, :], rhs=wd[:, j, h * 512:(h + 1) * 512], start=(j == 0), stop=(j == NJ - 1))
                        return i
                    P.op("pe", mm, r=[gg, wd], w=[y[h]])
                self.resid_ln(t, y, m_c if t < 2 else m_s, g, b, tmp, stats, mv, rstd)
            P.barrier()


    def finish_heads(self, st_tiles, hacc, gate, gfunc, ng, nb, t, col, extra=None):
        P = self.P
        if t < 2 and self.cur_layer == self.last_layer:
            return
        s1, s2, gm, grs, sq, gout = st_tiles
        hv = hacc[:].rearrange("p (h f) -> p h f", f=64)
        gv = gout[:].rearrange("p (h f) -> p h f", f=64)
        bcl = lambda a: a.unsqueeze(2).to_broadcast([128, 4, 64])
        P.op("dve", lambda e: e.tensor_reduce(out=s1[:], in_=hv, axis=AX.X, op=ALU.add), r=[hacc], w=[s1])
        P.op("pool", lambda e: e.tensor_tensor(out=sq[:], in0=hacc[:], in1=hacc[:], op=ALU.mult), r=[hacc], w=[sq])
        P.op("dve", lambda e: e.tensor_reduce(out=s2[:], in_=sq[:].rearrange("p (h f) -> p h f", f=64), axis=AX.X, op=ALU.add), r=[sq], w=[s2])
        P.op("dve", lambda e: e.tensor_scalar_mul(out=gm[:], in0=s1[:], scalar1=1.0 / 64), r=[s1], w=[gm])
        P.op("dve", lambda e: e.tensor_tensor(out=s1[:], in0=gm[:], in1=gm[:], op=ALU.mult), r=[gm, s1], w=[s1])
        P.op("dve", lambda e: e.scalar_tensor_tensor(out=grs[:], in0=s2[:], scalar=1.0 / 64, in1=s1[:], op0=ALU.mult, op1=ALU.subtract), r=[s2, s1], w=[grs])
        P.op("act", lambda e: e.activation(out=grs[:], in_=grs[:], func=AF.Sqrt, bias=self.epsc[:, 0:1], scale=1.0), r=[grs, self.epsc], w=[grs])
        P.op("dve", lambda e: e.reciprocal(out=grs[:], in_=grs[:]), r=[grs], w=[grs])
        P.op("dve", lambda e: e.tensor_tensor(out=gv, in0=hv, in1=bcl(gm[:]), op=ALU.subtract), r=[hacc, gm], w=[gout])
        P.op("dve", lambda e: e.tensor_tensor(out=gv, in0=gv, in1=bcl(grs[:]), op=ALU.mult), r=[gout, grs], w=[gout])
        P.op("pool", lambda e: e.tensor_tensor(out=gout[:], in0=gout[:], in1=ng[:], op=ALU.mult), r=[gout, ng], w=[gout])
        P.op("pool", lambda e: e.tensor_tensor(out=gout[:], in0=gout[:], in1=nb[:], op=ALU.add), r=[gout, nb], w=[gout])
        if extra is not None:
            P.op("pool", lambda e: e.tensor_tensor(out=gout[:], in0=gout[:], in1=extra[:], op=ALU.add), r=[gout, extra], w=[gout])
        if gfunc is not None:
            P.op("act", lambda e: e.activation(out=gate, in_=gate, func=gfunc), r=[gate], w=[gate])
        if gate is not None:
            P.op("dve", lambda e: e.tensor_tensor(out=gout[:], in0=gout[:], in1=gate, op=ALU.mult), r=[gout, gate], w=[gout])
        P.dma(self.S["mix"][t * 128:(t + 1) * 128, col:col + 256], gout[:], r=[gout], eng="act")

    def head_tiles(self, st):
        return (self.sbt(st, "gs1", [128, 4]), self.sbt(st, "gs2", [128, 4]), self.sbt(st, "ggm", [128, 4]), self.sbt(st, "grs", [128, 4]),
                self.sbt(st, "gsq", [128, 256]), self.sbt(st, "gout", [128, 256]))

    @staticmethod
    def order(d):
        return list(range(NT)) if d == 0 else [1, 0] + list(range(NT - 1, 1, -1))


    def wrap_sin(self, h, tmp, L):
        P = self.P
        for _ in range(2):
            P.op("dve", lambda e: e.tensor_scalar(out=tmp[:], in0=h[:], scalar1=-math.pi, scalar2=2.0 * math.pi, op0=ALU.is_lt, op1=ALU.mult), r=[h], w=[tmp])
            P.op("dve", lambda e: e.tensor_tensor(out=h[:], in0=h[:], in1=tmp[:], op=ALU.add), r=[h, tmp], w=[h])
            P.op("dve", lambda e: e.tensor_scalar(out=tmp[:], in0=h[:], scalar1=math.pi, scalar2=-2.0 * math.pi, op0=ALU.is_gt, op1=ALU.mult), r=[h], w=[tmp])
            P.op("dve", lambda e: e.tensor_tensor(out=h[:], in0=h[:], in1=tmp[:], op=ALU.add), r=[h, tmp], w=[h])
        P.op("act", lambda e: e.activation(out=h[:], in_=h[:], func=AF.Sin), r=[h], w=[h])

    def stage_hyena(self):
        for l in self.layers:
            for kind, L in (("lat", SEQ), ("ctx", CTX)):
                if kind == "ctx" and l == self.last_layer:
                    continue
                self.stage_hyena_one(l, kind, L)

    def stage_hyena_one(self, l, kind, L):
        P, S, I = self.P, self.S, self.I
        if True:
            if True:
                nft = L // 128
                with ExitStack() as st:
                    w1 = self.sbt(st, "hw1", [33, 64]); w2 = self.sbt(st, "hw2", [64, 64]); w3 = self.sbt(st, "hw3", [64, 1024])
                    b1 = self.sbt(st, "hb1", [64, 1]); b2 = self.sbt(st, "hb2", [64, 1]); fr = self.sbt(st, "hfr", [64, 1])
                    fT = self.sbt(st, "hfT", [33, L]); h1 = self.sbt(st, "hh1", [64, L]); h2 = self.sbt(st, "hh2", [64, L]); tmp = self.sbt(st, "htmp", [64, L])
                    P.dma(w1[:], I["hyena_w1"][l], w=[w1]); P.dma(w2[:], I["hyena_w2"][l], w=[w2]); P.dma(w3[:], I["hyena_w3"][l], w=[w3])
                    P.dma(b1[:], I["hyena_b1"][l].rearrange("(p o) -> p o", o=1), w=[b1])
                    P.dma(b2[:], I["hyena_b2"][l].rearrange("(p o) -> p o", o=1), w=[b2])
                    P.dma(fr[:], I["hyena_freq"][l].rearrange("(p o) -> p o", o=1), w=[fr])
                    P.dma(fT[:], I["k_hfeat_" + kind], w=[fT])
                    pb = Rot([self.bank(st, "hps") for _ in range(4)])
                    for (w, src, bb, dst) in ((w1, fT, b1, h1), (w2, h1, b2, h2)):
                        for c0 in range(0, L, 512):
                            n = min(512, L - c0)
                            p = pb.next()
                            P.op("pe", lambda e, p=p, w=w, src=src, c0=c0, n=n: e.matmul(p[0:64, 0:n], lhsT=w[:], rhs=src[:, c0:c0 + n], start=True, stop=True), r=[w, src], w=[p])
                            P.op("dve", lambda e, p=p, dst=dst, bb=bb, c0=c0, n=n: e.tensor_scalar(out=dst[:, c0:c0 + n], in0=p[0:64, 0:n], scalar1=bb[:, 0:1], scalar2=fr[:, 0:1], op0=ALU.add, op1=ALU.mult),
                                 r=[p, bb, fr], w=[dst])
                        self.wrap_sin(dst, tmp, L)
                    Pm = self.sbt(st, "hPm", [128, nft, 512], BF16); Mm = self.sbt(st, "hMm", [128, nft, 512], BF16)
                    win = Rot([self.sbt(st, "hwin", [128, 256]) for _ in range(2)])
                    hf = self.sbt(st, "hhf", [128, 512]); hb = self.sbt(st, "hhb", [128, 512])
                    for tt in range(nft):
                        wn = win.next(); pA = pb.next(); pB = pb.next()
                        P.dma(wn[:], I["k_hwin_" + kind][tt * 128:(tt + 1) * 128, :], w=[wn])
                        P.op("pe", lambda e, pA=pA, tt=tt: e.matmul(pA[:], lhsT=h2[:, tt * 128:(tt + 1) * 128], rhs=w3[:, 0:512], start=True, stop=True), r=[h2, w3], w=[pA])
                        P.op("pe", lambda e, pB=pB, tt=tt: e.matmul(pB[:], lhsT=h2[:, tt * 128:(tt + 1) * 128], rhs=w3[:, 512:1024], start=True, stop=True), r=[h2, w3], w=[pB])
                        for o in range(2):
                            P.op("dve", lambda e, pA=pA, wn=wn, o=o: e.tensor_tensor(out=hf[:, o * 256:(o + 1) * 256], in0=pA[:, o * 256:(o + 1) * 256], in1=wn[:], op=ALU.mult), r=[pA, wn], w=[hf])
                            P.op("dve", lambda e, pB=pB, wn=wn, o=o: e.tensor_tensor(out=hb[:, o * 256:(o + 1) * 256], in0=pB[:, o * 256:(o + 1) * 256], in1=wn[:], op=ALU.mult), r=[pB, wn], w=[hb])
                        P.op("pool", lambda e, tt=tt: e.tensor_tensor(out=Pm[:, tt, :], in0=hf[:], in1=hb[:], op=ALU.add), r=[hf, hb], w=[Pm])
                        P.op("pool", lambda e, tt=tt: e.tensor_tensor(out=Mm[:, tt, :], in0=hf[:], in1=hb[:], op=ALU.subtract), r=[hf, hb], w=[Mm])
                    bas = Rot([self.sbt(st, "hbas", [128, 2, nft, 128], BF16) for _ in range(2)])
                    tab = Rot([self.sbt(st, "htab", [128, 3, 512]) for _ in range(2)])
                    for ft in range(nft):
                        ba = bas.next(); tb = tab.next(); pK = pb.next(); pI = pb.next()
                        P.dma(ba[:], I["k_hF_" + kind][ft], w=[ba])

                        def mk(e, ba=ba, p=pK, ri=0, src=Pm):
                            for ch in range(nft):
                                i = e.matmul(p[:], lhsT=ba[:, ri, ch, :], rhs=src[:, ch, :], start=(ch == 0), stop=(ch == nft - 1))
                            return i
                        P.op("pe", mk, r=[ba, Pm], w=[pK])
                        P.op("pe", lambda e, ba=ba, pI=pI, mk=mk: mk(e, ba=ba, p=pI, ri=1, src=Mm), r=[ba, Mm], w=[pI])
                        P.op("act", lambda e, tb=tb, pK=pK: e.copy(out=tb[:, 0, :], in_=pK[:]), r=[pK], w=[tb])
                        P.op("dve", lambda e, tb=tb, pK=pK: e.tensor_copy(out=tb[:, 2, :], in_=pK[:]), r=[pK], w=[tb])
                        P.op("act", lambda e, tb=tb, pI=pI: e.copy(out=tb[:, 1, :], in_=pI[:]), r=[pI], w=[tb])
                        if ft == 0:
                            pN = pb.next()
                            P.op("pe", lambda e, ba=ba, pN=pN, mk=mk: mk(e, ba=ba, p=pN, ri=1, src=Pm), r=[ba, Pm], w=[pN])
                            P.op("pool", lambda e, tb=tb: e.memset(tb[0:1, 1, :], 0.0), r=[tb], w=[tb])
                            P.op("dve", lambda e, tb=tb, pN=pN: e.tensor_copy(out=tb[0:1, 2, :], in_=pN[0:1, :]), r=[pN, tb], w=[tb])
                        P.dma(S["HT%d_%s" % (l, kind)][ft], tb[:], r=[tb], eng="act")
                    P.barrier()

    def mix_hyena(self, s, l):
        for kind, tiles in (("lat", list(range(2, NT))), ("ctx", [0, 1])):
            if kind == "ctx" and l == self.last_layer:
                continue
            self.mix_hyena_one(s, l, kind, tiles)

    def mix_hyena_one(self, s, l, kind, tiles):
        P, S, I = self.P, self.S, self.I
        C0 = 1040
        if True:
            nft = len(tiles)
            with ExitStack() as st:
                store = self.sbt(st, "hst", [128, nft, 768])
                Vb = self.sbt(st, "hVb", [128, nft, 256], BF16)
                with ExitStack() as st2:
                    cw = self.sbt(st2, "hcw", [128, 3, 768]); cb = self.bc(st2, I["hyena_conv_b"][l], 768)
                    for tap in range(3):
                        P.dma(cw[:, tap, :], I["hyena_conv_w"][l, tap].partition_broadcast(128), w=[cw])
                    q3 = Rot([self.sbt(st2, "hq3", [128, 3, 768]) for _ in range(2)])
                    tmp = self.sbt(st2, "hctmp", [128, 768])
                    for i, t in enumerate(tiles):
                        q = q3.next(); rb = rowbase(t)
                        for tap in range(3):
                            P.dma(q[:, tap, :], S["proj"][rb - 1 + tap:rb - 1 + tap + 128, C0:C0 + 768], w=[q])
                        sti = store[:, i, :]; K = (store, i)
                        P.op("dve", lambda e, q=q, sti=sti: e.tensor_tensor(out=sti, in0=q[:, 0, :], in1=cw[:, 0, :], op=ALU.mult), r=[q, cw], w=[K])
                        P.op("pool", lambda e, q=q: e.tensor_tensor(out=tmp[:], in0=q[:, 1, :], in1=cw[:, 1, :], op=ALU.mult), r=[q, cw], w=[tmp])
                        P.op("dve", lambda e, sti=sti: e.tensor_tensor(out=sti, in0=sti, in1=tmp[:], op=ALU.add), r=[K, tmp], w=[K])
                        P.op("pool", lambda e, q=q: e.tensor_tensor(out=tmp[:], in0=q[:, 2, :], in1=cw[:, 2, :], op=ALU.mult), r=[q, cw], w=[tmp])
                        P.op("dve", lambda e, sti=sti: e.tensor_tensor(out=sti, in0=sti, in1=tmp[:], op=ALU.add), r=[K, tmp], w=[K])
                        P.op("pool", lambda e, sti=sti: e.tensor_tensor(out=sti, in0=sti, in1=cb[:], op=ALU.add), r=[K, cb], w=[K])
                        P.op("act", lambda e, i=i: e.copy(out=Vb[:, i, :], in_=store[:, i, 0:256]), r=[K], w=[(Vb, i)])
                    P.barrier()
                ng = self.bc(st, I["out_norm_g"][l, 256:512], 256); nb = self.bc(st, I["out_norm_b"][l, 256:512], 256)
                dd = self.sbt(st, "hdd", [128, 2, 256])
                for o in range(2):
                    P.dma(dd[:, o, :], I["hyena_d"][l, o].partition_broadcast(128), w=[dd])
                Ys = self.sbt(st, "hYs", [128, nft, 2, 256], BF16)
                bas = Rot([self.sbt(st, "hbas", [128, 2, nft, 128], BF16) for _ in range(4)])
                tab = Rot([self.sbt(st, "htab", [128, 3, 256]) for _ in range(4)])
                t1 = self.sbt(st, "ht1", [128, 256]); t2 = self.sbt(st, "ht2", [128, 256]); t3 = self.sbt(st, "ht3", [128, 256]); t4 = self.sbt(st, "ht4", [128, 256])
                yy = Rot([self.sbt(st, "hyy", [128, 256]) for _ in range(2)])
                ht = self.head_tiles(st)
                pV = Rot([self.bank(st, "hpV") for _ in range(2)]); pY = Rot([self.bank(st, "hpY") for _ in range(2)])
                allV = [(Vb, i) for i in range(nft)]; allY = [(Ys, i) for i in range(nft)]
                for o in range(2):
                    for ft in range(nft):
                        ba = bas.next(); tb = tab.next(); p = pV.next()
                        P.dma(ba[:], I["k_hF_" + kind][ft], w=[ba])
                        P.dma(tb[:], S["HT%d_%s" % (l, kind)][ft][:, :, o * 256:(o + 1) * 256], w=[tb])

                        def fw(e, ba=ba, p=p):
                            for ri in range(2):
                                for ch in range(nft):
                                    i = e.matmul(p[:, ri * 256:(ri + 1) * 256], lhsT=ba[:, ri, ch, :], rhs=Vb[:, ch, :], start=(ch == 0), stop=(ch == nft - 1))
                            return i
                        P.op("pe", fw, r=[ba] + allV, w=[p])
                        P.op("dve", lambda e, p=p, tb=tb: e.tensor_tensor(out=t1[:], in0=p[:, 0:256], in1=tb[:, 0, :], op=ALU.mult), r=[p, tb], w=[t1])
                        P.op("dve", lambda e, p=p, tb=tb: e.tensor_tensor(out=t2[:], in0=p[:, 256:512], in1=tb[:, 1, :], op=ALU.mult), r=[p, tb], w=[t2])
                        P.op("pool", lambda e, ft=ft: e.tensor_tensor(out=Ys[:, ft, 0, :], in0=t1[:], in1=t2[:], op=ALU.subtract), r=[t1, t2], w=[(Ys, ft)])
                        P.op("dve", lambda e, p=p, tb=tb: e.tensor_tensor(out=t3[:], in0=p[:, 0:256], in1=tb[:, 1, :], op=ALU.mult), r=[p, tb], w=[t3])
                        P.op("dve", lambda e, p=p, tb=tb: e.tensor_tensor(out=t4[:], in0=p[:, 256:512], in1=tb[:, 2, :], op=ALU.mult), r=[p, tb], w=[t4])
                        P.op("pool", lambda e, ft=ft: e.tensor_tensor(out=Ys[:, ft, 1, :], in0=t3[:], in1=t4[:], op=ALU.add), r=[t3, t4], w=[(Ys, ft)])
                    for i, t in enumerate(tiles):
                        ba = bas.next(); p = pY.next(); K = (store, i)
                        P.dma(ba[:], I["k_hB_" + kind][i], w=[ba])

                        def iv(e, ba=ba, p=p):
                            n = 0
                            for fc in range(nft):
                                for ri in range(2):
                                    ins = e.matmul(p[:, 0:256], lhsT=ba[:, ri, fc, :], rhs=Ys[:, fc, ri, :], start=(n == 0), stop=(n == 2 * nft - 1))
                                    n += 1
                            return ins
                        P.op("pe", iv, r=[ba] + allY, w=[p])
                        P.op("pool", lambda e, i=i, o=o: e.tensor_tensor(out=t1[:], in0=store[:, i, 0:256], in1=dd[:, o, :], op=ALU.mult), r=[K, dd], w=[t1])
                        P.op("dve", lambda e, p=p: e.tensor_tensor(out=t1[:], in0=p[:, 0:256], in1=t1[:], op=ALU.add), r=[p, t1], w=[t1])
                        if o == 0:
                            P.op("dve", lambda e, i=i: e.tensor_tensor(out=store[:, i, 0:256], in0=store[:, i, 256:512], in1=t1[:], op=ALU.mult), r=[K, t1], w=[K])
                            P.op("act", lambda e, i=i: e.copy(out=Vb[:, i, :], in_=store[:, i, 0:256]), r=[K], w=[(Vb, i)])
                        else:
                            y_ = yy.next()
                            P.op("dve", lambda e, i=i, y_=y_: e.tensor_tensor(out=y_[:], in0=store[:, i, 512:768], in1=t1[:], op=ALU.mult), r=[K, t1], w=[y_])
                            self.finish_heads(ht, y_, None, None, ng, nb, t, 256)
                P.barrier()

    def TT(self, eng, out, a, b, op, r, w):
        self.P.op(eng, lambda e: e.tensor_tensor(out=out, in0=a, in1=b, op=op), r=r, w=w)

    def TS(self, eng, out, a, s1, s2, op0, op1, r, w):
        self.P.op(eng, lambda e: e.tensor_scalar(out=out, in0=a, scalar1=s1, scalar2=s2, op0=op0, op1=op1), r=r, w=w)

    def TSM(self, eng, out, a, s1, r, w):
        self.P.op(eng, lambda e: e.tensor_scalar_mul(out=out, in0=a, scalar1=s1), r=r, w=w)

    def STT(self, out, a, sc, b, op0, op1, r, w):
        self.P.op("dve", lambda e: e.scalar_tensor_tensor(out=out, in0=a, scalar=sc, in1=b, op0=op0, op1=op1), r=r, w=w)

    def ACT(self, out, a, func, r, w, bias=None, scale=1.0):
        if bias is None:
            self.P.op("act", lambda e: e.activation(out=out, in_=a, func=func, scale=scale), r=r, w=w)
        else:
            self.P.op("act", lambda e: e.activation(out=out, in_=a, func=func, bias=bias, scale=scale), r=r, w=w)

    def CP(self, eng, out, a, r, w):
        if eng == "act":
            self.P.op("act", lambda e: e.copy(out=out, in_=a), r=r, w=w)
        else:
            self.P.op(eng, lambda e: e.tensor_copy(out=out, in_=a), r=r, w=w)

    def MM(self, specs, r, w):
        specs = list(specs)

        def f(e):
            for (o, l_, r_, st_, sp_) in specs:
                i = e.matmul(o, lhsT=l_, rhs=r_, start=st_, stop=sp_)
            return i
        self.P.op("pe", f, r=r, w=w)

    def mix_rwkv(self, s, l):
        P, S, I = self.P, self.S, self.I
        C0 = 2832
        EM05 = math.exp(-0.5)
        with ExitStack() as st:
            ng = self.bc(st, I["out_norm_g"][l, 768:1024], 256); nb = self.bc(st, I["out_norm_b"][l, 768:1024], 256)
            mu = self.sbt(st, "rmu", [128, 2, 896]); w0 = self.sbt(st, "rw0", [128, 2, 256]); a0 = self.sbt(st, "ra0", [128, 2, 256])
            for j in range(2):
                P.dma(mu[:, j, :], I["rwkv_mu"][l, j].partition_broadcast(128), w=[mu])
                P.dma(w0[:, j, :], I["rwkv_w0"][l, j].partition_broadcast(128), w=[w0])
                P.dma(a0[:, j, :], I["rwkv_a0"][l, j].partition_broadcast(128), w=[a0])
            kks = self.bc(st, I["rwkv_kk"][l], 256); ka = self.bc(st, I["rwkv_ka"][l], 256)
            rk = self.bc(st, I["rwkv_rk"][l].rearrange("a b -> (a b)"), 256)
            Wl = self.sbt(st, "rWl", [128, 1280])
            P.op("pool", lambda e: e.memset(Wl[:], 0.0), w=[Wl])
            for j in range(2):
                P.dma(Wl[0:32, j * 256:(j + 1) * 256], I["rwkv_w2"][l, j], w=[Wl])
                P.dma(Wl[32:64, 512 + j * 256:512 + (j + 1) * 256], I["rwkv_a2"][l, j], w=[Wl])
            P.dma(Wl[64:128, 1024:1280], I["rwkv_g2"][l], w=[Wl])
            cmask = self.sbt(st, "rcm", [128, 2, 256])
            for d in range(2):
                self.CP("dve", cmask[:, d, 0:128], self.tri[:, 2 + d, :], [self.tri], [cmask])
                self.CP("dve", cmask[:, d, 128:256], self.tri[:, d, :], [self.tri], [cmask])
            eps12 = self.sbt(st, "reps", [128, 1])
            P.op("pool", lambda e: e.memset(eps12[:], 1e-12), w=[eps12])
            hacc = [self.sbt(st, "hacc", [128, 256]) for _ in range(NT)]
            ht = self.head_tiles(st)
            q3 = Rot([self.sbt(st, "rq3", [128, 3, 896]) for _ in range(1)])
            z = self.sbt(st, "rz", [128, 896])
            Lt = self.sbt(st, "rLt", [128, 128]); LT = self.sbt(st, "rLT", [128, 128])
            F = {n: self.sbt(st, "r" + n, [128, 256]) for n in ("logw", "a", "kk", "kd", "t1", "t2", "cum", "gam", "game", "ginv")}
            ss = self.sbt(st, "rss", [128, 4]); s4 = self.sbt(st, "rs4", [128, 4])
            X4 = Rot([self.sbt(st, "rX4", [128, 4, 256], BF16) for _ in range(3)])
            Vb = Rot([self.sbt(st, "rVb", [128, 256], BF16) for _ in range(3)])
            TTt = Rot([self.sbt(st, "rTT", [128, 8, 128], BF16) for _ in range(2)])
            ZP = Rot([self.sbt(st, "rZP", [128, 4, 2, 128], BF16) for _ in range(2)])
            for zp in ZP.items:
                P.op("pool", lambda e, zp=zp: e.memset(zp[:], 0.0), w=[zp])
            GC = [Rot([self.sbt(st, "rgc", [128, 1]) for _ in range(3)]) for _ in range(2)]
            FG = Rot([self.sbt(st, "rFg", [128, 256]) for _ in range(3)]); FB = Rot([self.sbt(st, "rFb", [128, 256]) for _ in range(3)])
            rwm = self.sbt(st, "rwm", [128, 2, 8, 128], BF16)
            P.dma(rwm[:], I["k_rwm"], w=[rwm])
            t4 = lambda nm_, n_: Rot([self.sbt(st, nm_, [128, 4, 128], BF16) for _ in range(n_)])
            RX = t4("rX", 3); RXT = t4("rXT", 3); RQ = t4("rQ", 3); RQT = t4("rQT", 3); RE = t4("rE", 2); RE2 = t4("rE2", 2)
            AdN4 = self.sbt(st, "rAdN", [128, 4, 128], BF16); AdT4 = self.sbt(st, "rAdT", [128, 4, 128], BF16)
            LoN4 = [self.sbt(st, "rLoN", [128, 4, 128], BF16) for _ in range(3)]; LoT4 = [self.sbt(st, "rLoT", [128, 4, 128], BF16) for _ in range(3)]
            RbTr = t4("rRb", 2); PTr = t4("rPT", 2)
            SKr = Rot([self.sbt(st, "rSK", [128, 2, 4, 128], BF16) for _ in range(2)])
            Wb = [self.sbt(st, "rWb", [128, 128], BF16) for _ in range(2)]
            Ub = [self.sbt(st, "rUb", [128, 128], BF16) for _ in range(2)]
            Hf = [self.sbt(st, "rHf", [128, 128]) for _ in range(2)]
            Hb = [self.sbt(st, "rHb", [128, 128], BF16) for _ in range(2)]
            htmp = self.sbt(st, "rht", [128, 128])
            pT = self.pst(st, "rpT", [128, 8, 128], BF16)
            pg = Rot([self.bank(st, "rpg") for _ in range(1)])
            pi = Rot([self.bank(st, "rpi") for _ in range(4)])
            pq = Rot([self.bank(st, "rpq") for _ in range(2)])
            for d in range(2):
                for pr in range(2):
                    P.op("pool", lambda e, pr=pr: e.memset(Hf[pr][:], 0.0), w=[Hf[pr]])
                    P.op("pool", lambda e, pr=pr: e.memset(Hb[pr][:], 0.0), w=[Hb[pr]])
                def prep(t):
                    q = q3.next(); rb = rowbase(t)
                    Fg = FG.next(); Fbon = FB.next(); gcol = [GC[0].next(), GC[1].next()]
                    for tap in range(3):
                        P.dma(q[:, tap, :], S["proj"][rb - 1 + tap:rb - 1 + tap + 128, C0:C0 + 896], w=[q])
                    self.TT("dve", q[:, 0, :], q[:, 0, :], q[:, 1, :], ALU.subtract, [q], [q])
                    self.TT("pool", q[:, 0, :], q[:, 0, :], mu[:, 0, :], ALU.mult, [q, mu], [q])
                    self.TT("pool", q[:, 2, :], q[:, 2, :], q[:, 1, :], ALU.subtract, [q], [q])
                    self.TT("pool", q[:, 2, :], q[:, 2, :], mu[:, 1, :], ALU.mult, [q, mu], [q])
                    self.TT("dve", z[:], q[:, 1, :], q[:, 0, :], ALU.add, [q], [z])
                    self.TT("dve", z[:], z[:], q[:, 2, :], ALU.add, [z, q], [z])
                    r_ = z[:, 0:256]; k_ = z[:, 256:512]; v_ = z[:, 512:768]
                    self.ACT(Lt[:, 0:32], z[:, 768:800], AF.Tanh, [z], [Lt])
                    self.CP("dve", Lt[:, 32:64], z[:, 800:832], [z], [Lt])
                    self.ACT(Lt[:, 64:128], z[:, 832:896], AF.Sigmoid, [z], [Lt])
                    p0 = pg.next()
                    P.op("pe", lambda e, p0=p0: e.transpose(out=p0[:, 0:128], in_=Lt[:], identity=self.identf[:]), r=[Lt, self.identf], w=[p0])
                    self.CP("dve", LT[:], p0[:, 0:128], [p0], [LT])
                    p1 = pg.next()
                    self.MM([(p1[:, 0:256], LT[:], Wl[:, d * 256:(d + 1) * 256], True, True),
                             (p1[:, 256:512], LT[:], Wl[:, 512 + d * 256:512 + (d + 1) * 256], True, True)], [LT, Wl], [p1])
                    self.TT("dve", F["logw"][:], p1[:, 0:256], w0[:, d, :], ALU.add, [p1, w0], [F["logw"]])
                    self.ACT(F["logw"][:], F["logw"][:], AF.Sigmoid, [F["logw"]], [F["logw"]])
                    self.TSM("dve", F["logw"][:], F["logw"][:], -EM05, [F["logw"]], [F["logw"]])
                    self.TT("dve", F["a"][:], p1[:, 256:512], a0[:, d, :], ALU.add, [p1, a0], [F["a"]])
                    self.ACT(F["a"][:], F["a"][:], AF.Sigmoid, [F["a"]], [F["a"]])
                    if d == 1:
                        p2 = pg.next()
                        self.MM([(p2[:, 0:256], LT[:], Wl[:, 1024:1280], True, True)], [LT, Wl], [p2])
                        self.CP("act", Fg[:], p2[:, 0:256], [p2], [Fg])
                        self.TT("pool", F["t1"][:], r_, k_, ALU.mult, [z], [F["t1"]])
                        self.TT("pool", F["t1"][:], F["t1"][:], rk[:], ALU.mult, [F["t1"], rk], [F["t1"]])
                        P.op("dve", lambda e: e.tensor_reduce(out=s4[:], in_=F["t1"][:].rearrange("p (h f) -> p h f", f=64), axis=AX.X, op=ALU.add), r=[F["t1"]], w=[s4])
                        for h in range(4):
                            self.TSM("pool", Fbon[:, h * 64:(h + 1) * 64], z[:, 512 + h * 64:512 + (h + 1) * 64], s4[:, h:h + 1], [z, s4], [Fbon])
                    self.TT("pool", F["kk"][:], k_, kks[:], ALU.mult, [z, kks], [F["kk"]])
                    self.TT("pool", F["t2"][:], F["kk"][:], F["kk"][:], ALU.mult, [F["kk"]], [F["t2"]])
                    P.op("dve", lambda e: e.tensor_reduce(out=ss[:], in_=F["t2"][:].rearrange("p (h f) -> p h f", f=64), axis=AX.X, op=ALU.add), r=[F["t2"]], w=[ss])
                    self.ACT(ss[:], ss[:], AF.Sqrt, [ss, eps12], [ss], bias=eps12[:, 0:1])
                    P.op("dve", lambda e: e.reciprocal(out=ss[:], in_=ss[:]), r=[ss], w=[ss])
                    for h in range(4):
                        self.TSM("pool", F["kk"][:, h * 64:(h + 1) * 64], F["kk"][:, h * 64:(h + 1) * 64], ss[:, h:h + 1], [F["kk"], ss], [F["kk"]])
                    self.STT(F["t2"][:], F["a"][:], -1.0, ka[:], ALU.add, ALU.mult, [F["a"], ka], [F["t2"]])
                    self.STT(F["kd"][:], F["t2"][:], 1.0, k_, ALU.add, ALU.mult, [F["t2"], z], [F["kd"]])
                    p3 = pg.next()
                    self.MM([(p3[:, 0:256], self.tri[:, d, :], F["logw"][:], True, True)], [self.tri, F["logw"]], [p3])
                    self.CP("act", F["cum"][:], p3[:, 0:256], [p3], [F["cum"]])
                    self.ACT(F["gam"][:], F["cum"][:], AF.Exp, [F["cum"]], [F["gam"]])
                    self.ACT(F["ginv"][:], F["cum"][:], AF.Exp, [F["cum"]], [F["ginv"]], scale=-1.0)
                    self.TT("dve", F["game"][:], F["cum"][:], F["logw"][:], ALU.subtract, [F["cum"], F["logw"]], [F["game"]])
                    self.ACT(F["game"][:], F["game"][:], AF.Exp, [F["game"]], [F["game"]])
                    x4 = X4.next(); vb = Vb.next()
                    self.STT(x4[:, 0, :], F["kk"][:], -1.0, F["game"][:], ALU.mult, ALU.mult, [F["kk"], F["game"]], [x4])
                    self.TT("pool", x4[:, 1, :], r_, F["gam"][:], ALU.mult, [z, F["gam"]], [x4])
                    self.TT("dve", F["t2"][:], F["kk"][:], F["a"][:], ALU.mult, [F["kk"], F["a"]], [F["t2"]])
                    self.TT("dve", x4[:, 2, :], F["t2"][:], F["ginv"][:], ALU.mult, [F["t2"], F["ginv"]], [x4])
                    self.TT("pool", x4[:, 3, :], F["kd"][:], F["ginv"][:], ALU.mult, [F["kd"], F["ginv"]], [x4])
                    self.CP("act", vb[:], v_, [z], [vb])
                    for pr in range(2):
                        p4 = pg.next()
                        self.MM([(p4[:, 0:1], F["logw"][:, pr * 128:(pr + 1) * 128], self.ones[:, 0:1], True, True)], [F["logw"], self.ones], [p4])
                        self.ACT(gcol[pr][:], p4[:, 0:1], AF.Exp, [p4], [gcol[pr]])
                    return dict(x4=x4, vb=vb, Fg=Fg, Fbon=Fbon, gcol=gcol)

                def inv(t, H):
                    x4 = H["x4"]
                    tt = TTt.next(); zp = ZP.next()
                    RbT = RbTr.next(); SK = SKr.next()
                    H["tt"] = tt; H["RbT"] = RbT; H["SK"] = SK
                    def tr(e, x4=x4):
                        for var in range(4):
                            for pr in range(2):
                                i = e.transpose(out=pT[:, var * 2 + pr, :], in_=x4[:, var, pr * 128:(pr + 1) * 128], identity=self.identb[:])
                        return i
                    P.op("pe", tr, r=[x4, self.identb], w=[pT])
                    self.CP("dve", tt[:], pT[:], [pT], [tt])
                    for var in range(2):
                        self.CP("dve", zp[0:64, 0:4:2, var, :], pT[0:64, var * 2:var * 2 + 2, :], [pT], [zp])
                        self.CP("dve", zp[64:128, 1:4:2, var, :], pT[64:128, var * 2:var * 2 + 2, :], [pT], [zp])
                    bv = lambda bk: bk[:].rearrange("p (a b) -> p a b", b=128)
                    bc4 = lambda ap_: ap_.unsqueeze(1).to_broadcast([128, 4, 128])
                    hc = lambda h: slice(h * 128, (h + 1) * 128)
                    pA = pi.next(); pN = pi.next()
                    self.MM([(pA[:, hc(h)], tt[:, 4 + h // 2, :], zp[:, h, 0, :], True, True) for h in range(4)], [tt, zp], [pA])
                    self.MM([(pN[:, hc(h)], zp[:, h, 0, :], tt[:, 4 + h // 2, :], True, True) for h in range(4)], [tt, zp], [pN])
                    self.TT("dve", AdT4[:], bv(pA), bc4(rwm[:, d, 0, :]), ALU.mult, [pA, rwm], [AdT4])
                    self.TT("dve", AdN4[:], bv(pN), bc4(rwm[:, d, 1, :]), ALU.mult, [pN, rwm], [AdN4])
                    for li in range(3):
                        self.TT("dve", LoT4[li][:], bv(pA), bc4(rwm[:, d, 2 + li, :]), ALU.mult, [pA, rwm], [LoT4[li]])
                        self.TT("dve", LoN4[li][:], bv(pN), bc4(rwm[:, d, 5 + li, :]), ALU.mult, [pN, rwm], [LoN4[li]])
                    pB = pi.next()
                    self.MM([(pB[:, hc(h)], tt[:, 4 + h // 2, :], zp[:, h, 1, :], True, True) for h in range(4)], [tt, zp], [pB])
                    self.TT("dve", RbT[:], bv(pB), bc4(cmask[:, d, 128:256]), ALU.mult, [pB, cmask], [RbT])
                    pC = pi.next()
                    self.MM([(pC[:, hc(h)], tt[:, 6 + h // 2, :], zp[:, h, 0, :], True, True) for h in range(4)], [tt, zp], [pC])
                    self.TT("dve", SK[:, 0], bv(pC), bc4(cmask[:, d, 0:128]), ALU.mult, [pC, cmask], [SK])
                    pD = pi.next()
                    self.MM([(pD[:, hc(h)], tt[:, 6 + h // 2, :], zp[:, h, 1, :], True, True) for h in range(4)], [tt, zp], [pD])
                    self.TT("dve", SK[:, 1], bv(pD), bc4(cmask[:, d, 128:256]), ALU.mult, [pD, cmask], [SK])
                    X = AdN4; XT = AdT4
                    Q = RQ.next(); QT = RQT.next()
                    self.TT("dve", Q[:], AdN4[:], bc4(self.identb[:]), ALU.add, [AdN4, self.identb], [Q])
                    self.TT("dve", QT[:], AdT4[:], bc4(self.identb[:]), ALU.add, [AdT4, self.identb], [QT])
                    for k in range(4):
                        if k < 3:
                            pXT = pi.next(); pX = pi.next()
                            self.MM([(pXT[:, hc(h)], X[:, h, :], XT[:, h, :], True, True) for h in range(4)], [X, XT], [pXT])
                            self.MM([(pX[:, hc(h)], XT[:, h, :], X[:, h, :], True, True) for h in range(4)], [X, XT], [pX])
                        if k >= 1:
                            pQT = pi.next(); pQ = pi.next()
                            self.MM([(pQT[:, hc(h)], X[:, h, :], QT[:, h, :], True, True) for h in range(4)], [X, QT], [pQT])
                            self.MM([(pQ[:, hc(h)], XT[:, h, :], Q[:, h, :], True, True) for h in range(4)], [XT, Q], [pQ])
                        if k < 3:
                            nXT = RXT.next(); nX = RX.next()
                            self.CP("act", nXT[:], bv(pXT), [pXT], [nXT])
                            self.CP("act", nX[:], bv(pX), [pX], [nX])
                        if k >= 1:
                            nQT = RQT.next(); nQ = RQ.next()
                            self.TT("dve", nQT[:], bv(pQT), QT[:], ALU.add, [pQT, QT], [nQT])
                            self.TT("dve", nQ[:], bv(pQ), Q[:], ALU.add, [pQ, Q], [nQ])
                            Q = nQ; QT = nQT
                        if k < 3:
                            X = nX; XT = nXT
                    for li in range(3):
                        last = li == 2
                        pE2 = pi.next()
                        self.MM([(pE2[:, hc(h)], LoN4[li][:, h, :], QT[:, h, :], True, True) for h in range(4)], [LoN4[li], QT], [pE2])
                        e2 = RE2.next()
                        self.CP("act", e2[:], bv(pE2), [pE2], [e2])
                        if not last:
                            pE = pi.next()
                            self.MM([(pE[:, hc(h)], LoT4[li][:, h, :], Q[:, h, :], True, True) for h in range(4)], [LoT4[li], Q], [pE])
                            e1 = RE.next()
                            self.CP("act", e1[:], bv(pE), [pE], [e1])
                        pD2 = pi.next()
                        self.MM([(pD2[:, hc(h)], Q[:, h, :], e2[:, h, :], True, True) for h in range(4)], [Q, e2], [pD2])
                        nQT = PTr.next() if last else RQT.next()
                        self.TT("dve", nQT[:], bv(pD2), QT[:], ALU.add, [pD2, QT], [nQT])
                        if not last:
                            pDD = pi.next()
                            self.MM([(pDD[:, hc(h)], QT[:, h, :], e1[:, h, :], True, True) for h in range(4)], [QT, e1], [pDD])
                            nQ = RQ.next()
                            self.TT("dve", nQ[:], bv(pDD), Q[:], ALU.add, [pDD, Q], [nQ])
                            Q = nQ
                        QT = nQT
                    H["QT"] = QT

                def seq(t, H):
                    x4 = H["x4"]; vb = H["vb"]; tt = H["tt"]; Fg = H["Fg"]; Fbon = H["Fbon"]; gcol = H["gcol"]
                    RbT = H["RbT"]; SK = H["SK"]; QT = H["QT"]
                    for pr in range(2):
                        pqx = pq.next()
                        hs = [pr * 2, pr * 2 + 1]
                        sp = [(pqx[:, 0:128], tt[:, 0 + pr, :], Hb[pr][:], True, False)]
                        for hh, h in enumerate(hs):
                            sp.append((pqx[:, hh * 64:(hh + 1) * 64], SK[:, 0, h, :], vb[:, h * 64:(h + 1) * 64], False, hh == 1))
                        self.MM(sp, [tt, Hb[pr], vb, SK], [pqx])
                        self.CP("act", Wb[pr][:], pqx[:, 0:128], [pqx], [Wb[pr]])
                        sp = []
                        for hh, h in enumerate(hs):
                            sp.append((pqx[:, 128 + hh * 64:128 + (hh + 1) * 64], QT[:, h, :], Wb[pr][:, hh * 64:(hh + 1) * 64], True, True))
                        self.MM(sp, [Wb[pr], QT], [pqx])
                        self.CP("dve", Ub[pr][:], pqx[:, 128:256], [pqx], [Ub[pr]])
                        sp = [(pqx[:, 256:384], tt[:, 2 + pr, :], Hb[pr][:], True, False)]
                        for hh, h in enumerate(hs):
                            cs = slice(256 + hh * 64, 256 + (hh + 1) * 64)
                            sp.append((pqx[:, cs], RbT[:, h, :], Ub[pr][:, hh * 64:(hh + 1) * 64], False, False))
                            sp.append((pqx[:, cs], SK[:, 1, h, :], vb[:, h * 64:(h + 1) * 64], False, hh == 1))
                        self.MM(sp, [tt, Hb[pr], Ub[pr], vb, RbT, SK], [pqx])
                        dst = hacc[t][:, pr * 128:(pr + 1) * 128]
                        if d == 0:
                            self.CP("act", dst, pqx[:, 256:384], [pqx], [hacc[t]])
                        else:
                            self.TT("dve", dst, pqx[:, 256:384], dst, ALU.add, [pqx, hacc[t]], [hacc[t]])
                        self.MM([(pqx[:, 384:512], x4[:, 2, pr * 128:(pr + 1) * 128], Ub[pr][:], True, False),
                                 (pqx[:, 384:512], x4[:, 3, pr * 128:(pr + 1) * 128], vb[:, pr * 128:(pr + 1) * 128], False, True)],
                                [x4, Ub[pr], vb], [pqx])
                        for hh in range(2):
                            blk = slice(hh * 64, (hh + 1) * 64); cb_ = slice(384 + hh * 64, 384 + (hh + 1) * 64)
                            self.TT("dve", htmp[blk, blk], pqx[blk, cb_], Hf[pr][blk, blk], ALU.add, [pqx, Hf[pr]], [htmp])
                            self.TSM("dve", Hf[pr][blk, blk], htmp[blk, blk], gcol[pr][blk, 0:1], [htmp, gcol[pr]], [Hf[pr]])
                        self.CP("act", Hb[pr][:], Hf[pr][:], [Hf[pr]], [Hb[pr]])
                    if d == 1:
                        self.finish_heads(ht, hacc[t], Fg[:], None, ng, nb, t, 768, extra=Fbon)
                order_ = self.order(d)
                n_ = len(order_)
                Hs = {}
                for step in range(n_ + 2):
                    if step < n_:
                        Hs[step] = prep(order_[step])
                    if 0 <= step - 1 < n_:
                        inv(order_[step - 1], Hs[step - 1])
                    if 0 <= step - 2 < n_:
                        seq(order_[step - 2], Hs.pop(step - 2))
            P.barrier()

    def mix_mlstm(self, s, l):
        P, S, I = self.P, self.S, self.I
        with ExitStack() as st:
            ng = self.bc(st, I["out_norm_g"][l, 0:256], 256); nb = self.bc(st, I["out_norm_b"][l, 0:256], 256)
            cw = self.sbt(st, "cw", [128, 3, 512]); cb = self.bc(st, I["mlstm_conv_b"][l], 512)
            for tap in range(3):
                P.dma(cw[:, tap, :], I["mlstm_conv_w"][l, tap].partition_broadcast(128), w=[cw])
            gb = self.bc(st, I["mlstm_gate_b"][l].rearrange("a b -> (a b)"), 16)
            tri4 = self.sbt(st, "tri4", [128, 4, 4, 128])
            P.dma(tri4[:], I["k_tri4"], w=[tri4])
            hacc = [self.sbt(st, "hacc", [128, 256]) for _ in range(NT)]
            ht = self.head_tiles(st)
            q3 = Rot([self.sbt(st, "q3", [128, 3, 512]) for _ in range(2)])
            rest = Rot([self.sbt(st, "rest", [128, 528]) for _ in range(2)])
            acc = self.sbt(st, "acc", [128, 512]); acc2 = self.sbt(st, "acc2", [128, 512])
            qkb = Rot([self.sbt(st, "qkb", [128, 512], BF16) for _ in range(2)])
            va = Rot([self.sbt(st, "va", [128, 4, 65], BF16) for _ in range(2)])
            TT = Rot([self.sbt(st, "TT", [128, 4, 128], BF16) for _ in range(2)])
            AM = Rot([self.sbt(st, "AM", [128, 4, 128], BF16) for _ in range(2)])
            TQ = Rot([self.sbt(st, "TQ", [128, 4, 128], BF16) for _ in range(2)])
            for tq_ in TQ.items:
                P.op("pool", lambda e, tq_=tq_: e.memset(tq_[:], 0.0), w=[tq_])
            G = self.sbt(st, "G", [128, 16]); lf = self.sbt(st, "lf", [128, 4])
            EE = Rot([self.sbt(st, "ee", [128, 4]) for _ in range(2)]); SC8 = Rot([self.sbt(st, "sc8", [128, 8]) for _ in range(2)]); ZS = Rot([self.sbt(st, "zs", [128, 4]) for _ in range(2)])
            Z = self.sbt(st, "Z", [128, 4]); rZ = self.sbt(st, "rZ", [128, 4]); wk = self.sbt(st, "wk", [128, 4]); wi = self.sbt(st, "wi", [128, 4])
            thr = self.sbt(st, "thr", [128, 4]); Em = self.sbt(st, "Em", [128, 4]); dm = self.sbt(st, "dm", [128, 4]); hsc = self.sbt(st, "hsc", [128, 256])
            CN = [self.sbt(st, "CN", [128, 130]) for _ in range(2)]
            CNw = [self.sbt(st, "CNw", [128, 130]) for _ in range(2)]
            CNb = [Rot([self.sbt(st, "CNb", [128, 130], BF16) for _ in range(2)]) for _ in range(2)]
            ptt = Rot([self.pst(st, "ptt", [128, 8, 128], BF16) for _ in range(1)])
            psc = Rot([self.bank(st, "psc") for _ in range(2)])
            pO = Rot([self.bank(st, "pO") for _ in range(2)])
            pSt = Rot([self.bank(st, "pSt") for _ in range(1)])
            pg = Rot([self.bank(st, "pg") for _ in range(2)])
            for d in range(2):
                for pr in range(2):
                    P.op("pool", lambda e, pr=pr: e.memset(CN[pr][:], 0.0), w=[CN[pr]])
                    P.op("pool", lambda e, pr=pr: e.memset(CNw[pr][:], 0.0), w=[CNw[pr]])
                P.op("pool", lambda e: e.memset(Em[:], 1.0), w=[Em])
                def prep(t):
                    q = q3.next(); rs = rest.next(); qk = qkb.next()
                    sc8 = SC8.next(); ee = EE.next(); zs = ZS.next()
                    rb = rowbase(t)
                    for tap in range(3):
                        P.dma(q[:, tap, :], S["proj"][rb - 1 + tap:rb - 1 + tap + 128, 0:512], w=[q])
                    P.dma(rs[:], S["proj"][rb:rb + 128, 512:1040], w=[rs])
                    P.op("dve", lambda e, q=q: e.tensor_tensor(out=acc[:], in0=q[:, 0, :], in1=cw[:, 0, :], op=ALU.mult), r=[q, cw], w=[acc])
                    P.op("pool", lambda e, q=q: e.tensor_tensor(out=acc2[:], in0=q[:, 1, :], in1=cw[:, 1, :], op=ALU.mult), r=[q, cw], w=[acc2])
                    P.op("dve", lambda e: e.tensor_tensor(out=acc[:], in0=acc[:], in1=acc2[:], op=ALU.add), r=[acc, acc2], w=[acc])
                    P.op("pool", lambda e, q=q: e.tensor_tensor(out=acc2[:], in0=q[:, 2, :], in1=cw[:, 2, :], op=ALU.mult), r=[q, cw], w=[acc2])
                    P.op("dve", lambda e: e.tensor_tensor(out=acc[:], in0=acc[:], in1=acc2[:], op=ALU.add), r=[acc, acc2], w=[acc])
                    P.op("dve", lambda e: e.tensor_tensor(out=acc[:], in0=acc[:], in1=cb[:], op=ALU.add), r=[acc, cb], w=[acc])
                    P.op("act", lambda e: e.activation(out=acc[:], in_=acc[:], func=AF.Silu), r=[acc], w=[acc])
                    P.op("act", lambda e, qk=qk: e.mul(out=qk[:, 0:256], in_=acc[:, 0:256], mul=0.125), r=[acc], w=[qk])
                    P.op("dve", lambda e, qk=qk: e.tensor_copy(out=qk[:, 256:512], in_=acc[:, 256:512]), r=[acc], w=[qk])
                    P.op("dve", lambda e, rs=rs: e.tensor_tensor(out=G[:], in0=rs[:, 512:528], in1=gb[:], op=ALU.add), r=[rs, gb], w=[G])
                    igs = G[:, d * 8:d * 8 + 4]; fps = G[:, d * 8 + 4:d * 8 + 8]
                    P.op("act", lambda e, fps=fps: e.activation(out=lf[:], in_=fps, func=AF.Exp, scale=-1.0), r=[G], w=[lf])
                    P.op("act", lambda e: e.activation(out=lf[:], in_=lf[:], func=AF.Ln, bias=self.ones[:, 0:1], scale=1.0), r=[lf, self.ones], w=[lf])
                    P.op("dve", lambda e: e.tensor_scalar_mul(out=lf[:], in0=lf[:], scalar1=-1.0), r=[lf], w=[lf])
                    g1 = pg.next()

                    def gm(e, g1=g1, d=d):
                        e.matmul(g1[:, 0:4], lhsT=self.tri[:, d, :], rhs=lf[:], start=True, stop=True)
                        return e.matmul(g1[:, 4:8], lhsT=self.ones[:], rhs=lf[:], start=True, stop=True)
                    P.op("pe", gm, r=[lf, self.tri, self.ones], w=[g1])
                    P.op("dve", lambda e, g1=g1: e.tensor_copy(out=sc8[:], in_=g1[:, 0:8]), r=[g1], w=[sc8])
                    P.op("dve", lambda e, igs=igs: e.tensor_tensor(out=ee[:], in0=igs, in1=sc8[:, 0:4], op=ALU.subtract), r=[G, sc8], w=[ee])
                    P.op("act", lambda e: e.activation(out=ee[:], in_=ee[:], func=AF.Exp), r=[ee], w=[ee])
                    g2 = pg.next()
                    P.op("pe", lambda e, g2=g2: e.matmul(g2[:, 0:4], lhsT=self.ones[:], rhs=ee[:], start=True, stop=True), r=[ee, self.ones], w=[g2])
                    P.op("act", lambda e, g2=g2: e.copy(out=zs[:], in_=g2[:, 0:4]), r=[g2], w=[zs])
                    return dict(rs=rs, qk=qk, sc8=sc8, ee=ee, zs=zs)

                def chain(t, H):
                    rs = H["rs"]; qk = H["qk"]; sc8 = H["sc8"]; ee = H["ee"]; zs = H["zs"]
                    v = va.next(); tt = TT.next(); am = AM.next()
                    P.op("dve", lambda e: e.tensor_tensor(out=Z[:], in0=zs[:], in1=Em[:], op=ALU.add), r=[zs, Em], w=[Z])
                    P.op("dve", lambda e: e.reciprocal(out=rZ[:], in_=Z[:]), r=[Z], w=[rZ])
                    P.op("dve", lambda e: e.tensor_tensor(out=wk[:], in0=ee[:], in1=rZ[:], op=ALU.mult), r=[ee, rZ], w=[wk])
                    P.op("dve", lambda e: e.tensor_tensor(out=wi[:], in0=Em[:], in1=rZ[:], op=ALU.mult), r=[Em, rZ], w=[wi])
                    P.op("act", lambda e: e.activation(out=thr[:], in_=sc8[:, 0:4], func=AF.Exp, scale=-1.0), r=[sc8], w=[thr])
                    P.op("dve", lambda e: e.tensor_tensor(out=thr[:], in0=thr[:], in1=rZ[:], op=ALU.mult), r=[thr, rZ], w=[thr])
                    P.op("act", lambda e: e.activation(out=Em[:], in_=sc8[:, 4:8], func=AF.Exp), r=[sc8, wi], w=[Em])
                    P.op("dve", lambda e: e.tensor_tensor(out=Em[:], in0=Em[:], in1=Z[:], op=ALU.mult), r=[Em, Z], w=[Em])
                    for h in range(4):
                        P.op("pool", lambda e, rs=rs, v=v, h=h: e.tensor_scalar_mul(out=v[:, h, 0:64], in0=rs[:, h * 64:(h + 1) * 64], scalar1=wk[:, h:h + 1]), r=[rs, wk], w=[v])
                    P.op("pool", lambda e, v=v: e.tensor_copy(out=v[:, :, 64], in_=wk[:]), r=[wk], w=[v])
                    ptb = ptt.next(); pt = ptb[:, 0:4, :]

                    def tr(e, qk=qk, pt=pt):
                        for k in range(4):
                            i = e.transpose(out=pt[:, k, :], in_=qk[:, k * 128:(k + 1) * 128], identity=self.identb[:])
                        return i
                    P.op("pe", tr, r=[qk, self.identb], w=[ptb])
                    P.op("dve", lambda e, pt=pt, tt=tt: e.tensor_copy(out=tt[:], in_=pt), r=[ptb], w=[tt])
                    tq = TQ.next()
                    P.op("dve", lambda e, pt=pt, tq=tq: e.tensor_copy(out=tq[0:64, 0:4:2, :], in_=pt[0:64, 0:2, :]), r=[ptb], w=[tq])
                    P.op("dve", lambda e, pt=pt, tq=tq: e.tensor_copy(out=tq[64:128, 1:4:2, :], in_=pt[64:128, 0:2, :]), r=[ptb], w=[tq])
                    psb = psc.next(); ps = psb[:].rearrange("p (a b) -> p a b", b=128)

                    def sc(e, tt=tt, ps=ps, tq=tq):
                        for h in range(4):
                            i = e.matmul(ps[:, h, :], lhsT=tt[:, 2 + h // 2, :], rhs=tq[:, h, :], start=True, stop=True)
                        return i
                    P.op("pe", sc, r=[tt, tq], w=[psb])
                    P.op("dve", lambda e, ps=ps, am=am, d=d: e.tensor_tensor(out=am[:], in0=ps, in1=tri4[:, d], op=ALU.mult), r=[psb, tri4], w=[am])
                    po = pO.next()
                    for pr in range(2):
                        pS = pSt.next(); cb_ = CNb[pr].next()
                        c0 = pr * 130
                        for hh in range(2):
                            h = pr * 2 + hh
                            blk = slice(hh * 64, (hh + 1) * 64); cblk = slice(hh * 65, (hh + 1) * 65)
                            P.op("dve", lambda e, pr=pr, blk=blk, cblk=cblk, h=h: e.tensor_scalar_mul(out=CNw[pr][blk, cblk], in0=CN[pr][blk, cblk], scalar1=wi[blk, h:h + 1]),
                                 r=[CN[pr], wi], w=[CNw[pr]])
                        P.op("act", lambda e, pr=pr, cb_=cb_: e.copy(out=cb_[:], in_=CNw[pr][:]), r=[CNw[pr]], w=[cb_])

                        def mo(e, tt=tt, am=am, v=v, po=po, pr=pr, cb_=cb_, c0=c0):
                            e.matmul(po[:, c0:c0 + 130], lhsT=tt[:, pr, :], rhs=cb_[:], start=True, stop=False)
                            for hh in range(2):
                                h = pr * 2 + hh
                                i = e.matmul(po[:, c0 + hh * 65:c0 + (hh + 1) * 65], lhsT=am[:, h, :], rhs=v[:, h, :], start=False, stop=(hh == 1))
                            return i
                        P.op("pe", mo, r=[tt, am, v, cb_], w=[(po, pr)])
                        P.op("pe", lambda e, qk=qk, v=v, pS=pS, pr=pr: e.matmul(pS[:, 0:130], lhsT=qk[:, 256 + pr * 128:256 + (pr + 1) * 128],
                                                                                 rhs=v[:, pr * 2:pr * 2 + 2, :].rearrange("p a b -> p (a b)"), start=True, stop=True), r=[qk, v], w=[pS])
                        for hh in range(2):
                            blk = slice(hh * 64, (hh + 1) * 64); cblk = slice(hh * 65, (hh + 1) * 65)
                            P.op("dve", lambda e, pS=pS, pr=pr, blk=blk, cblk=cblk: e.tensor_tensor(out=CN[pr][blk, cblk], in0=CNw[pr][blk, cblk], in1=pS[blk, cblk], op=ALU.add),
                                 r=[pS, CNw[pr]], w=[CN[pr]])
                    pov = po[:, 0:260].rearrange("p (a b) -> p a b", b=65)
                    pok = [(po, 0), (po, 1)]
                    P.op("act", lambda e, pov=pov: e.activation(out=dm[:], in_=pov[:, :, 64], func=AF.Abs), r=pok, w=[dm])
                    P.op("dve", lambda e: e.tensor_tensor(out=dm[:], in0=dm[:], in1=thr[:], op=ALU.max), r=[dm, thr], w=[dm])
                    P.op("dve", lambda e: e.reciprocal(out=dm[:], in_=dm[:]), r=[dm], w=[dm])
                    dmb = dm[:].unsqueeze(2).to_broadcast([128, 4, 64])
                    hv = hacc[t][:].rearrange("p (h f) -> p h f", f=64)
                    if d == 0:
                        P.op("dve", lambda e, pov=pov, hv=hv, dmb=dmb: e.tensor_tensor(out=hv, in0=pov[:, :, 0:64], in1=dmb, op=ALU.mult), r=pok + [dm], w=[hacc[t]])
                    else:
                        P.op("dve", lambda e, pov=pov, dmb=dmb: e.tensor_tensor(out=hsc[:].rearrange("p (h f) -> p h f", f=64), in0=pov[:, :, 0:64], in1=dmb, op=ALU.mult), r=pok + [dm], w=[hsc])
                        P.op("pool", lambda e, t=t: e.tensor_tensor(out=hacc[t][:], in0=hacc[t][:], in1=hsc[:], op=ALU.add), r=[hacc[t], hsc], w=[hacc[t]])
                    if d == 1:
                        gate = rs[:, 256:512]
                        self.finish_heads(ht, hacc[t], gate, AF.Sigmoid, ng, nb, t, 0)
                order_ = self.order(d)
                hn_ = prep(order_[0])
                for i_, t in enumerate(order_):
                    hc_ = hn_
                    if i_ + 1 < len(order_):
                        hn_ = prep(order_[i_ + 1])
                    chain(t, hc_)
            P.barrier()

    def mix_retention(self, s, l):
        P, S, I = self.P, self.S, self.I
        C0 = 1808
        with ExitStack() as st:
            ng = self.bc(st, I["out_norm_g"][l, 512:768], 256); nb = self.bc(st, I["out_norm_b"][l, 512:768], 256)
            mask = self.sbt(st, "rtmask", [128, 2, 4, 128]); dec = self.sbt(st, "rtdec", [128, 24])
            P.dma(mask[:], I["k_rtmask"], w=[mask]); P.dma(dec[:], I["k_rtdec"], w=[dec])
            hacc = [self.sbt(st, "hacc", [128, 256]) for _ in range(NT)]
            ht = self.head_tiles(st)
            raw = Rot([self.sbt(st, "raw", [128, 1024]) for _ in range(2)])
            rope = Rot([self.sbt(st, "rope", [128, 2, 256]) for _ in range(2)])
            rt1 = self.sbt(st, "rt1", [128, 256]); rt2 = self.sbt(st, "rt2", [128, 256])
            qkb = Rot([self.sbt(st, "qkb", [128, 512], BF16) for _ in range(2)])
            vb = Rot([self.sbt(st, "vb", [128, 256], BF16) for _ in range(2)])
            vd = Rot([self.sbt(st, "vd", [128, 256], BF16) for _ in range(2)])
            TT = Rot([self.sbt(st, "TT", [128, 4, 128], BF16) for _ in range(2)])
            AM = Rot([self.sbt(st, "AM", [128, 4, 128], BF16) for _ in range(2)])
            TQ = Rot([self.sbt(st, "TQ", [128, 4, 128], BF16) for _ in range(2)])
            for tq_ in TQ.items:
                P.op("pool", lambda e, tq_=tq_: e.memset(tq_[:], 0.0), w=[tq_])
            o1 = Rot([self.sbt(st, "o1", [128, 128]) for _ in range(2)])
            Sf = [self.sbt(st, "Sf", [128, 128]) for _ in range(2)]
            Sb = [Rot([self.sbt(st, "Sb", [128, 128], BF16) for _ in range(2)]) for _ in range(2)]
            ptt = Rot([self.pst(st, "ptt", [128, 8, 128], BF16) for _ in range(1)])
            psc = Rot([self.bank(st, "psc") for _ in range(2)])
            pO1 = Rot([self.bank(st, "pO1") for _ in range(2)])
            pO2 = Rot([self.bank(st, "pO2") for _ in range(2)])
            pSt = Rot([self.bank(st, "pSt") for _ in range(1)])
            for d in range(2):
                sbc = []
                for pr in range(2):
                    P.op("pool", lambda e, pr=pr: e.memset(Sf[pr][:], 0.0), w=[Sf[pr]])
                    b0 = Sb[pr].next()
                    P.op("pool", lambda e, b0=b0: e.memset(b0[:], 0.0), w=[b0])
                    sbc.append(b0)
                def prep(t):
                    rw = raw.next(); qk = qkb.next(); v = vb.next(); vdd = vd.next()
                    P.dma(rw[:], S["proj"][rowbase(t):rowbase(t) + 128, C0:C0 + 1024], w=[rw])
                    if t >= 2:
                        rp = rope.next()
                        P.dma(rp[:], I["k_rope"][(t - 2) * 128:(t - 1) * 128], w=[rp])
                        for qi in range(2):
                            src = rw[:, qi * 256:(qi + 1) * 256]
                            sv = src.rearrange("p (a two f) -> p a two f", two=2, f=16)
                            snv = rp[:, 1, :].rearrange("p (a two f) -> p a two f", two=2, f=16)
                            t1v = rt1[:].rearrange("p (a two f) -> p a two f", two=2, f=16)
                            P.op("pool", lambda e, sv=sv, snv=snv, t1v=t1v: e.tensor_tensor(out=t1v[:, :, 0, :], in0=sv[:, :, 1, :], in1=snv[:, :, 0, :], op=ALU.mult), r=[rw, rp], w=[rt1])
                            P.op("pool", lambda e, sv=sv, snv=snv, t1v=t1v: e.tensor_tensor(out=t1v[:, :, 1, :], in0=sv[:, :, 0, :], in1=snv[:, :, 1, :], op=ALU.mult), r=[rw, rp], w=[rt1])
                            P.op("dve", lambda e, src=src, rp=rp: e.tensor_tensor(out=rt2[:], in0=src, in1=rp[:, 0, :], op=ALU.mult), r=[rw, rp], w=[rt2])
                            P.op("dve", lambda e: e.tensor_tensor(out=rt2[:], in0=rt2[:], in1=rt1[:], op=ALU.add), r=[rt1, rt2], w=[rt2])
                            P.op("act", lambda e, qi=qi, qk=qk: e.mul(out=qk[:, qi * 256:(qi + 1) * 256], in_=rt2[:], mul=(0.125 if qi == 0 else 1.0)), r=[rt2], w=[qk])
                    else:
                        P.op("act", lambda e, rw=rw, qk=qk: e.mul(out=qk[:, 0:256], in_=rw[:, 0:256], mul=0.125), r=[rw], w=[qk])
                        P.op("act", lambda e, rw=rw, qk=qk: e.copy(out=qk[:, 256:512], in_=rw[:, 256:512]), r=[rw], w=[qk])
                    P.op("dve", lambda e, rw=rw, v=v: e.tensor_copy(out=v[:], in_=rw[:, 512:768]), r=[rw], w=[v])
                    for h in range(4):
                        P.op("pool", lambda e, rw=rw, vdd=vdd, h=h, d=d: e.tensor_scalar_mul(out=vdd[:, h * 64:(h + 1) * 64], in0=rw[:, 512 + h * 64:512 + (h + 1) * 64],
                                                                                       scalar1=dec[:, 8 + d * 4 + h:8 + d * 4 + h + 1]), r=[rw, dec], w=[vdd])
                    return dict(rw=rw, qk=qk, v=v, vdd=vdd)

                def chain(t, H):
                    rw = H["rw"]; qk = H["qk"]; v = H["v"]; vdd = H["vdd"]
                    tt = TT.next(); am = AM.next()
                    ptb = ptt.next(); pt = ptb[:, 0:4, :]

                    def tr(e, qk=qk, pt=pt):
                        for k in range(4):
                            i = e.transpose(out=pt[:, k, :], in_=qk[:, k * 128:(k + 1) * 128], identity=self.identb[:])
                        return i
                    P.op("pe", tr, r=[qk, self.identb], w=[ptb])
                    P.op("dve", lambda e, pt=pt, tt=tt: e.tensor_copy(out=tt[:], in_=pt), r=[ptb], w=[tt])
                    tq = TQ.next()
                    P.op("dve", lambda e, pt=pt, tq=tq: e.tensor_copy(out=tq[0:64, 0:4:2, :], in_=pt[0:64, 0:2, :]), r=[ptb], w=[tq])
                    P.op("dve", lambda e, pt=pt, tq=tq: e.tensor_copy(out=tq[64:128, 1:4:2, :], in_=pt[64:128, 0:2, :]), r=[ptb], w=[tq])
                    psb = psc.next(); ps = psb[:].rearrange("p (a b) -> p a b", b=128)

                    def sc(e, tt=tt, ps=ps, tq=tq):
                        for h in range(4):
                            i = e.matmul(ps[:, h, :], lhsT=tt[:, 2 + h // 2, :], rhs=tq[:, h, :], start=True, stop=True)
                        return i
                    P.op("pe", sc, r=[tt, tq], w=[psb])
                    P.op("dve", lambda e, ps=ps, am=am, d=d: e.tensor_tensor(out=am[:], in0=ps, in1=mask[:, d], op=ALU.mult), r=[psb, mask], w=[am])
                    for pr in range(2):
                        p1 = pO1.next(); p2 = pO2.next(); pS = pSt.next(); oo = o1.next()
                        sb_old = sbc[pr]

                        def m1(e, am=am, v=v, p1=p1, pr=pr):
                            for hh in range(2):
                                h = pr * 2 + hh
                                i = e.matmul(p1[:, hh * 64:(hh + 1) * 64], lhsT=am[:, h, :], rhs=v[:, h * 64:(h + 1) * 64], start=True, stop=True)
                            return i
                        P.op("pe", m1, r=[am, v], w=[p1])
                        P.op("pe", lambda e, tt=tt, p2=p2, pr=pr, sb_old=sb_old: e.matmul(p2[:, 0:128], lhsT=tt[:, pr, :], rhs=sb_old[:], start=True, stop=True), r=[tt, sb_old], w=[p2])
                        P.op("pe", lambda e, qk=qk, vdd=vdd, pS=pS, pr=pr: e.matmul(pS[:, 0:128], lhsT=qk[:, 256 + pr * 128:256 + (pr + 1) * 128], rhs=vdd[:, pr * 128:(pr + 1) * 128], start=True, stop=True),
                             r=[qk, vdd], w=[pS])
                        P.op("act", lambda e, p1=p1, oo=oo: e.copy(out=oo[:], in_=p1[:, 0:128]), r=[p1], w=[oo])
                        for hh in range(2):
                            h = pr * 2 + hh
                            dst = hacc[t][:, h * 64:(h + 1) * 64]
                            P.op("dve", lambda e, p2=p2, oo=oo, hh=hh, h=h, d=d, dst=dst: e.scalar_tensor_tensor(out=(oo[:, hh * 64:(hh + 1) * 64] if d == 1 else dst), in0=p2[:, hh * 64:(hh + 1) * 64],
                                                                                                             scalar=dec[:, d * 4 + h:d * 4 + h + 1], in1=oo[:, hh * 64:(hh + 1) * 64], op0=ALU.mult, op1=ALU.add),
                                 r=[p2, oo, dec], w=[oo if d == 1 else hacc[t]])
                            if d == 1:
                                P.op("pool", lambda e, oo=oo, hh=hh, dst=dst: e.tensor_tensor(out=dst, in0=dst, in1=oo[:, hh * 64:(hh + 1) * 64], op=ALU.add), r=[oo, hacc[t]], w=[hacc[t]])
                            blk = slice(hh * 64, (hh + 1) * 64)
                            P.op("dve", lambda e, pS=pS, pr=pr, blk=blk, h=h, d=d: e.scalar_tensor_tensor(out=Sf[pr][blk, blk], in0=Sf[pr][blk, blk], scalar=dec[blk, 16 + d * 4 + h:16 + d * 4 + h + 1],
                                                                                                     in1=pS[blk, blk], op0=ALU.mult, op1=ALU.add), r=[pS, Sf[pr], dec], w=[Sf[pr]])
                        nb_ = Sb[pr].next()
                        P.op("act", lambda e, nb_=nb_, pr=pr: e.copy(out=nb_[:], in_=Sf[pr][:]), r=[Sf[pr]], w=[nb_])
                        sbc[pr] = nb_
                    if d == 1:
                        gate = rw[:, 768:1024]
                        self.finish_heads(ht, hacc[t], gate, AF.Silu, ng, nb, t, 512)
                for t in self.order(d):
                    chain(t, prep(t))
            P.barrier()

    def ln_mod(self, s, l, uT, kind):
        P = self.P
        with ExitStack() as st:
            stats = self.sbt(st, "stats", [128, 2, 6])
            mv = self.sbt(st, "mv", [128, 2])
            rstd = self.sbt(st, "rstd", [128, 1])
            utmp = self.sbt(st, "utmp", [128, 8, 128])
            xn = Rot([self.sbt(st, "xn", [128, D], BF16) for _ in range(2)])
            pt = Rot([self.pst(st, "ptr", [128, 8, 128], BF16) for _ in range(2)])
            for t in range(2 if (kind == 1 and l == self.last_layer) else 0, NT):
                r = 4 if t < 2 else s
                x = self.xres[t]
                a = xn.next(); p = pt.next()
                self.ln_stats(x, stats, mv, rstd)
                P.op("dve", lambda e, a=a, x=x: e.tensor_scalar(out=a[:], in0=x[:], scalar1=mv[:, 0:1], scalar2=rstd[:, 0:1],
                                                                op0=ALU.subtract, op1=ALU.mult), r=[x, mv, rstd], w=[a])

                def tr(e, a=a, p=p):
                    for k in range(8):
                        i = e.transpose(out=p[:, k, :], in_=a[:, k * 128:(k + 1) * 128], identity=self.identb[:])
                    return i
                P.op("pe", tr, r=[a, self.identb], w=[p])
                sc = self.modp[:, l, r, 2 * kind + 1, :].unsqueeze(2).to_broadcast([128, 8, 128])
                sh = self.modp[:, l, r, 2 * kind, :].unsqueeze(2).to_broadcast([128, 8, 128])
                dst = uT[:, :, t * 128:(t + 1) * 128]
                P.op("dve", lambda e, p=p, sc=sc: e.tensor_tensor(out=utmp[:], in0=p[:], in1=sc, op=ALU.mult), r=[p, self.modp], w=[utmp])
                P.op("dve", lambda e, sh=sh, dst=dst: e.tensor_tensor(out=dst, in0=utmp[:], in1=sh, op=ALU.add), r=[utmp, self.modp], w=[(uT, t)])
            P.barrier()

    def ln_stats(self, x, stats, mv, rstd, width=D):
        P = self.P
        ng = width // 512
        for g in range(ng):
            P.op("dve", lambda e, g=g: e.bn_stats(out=stats[:, g, :], in_=x[:, g * 512:(g + 1) * 512]), r=[x], w=[stats])
        P.op("dve", lambda e: e.bn_aggr(out=mv[:], in_=stats[:, 0:ng, :].rearrange("p g s -> p (g s)")), r=[stats], w=[mv])
        P.op("act", lambda e: e.activation(out=rstd[:], in_=mv[:, 1:2], func=AF.Sqrt, bias=self.epsc[:, 0:1], scale=1.0), r=[mv, self.epsc], w=[rstd])
        P.op("dve", lambda e: e.reciprocal(out=rstd[:], in_=rstd[:]), r=[rstd], w=[rstd])

    def in_proj(self, l, uT):
        P, S = self.P, self.S
        with ExitStack() as st:
            wch = Rot([self.sbt(st, "wch", [128, 8, 512], BF16) for _ in range(2)])
            stg = Rot([self.sbt(st, "stg", [128, 512]) for _ in range(4)])
            ps = Rot([self.pst(st, "pps", [128, 512]) for _ in range(4)])
            ev = Rot(["act", "act", "act", "dve"])
            for g in range(8):
                c0 = g * 512
                nco = min(512, DPROJ - c0)
                w = wch.next()
                P.dma(w[:, :, 0:nco], S["wb_in%d" % l][:, c0:c0 + nco].rearrange("(k p) n -> p k n", p=128), w=[w])
                for t in range(NT):
                    p = ps.next(); sg = stg.next()

                    def mm(e, p=p, w=w, t=t, nco=nco):
                        for k in range(8):
                            i = e.matmul(p[:, 0:nco], lhsT=uT[:, k, t * 128:(t + 1) * 128], rhs=w[:, k, 0:nco], start=(k == 0), stop=(k == 7))
                        return i
                    P.op("pe", mm, r=[w, (uT, t)], w=[p])
                    eve = ev.next()
                    if eve == "act":
                        P.op("act", lambda e, p=p, sg=sg, nco=nco: e.copy(out=sg[:, 0:nco], in_=p[:, 0:nco]), r=[p], w=[sg])
                    else:
                        P.op("dve", lambda e, p=p, sg=sg, nco=nco: e.tensor_copy(out=sg[:, 0:nco], in_=p[:, 0:nco]), r=[p], w=[sg])
                    P.dma(S["proj"][rowbase(t):rowbase(t) + 128, c0:c0 + nco], sg[:, 0:nco], r=[sg], eng="act")
            P.barrier()


_CACHE = {}


def kernel(**inputs):
    consts = make_consts()
    n_cores = 8
    if "nc" not in _CACHE:
        _CACHE["nc"] = Builder(n_seq=4).build(consts)
    nc = _CACHE["nc"]
    in_maps = []
    for cidx in range(n_cores):
        m = {}
        for k, v in inputs.items():
            v = np.asarray(v)
            if k in ("x", "c", "ctx"):
                m[k] = np.ascontiguousarray(v[cidx * 4:(cidx + 1) * 4])
            else:
                m[k] = np.ascontiguousarray(v)
        for k, v in consts.items():
            m["k_" + k] = v
        in_maps.append(m)
    res = run_bass_kernel_spmd(nc, in_maps, core_ids=list(range(n_cores)))
    return np.concatenate([r["out"] for r in res.results], axis=0).astype(np.float32)
```

```python
import math
import os
from contextlib import ExitStack
import numpy as np
import ml_dtypes
import concourse.bass as bass
import concourse.mybir as mybir
from concourse.bass_utils import run_bass_kernel_spmd

F32 = mybir.dt.float32
BF16 = mybir.dt.bfloat16
ALU = mybir.AluOpType
AF = mybir.ActivationFunctionType
AX = mybir.AxisListType

ENGS = ("pe", "act", "dve", "pool", "sp")

D = 1024
SEQ = 2048
CTX = 256
NT = 18
TOK = NT * 128
DPROJ = 3728
DFF = 2816
GW = 256
ALPHA = 4 ** 0.25
NROWS = 2307


def rowbase(t):
    return 1 + 128 * t if t < 2 else 2 + 128 * t


class Prog:
    N_DMA_SEMS = 24

    def __init__(self, nc):
        self.nc = nc
        self.q = {e: [] for e in ENGS}
        self.cnt = {e: 0 for e in ENGS}
        self.waited = {}
        self.lastw = {}
        self.reads = {}
        self.dma_uses = [0] * self.N_DMA_SEMS
        self.dma_rr = 0
        self.n_inst = 0
        self.epoch = {e: 0 for e in ENGS}

    @staticmethod
    def key(x):
        if isinstance(x, tuple):
            return (Prog.key(x[0]),) + tuple(x[1:])
        if isinstance(x, str):
            return x
        t = getattr(x, "tensor", x)
        return t.name

    def _need(self, eng, deps, waits):
        for (src, val) in deps:
            if self.waited.get((eng, src), 0) < val:
                self.waited[(eng, src)] = val
                waits.append((src, val))

    def _deps(self, eng, r, w):
        waits = []
        deps = []
        for k in r:
            k = self.key(k)
            if k in self.lastw:
                deps.append(self.lastw[k])
        for k in w:
            k = self.key(k)
            if k in self.lastw:
                deps.append(self.lastw[k])
            for s, v in self.reads.get(k, {}).items():
                deps.append((s, v))
        self._need(eng, deps, waits)
        return waits

    def _commit(self, src, val, r, w):
        for k in r:
            k = self.key(k)
            d = self.reads.setdefault(k, {})
            if d.get(src, 0) < val:
                d[src] = val
        for k in w:
            k = self.key(k)
            self.lastw[k] = (src, val)
            self.reads[k] = {}

    def op(self, eng, fn, r=(), w=()):
        waits = self._deps(eng, r, w)
        self.cnt[eng] += 1
        val = self.cnt[eng]
        src = ("eng", eng, self.epoch[eng])
        self.q[eng].append(("op", fn, waits, src))
        self._commit(src, val, r, w)
        self.n_inst += 1

    def dma(self, out, in_, r=(), w=(), eng=None, **kw):
        if eng is None:
            eng = "sp"
        waits = self._deps(eng, r, w)
        s = self.dma_rr
        self.dma_rr = (self.dma_rr + 1) % self.N_DMA_SEMS
        src = ("dma", s)
        prev = self.dma_uses[s] * 16
        if prev and self.waited.get((eng, src), 0) < prev:
            self.waited[(eng, src)] = prev
            waits.append((src, prev))
        self.dma_uses[s] += 1
        val = self.dma_uses[s] * 16
        self.q[eng].append(("dma", (out, in_, kw), waits, s))
        self._commit(src, val, r, w)
        self.n_inst += 1

    def barrier(self):
        for e in ENGS:
            waits = []
            deps = [(("eng", o, self.epoch[o]), self.cnt[o]) for o in ENGS if o != e and self.cnt[o] > 0]
            deps += [(("dma", s), self.dma_uses[s] * 16)
                     for s in range(self.N_DMA_SEMS) if self.dma_uses[s]]
            self._need(e, deps, waits)
            if waits:
                self.q[e].append(("wait", None, waits, None))
        self.lastw.clear()
        self.reads.clear()
        for e in ENGS:
            if self.cnt[e] > 1000000000:
                self.epoch[e] += 1
                self.cnt[e] = 0

    def emit(self, stack):
        nc = self.nc
        self.barrier()
        esem = {}
        for e in ENGS:
            for ep in range(self.epoch[e] + 1):
                esem[(e, ep)] = stack.enter_context(nc.semaphore("s_%s_%d" % (e, ep)))
        dsem = [stack.enter_context(nc.semaphore("d%d" % i)) for i in range(self.N_DMA_SEMS)]

        def semof(src):
            return dsem[src[1]] if src[0] == "dma" else esem[(src[1], src[2])]

        block = stack.enter_context(nc.Block())

        def run(e, engobj):
            for kind, fn, waits, s in self.q[e]:
                for (src, val) in waits:
                    engobj.wait_ge(semof(src), val)
                if kind == "op":
                    fn(engobj).then_inc(semof(s), 1)
                elif kind == "dma":
                    out, in_, kw = fn
                    engobj.dma_start(out=out, in_=in_, **kw).then_inc(dsem[s], 16)

        @block.tensor
        def _(eng):
            run("pe", eng)

        @block.scalar
        def _(eng):
            run("act", eng)

        @block.vector
        def _(eng):
            run("dve", eng)

        @block.gpsimd
        def _(eng):
            run("pool", eng)

        @block.sync
        def _(eng):
            run("sp", eng)


class Rot:
    def __init__(self, items):
        self.items = list(items)
        self.i = 0

    def next(self):
        x = self.items[self.i % len(self.items)]
        self.i += 1
        return x


def make_consts():
    c = {}
    c["ident"] = np.eye(128, dtype=np.float32)
    j = np.arange(128)[:, None]
    i = np.arange(128)[None, :]
    c["tri"] = np.stack([(j <= i), (j >= i), (j < i), (j > i)]).astype(np.float32)
    c["ones"] = np.ones((128, 128), np.float32)
    c["tri4"] = np.ascontiguousarray(np.broadcast_to(c["tri"].transpose(1, 0, 2)[:, :, None, :], (128, 4, 4, 128))).astype(np.float32)
    lg_f = np.log(1.0 - 2.0 ** (-5.0 - np.arange(4, dtype=np.float64)))
    lg = np.stack([lg_f, lg_f[::-1]])
    jj = np.arange(128, dtype=np.float64)
    rtmask = np.zeros((128, 2, 4, 128), np.float64)
    qdec = np.zeros((128, 8)); kdec = np.zeros((128, 8)); cdec = np.zeros((128, 8))
    for d in range(2):
        for h in range(4):
            g = lg[d, h]
            diff = (jj[None, :] - jj[:, None]) if d == 0 else (jj[:, None] - jj[None, :])
            rtmask[:, d, h, :] = np.where(diff >= 0, np.exp(np.maximum(diff, 0) * g), 0.0)
            pos = jj if d == 0 else 127 - jj
            qdec[:, d * 4 + h] = np.exp((pos + 1.0) * g)
            kdec[:, d * 4 + h] = np.exp((127.0 - pos) * g)
            cdec[:, d * 4 + h] = np.exp(128.0 * g)
    c["rtmask"] = rtmask.astype(np.float32)
    c["rtdec"] = np.concatenate([qdec, kdec, cdec], 1).astype(np.float32)
    freqs = 10000.0 ** (-np.arange(16, dtype=np.float32) / 16)
    row = np.repeat(np.arange(32, dtype=np.float32), 64); col = np.tile(np.arange(64, dtype=np.float32), 32)
    ar = (row[:, None] * freqs).astype(np.float32); ac = (col[:, None] * freqs).astype(np.float32)
    cos64 = np.concatenate([np.cos(ar), np.cos(ar), np.cos(ac), np.cos(ac)], 1)
    sin64 = np.concatenate([-np.sin(ar), np.sin(ar), -np.sin(ac), np.sin(ac)], 1)
    c["rope"] = np.stack([np.tile(cos64, (1, 4)), np.tile(sin64, (1, 4))], 1).astype(np.float32)
    def lo(n, rows_odd):
        same = (j // (2 * n)) == (i // (2 * n))
        return same & (((j // n) % 2) == (1 if rows_odd else 0)) & (((i // n) % 2) == (0 if rows_odd else 1))
    bd = (j // 16) == (i // 16)
    rwm = np.zeros((128, 2, 8, 128), np.float32)
    for d in range(2):
        strictT = (j < i) if d == 0 else (j > i)
        strictN = (i < j) if d == 0 else (i > j)
        rwm[:, d, 0, :] = bd & strictT
        rwm[:, d, 1, :] = bd & strictN
        for li, n in enumerate((16, 32, 64)):
            rwm[:, d, 2 + li, :] = lo(n, d == 1)
            rwm[:, d, 5 + li, :] = lo(n, d == 0)
    c["rwm"] = rwm.astype(ml_dtypes.bfloat16)
    for kind, L in (("lat", SEQ), ("ctx", CTX)):
        N = 2 * L
        nft = L // 128
        t = np.arange(L, dtype=np.int64)[:, None]
        f = np.arange(L, dtype=np.int64)[None, :]
        ang = 2.0 * np.pi * ((t * f) % N).astype(np.float64) / N
        sgn = (-1.0) ** np.arange(L)
        Fre = np.cos(ang); Fim = -np.sin(ang); Fim[:, 0] = sgn
        F = np.stack([Fre, Fim], 0).reshape(2, nft, 128, nft, 128).transpose(3, 2, 0, 1, 4)
        c["hF_" + kind] = np.ascontiguousarray(F).astype(ml_dtypes.bfloat16)
        Bre = (2.0 / N) * np.cos(ang.T); Bre[0, :] = 1.0 / N
        Bim = -(2.0 / N) * np.sin(ang.T); Bim[0, :] = sgn / N
        B = np.stack([Bre, Bim], 0).reshape(2, nft, 128, nft, 128).transpose(3, 2, 0, 1, 4)
        c["hB_" + kind] = np.ascontiguousarray(B).astype(ml_dtypes.bfloat16)
        pos = np.arange(L, dtype=np.float32)
        tl = np.linspace(0.0, 1.0, L, dtype=np.float32)[:, None]
        a2 = (2.0 * math.pi * pos[:, None] / L).astype(np.float32)
        bands = np.linspace(1e-4, 15.0, 16, dtype=np.float32)[None, :]
        feats = np.concatenate([tl, np.cos(bands * a2), -np.sin(bands * a2)], -1).astype(np.float32)
        c["hfeat_" + kind] = np.ascontiguousarray(feats.T)
        max_decay = math.log(1e-2) / 0.3; min_decay = math.log(1e-2) / 1.5
        deltas = np.abs(np.linspace(min_decay, max_decay, GW, dtype=np.float32))
        c["hwin_" + kind] = np.exp(-tl * deltas).astype(np.float32)
    return c


CONST_SPECS = None


class Builder:
    def __init__(self, n_seq=4, layers=(0, 1), dbg=False, stages=None):
        self.n_seq = n_seq
        self.layers = layers
        self.dbg = dbg
        self.stages = stages
        self.uid = 0
        self.last_layer = 1

    def nm(self, base):
        self.uid += 1
        return "%s_%d" % (base, self.uid)

    def build(self, consts):
        nc = bass.Bass("TRN2", target_bir_lowering=False)
        self.nc = nc
        self.P = Prog(nc)
        n_seq = self.n_seq
        I = {}

        def inp(name, shape, dt=F32):
            I[name] = nc.dram_tensor(name, list(shape), dt, kind="ExternalInput").ap()

        inp("x", [n_seq, SEQ, D]); inp("c", [n_seq, D]); inp("ctx", [n_seq, CTX, D]); inp("c_ctx", [D])
        inp("ada_w", [2, D, 6 * D]); inp("ada_b", [2, 6 * D]); inp("w_in", [2, D, DPROJ])
        inp("mlstm_conv_w", [2, 3, 512]); inp("mlstm_conv_b", [2, 512]); inp("mlstm_gate_b", [2, 4, 4])
        inp("hyena_conv_w", [2, 3, 768]); inp("hyena_conv_b", [2, 768]); inp("hyena_w1", [2, 33, 64])
        inp("hyena_b1", [2, 64]); inp("hyena_w2", [2, 64, 64]); inp("hyena_b2", [2, 64]); inp("hyena_w3", [2, 64, 1024])
        inp("hyena_freq", [2, 64]); inp("hyena_d", [2, 2, 256])
        inp("rwkv_mu", [2, 2, 896]); inp("rwkv_w0", [2, 2, 256]); inp("rwkv_w2", [2, 2, 32, 256])
        inp("rwkv_a0", [2, 2, 256]); inp("rwkv_a2", [2, 2, 32, 256]); inp("rwkv_g2", [2, 64, 256])
        inp("rwkv_kk", [2, 256]); inp("rwkv_ka", [2, 256]); inp("rwkv_rk", [2, 4, 64])
        inp("out_norm_g", [2, D]); inp("out_norm_b", [2, D]); inp("w_out", [2, D, D])
        inp("ln1_g", [2, D]); inp("ln1_b", [2, D]); inp("ffn_w_up", [2, D, 2 * DFF])
        inp("ffn_conv_w", [2, 3, DFF]); inp("ffn_conv_b", [2, DFF]); inp("ffn_w_down", [2, DFF, D])
        inp("ln2_g", [2, D]); inp("ln2_b", [2, D])
        for k, v in consts.items():
            inp("k_" + k, v.shape, BF16 if v.dtype == ml_dtypes.bfloat16 else F32)
        if self.dbg:
            inp("mix_in", [TOK, D])
        self.I = I
        self.out = nc.dram_tensor("out", [n_seq, SEQ, D], F32, kind="ExternalOutput").ap()

        def scr(name, shape, dt=F32):
            kind = "ExternalOutput" if (self.dbg and name in ("proj", "mix", "modv0", "xdump")) else None
            if kind:
                return nc.dram_tensor(name, list(shape), dt, kind=kind).ap()
            return nc.dram_tensor(name, list(shape), dt).ap()

        S = {}
        for l in (0, 1):
            S["wb_in%d" % l] = scr("wb_in%d" % l, [D, DPROJ], BF16)
            S["wb_out%d" % l] = scr("wb_out%d" % l, [D, D], BF16)
            S["wb_up%d" % l] = scr("wb_up%d" % l, [D, 2 * DFF], BF16)
            S["wb_dn%d" % l] = scr("wb_dn%d" % l, [DFF, D], BF16)
            S["modv%d" % l] = scr("modv%d" % l, [5, 6 * D])
        for l in (0, 1):
            S["HT%d_lat" % l] = scr("HT%d_lat" % l, [16, 128, 3, 512])
            S["HT%d_ctx" % l] = scr("HT%d_ctx" % l, [2, 128, 3, 512])
        S["proj"] = scr("proj", [NROWS, DPROJ])
        S["mix"] = scr("mix", [TOK, D])
        S["gT"] = scr("gT", [DFF, TOK], BF16)
        if self.dbg:
            S["xdump"] = scr("xdump", [TOK, D])
        self.S = S

        with ExitStack() as top:
            self.top = top
            self.alloc_globals(top)
            self.stage_weights()
            self.stage_mod()
            if self.on("hyena"):
                self.stage_hyena()
            for s in range(n_seq):
                self.run_sequence(s)
            self.P.emit(top)
        return nc

    def sbt(self, st, base, shape, dt=F32):
        return st.enter_context(self.nc.sbuf_tensor(self.nm(base), list(shape), dt))

    def pst(self, st, base, shape, dt=F32):
        return st.enter_context(self.nc.psum_tensor(self.nm(base), list(shape), dt))

    def bank(self, st, base, dt=F32):
        return self.pst(st, base, [128, 512 if dt == F32 else 1024], dt)

    def alloc_globals(self, st):
        P, I = self.P, self.I
        self.xres = [self.sbt(st, "xres", [128, D]) for _ in range(NT)]
        self.identf = self.sbt(st, "identf", [128, 128])
        self.identb = self.sbt(st, "identb", [128, 128], BF16)
        self.tri = self.sbt(st, "tri", [128, 4, 128])
        self.ones = self.sbt(st, "ones", [128, 128])
        self.epsc = self.sbt(st, "epsc", [128, 1])
        self.zero = self.sbt(st, "zero", [128, 512])
        self.modp = self.sbt(st, "modp", [128, 2, 5, 4, 8])
        P.dma(self.identf[:], I["k_ident"], w=[self.identf])
        P.dma(self.tri[:], I["k_tri"].rearrange("m j i -> j m i"), w=[self.tri])
        P.dma(self.ones[:], I["k_ones"], w=[self.ones])
        P.op("dve", lambda e: e.tensor_copy(out=self.identb[:], in_=self.identf[:]), r=[self.identf], w=[self.identb])
        P.op("pool", lambda e: e.memset(self.epsc[:], 1e-5), w=[self.epsc])
        P.op("pool", lambda e: e.memset(self.modp[:], 0.0), w=[self.modp])
        P.op("pool", lambda e: e.memset(self.zero[:], 0.0), w=[self.zero])
        for row in (0, 257, 2306):
            for c0 in range(0, DPROJ, 512):
                n = min(512, DPROJ - c0)
                P.dma(self.S["proj"][row:row + 1, c0:c0 + n], self.zero[0:1, 0:n], r=[self.zero])

    def stage_weights(self):
        P, I, S = self.P, self.I, self.S
        with ExitStack() as st:
            stf = Rot([self.sbt(st, "wstf", [128, 2 * DFF]) for _ in range(2)])
            stb = Rot([self.sbt(st, "wstb", [128, 2 * DFF], BF16) for _ in range(2)])
            engs = Rot(["act", "dve", "pool"])
            for l in self.layers:
                for (src, dst, K, N) in ((I["w_in"][l], S["wb_in%d" % l], D, DPROJ),
                                         (I["w_out"][l], S["wb_out%d" % l], D, D),
                                         (I["ffn_w_up"][l], S["wb_up%d" % l], D, 2 * DFF),
                                         (I["ffn_w_down"][l], S["wb_dn%d" % l], DFF, D)):
                    for kk in range(K // 128):
                        a = stf.next(); b = stb.next()
                        P.dma(a[:, 0:N], src[kk * 128:(kk + 1) * 128, :], w=[a])
                        e = engs.next()
                        if e == "act":
                            P.op("act", lambda eng, a=a, b=b, N=N: eng.copy(out=b[:, 0:N], in_=a[:, 0:N]), r=[a], w=[b])
                        else:
                            P.op(e, lambda eng, a=a, b=b, N=N: eng.tensor_copy(out=b[:, 0:N], in_=a[:, 0:N]), r=[a], w=[b])
                        P.dma(dst[kk * 128:(kk + 1) * 128, :], b[:, 0:N], r=[b], eng="act")
            P.barrier()

    def stage_mod(self):
        P, I, S, nc = self.P, self.I, self.S, self.nc
        n_seq = self.n_seq
        with ExitStack() as st:
            cT = self.sbt(st, "cT", [128, 8, 5])
            aw = Rot([self.sbt(st, "aw", [128, 8, 512]) for _ in range(2)])
            ab = self.sbt(st, "ab", [5, 6 * D])
            msb = self.sbt(st, "msb", [5, 6 * D])
            ps = Rot([self.pst(st, "mps", [128, 512]) for _ in range(2)])
            P.op("pool", lambda e: e.memset(cT[:], 0.0), w=[cT])
            for r in range(n_seq):
                P.dma(cT[:, :, r], I["c"][r].rearrange("(k p) -> p k", p=128), w=[cT], allow_slow_non_contiguous=True)
            P.dma(cT[:, :, 4], I["c_ctx"].rearrange("(k p) -> p k", p=128), w=[cT], allow_slow_non_contiguous=True)
            P.op("act", lambda e: e.activation(out=cT[:], in_=cT[:], func=AF.Silu), r=[cT], w=[cT])
            for l in self.layers:
                P.dma(ab[:], I["ada_b"][l].partition_broadcast(5), w=[ab])
                for cg in range(12):
                    a = aw.next(); p = ps.next()
                    P.dma(a[:], I["ada_w"][l][:, cg * 512:(cg + 1) * 512].rearrange("(k p) n -> p k n", p=128), w=[a])

                    def mm(e, a=a, p=p):
                        for k in range(8):
                            i = e.matmul(p[0:5, :], lhsT=cT[:, k, :], rhs=a[:, k, :], start=(k == 0), stop=(k == 7))
                        return i
                    P.op("pe", mm, r=[cT, a], w=[p])
                    P.op("dve", lambda e, p=p, cg=cg: e.tensor_tensor(out=msb[:, cg * 512:(cg + 1) * 512], in0=p[0:5, :],
                                                                      in1=ab[:, cg * 512:(cg + 1) * 512], op=ALU.add),
                         r=[p, ab], w=[msb])
                P.dma(S["modv%d" % l], msb[:], r=[msb])
                P.barrier()
                for r in range(5):
                    for jj, j in enumerate((0, 1, 3, 4)):
                        P.dma(self.modp[:, l, r, jj, :], S["modv%d" % l][r, j * D:(j + 1) * D].rearrange("(k p) -> p k", p=128),
                              w=[self.modp], allow_slow_non_contiguous=True)
            for jj in (1, 3):
                P.op("dve", lambda e, jj=jj: e.tensor_scalar_add(out=self.modp[:, :, :, jj, :], in0=self.modp[:, :, :, jj, :], scalar1=1.0),
                     r=[self.modp], w=[self.modp])
            P.barrier()

    def run_sequence(self, s):
        P, I = self.P, self.I
        for t in range(NT):
            src = I["ctx"][s, t * 128:(t + 1) * 128, :] if t < 2 else I["x"][s, (t - 2) * 128:(t - 1) * 128, :]
            P.dma(self.xres[t][:], src, w=[self.xres[t]])
        for l in self.layers:
            self.run_layer(s, l)
        for t in range(2, NT):
            P.dma(self.out[s, (t - 2) * 128:(t - 1) * 128, :], self.xres[t][:], r=[self.xres[t]])
        if self.dbg:
            for t in range(NT):
                P.dma(self.S["xdump"][t * 128:(t + 1) * 128, :], self.xres[t][:], r=[self.xres[t]])
        P.barrier()

    def on(self, name):
        return self.stages is None or name in self.stages

    def run_layer(self, s, l):
        P = self.P
        self.cur_layer = l
        with ExitStack() as st:
            uT = self.sbt(st, "uT", [128, 8, TOK], BF16)
            self.ln_mod(s, l, uT, 0)
            if self.on("proj"):
                self.in_proj(l, uT)
            P.barrier()
        if self.on("ret"):
            self.mix_retention(s, l)
        if self.on("mlstm"):
            self.mix_mlstm(s, l)
        if self.on("hyena"):
            self.mix_hyena(s, l)
        if self.on("rwkv"):
            self.mix_rwkv(s, l)
        if self.dbg and self.on("mixin"):
            for t in range(NT):
                P.dma(self.S["mix"][t * 128:(t + 1) * 128, :], self.I["mix_in"][t * 128:(t + 1) * 128, :])
            P.barrier()
        if self.on("post"):
            self.post_mix(s, l)
        if self.on("ffn") or self.on("ffnup"):
            with ExitStack() as st:
                uT = self.sbt(st, "uT", [128, 8, TOK], BF16)
                self.ln_mod(s, l, uT, 1)
                self.ffn_up(l, uT)
                P.barrier()
        if self.on("ffn") or self.on("ffndown"):
            self.ffn_down(s, l)
        P.barrier()

    def bc(self, st, src, n, name="bc"):
        t = self.sbt(st, name, [128, n])
        self.P.dma(t[:], src.partition_broadcast(128), w=[t])
        return t

    def resid_ln(self, t, py, mbc, g, b, tmp, stats, mv, rstd):
        P = self.P
        x = self.xres[t]
        for h in range(2):
            P.op("dve", lambda e, h=h: e.tensor_tensor(out=tmp[:, h * 512:(h + 1) * 512], in0=py[h][:], in1=mbc[:, h * 512:(h + 1) * 512], op=ALU.mult),
                 r=[py[h], mbc], w=[tmp])
        P.op("dve", lambda e: e.scalar_tensor_tensor(out=tmp[:], in0=x[:], scalar=ALPHA, in1=tmp[:], op0=ALU.mult, op1=ALU.add), r=[x, tmp], w=[tmp])
        self.ln_stats(tmp, stats, mv, rstd)
        P.op("dve", lambda e: e.tensor_scalar(out=tmp[:], in0=tmp[:], scalar1=mv[:, 0:1], scalar2=rstd[:, 0:1], op0=ALU.subtract, op1=ALU.mult),
             r=[tmp, mv, rstd], w=[tmp])
        P.op("pool", lambda e: e.tensor_tensor(out=tmp[:], in0=tmp[:], in1=g[:], op=ALU.mult), r=[tmp, g], w=[tmp])
        P.op("dve", lambda e: e.tensor_tensor(out=x[:], in0=tmp[:], in1=b[:], op=ALU.add), r=[tmp, b], w=[x])

    def post_mix(self, s, l):
        P, S, I = self.P, self.S, self.I
        with ExitStack() as st:
            wo = self.sbt(st, "wo", [128, 8, D], BF16)
            P.dma(wo[:], S["wb_out%d" % l].rearrange("(k p) n -> p k n", p=128), w=[wo])
            g = self.bc(st, I["ln1_g"][l], D); b = self.bc(st, I["ln1_b"][l], D)
            m_s = self.bc(st, S["modv%d" % l][s, 2 * D:3 * D], D); m_c = self.bc(st, S["modv%d" % l][4, 2 * D:3 * D], D)
            mx = Rot([self.sbt(st, "mx", [128, D]) for _ in range(3)])
            mxb = Rot([self.sbt(st, "mxb", [128, D], BF16) for _ in range(2)])
            mT = Rot([self.sbt(st, "mT", [128, 8, 128], BF16) for _ in range(2)])
            tmp = self.sbt(st, "tmp", [128, D])
            stats = self.sbt(st, "stats", [128, 2, 6]); mv = self.sbt(st, "mv", [128, 2]); rstd = self.sbt(st, "rstd", [128, 1])
            pt = Rot([self.pst(st, "ptr", [128, 8, 128], BF16) for _ in range(2)])
            py = Rot([[self.pst(st, "py", [128, 512]) for _ in range(2)] for _ in range(2)])
            for t in range(2 if l == self.last_layer else 0, NT):
                a = mx.next(); ab = mxb.next(); m = mT.next(); p = pt.next(); y = py.next()
                P.dma(a[:], S["mix"][t * 128:(t + 1) * 128, :], w=[a])
                P.op("act", lambda e, a=a, ab=ab: e.copy(out=ab[:], in_=a[:]), r=[a], w=[ab])

                def tr(e, ab=ab, p=p):
                    for k in range(8):
                        i = e.transpose(out=p[:, k, :], in_=ab[:, k * 128:(k + 1) * 128], identity=self.identb[:])
                    return i
                P.op("pe", tr, r=[ab, self.identb], w=[p])
                P.op("act", lambda e, p=p, m=m: e.copy(out=m[:], in_=p[:]), r=[p], w=[m])
                for h in range(2):
                    def mm(e, m=m, y=y, h=h):
                        for k in range(8):
                            i = e.matmul(y[h][:], lhsT=m[:, k, :], rhs=wo[:, k, h * 512:(h + 1) * 512], start=(k == 0), stop=(k == 7))
                        return i
                    P.op("pe", mm, r=[m, wo], w=[y[h]])
                self.resid_ln(t, y, m_c if t < 2 else m_s, g, b, tmp, stats, mv, rstd)
            P.barrier()

    def ffn_up(self, l, uT):
        P, S, I = self.P, self.S, self.I
        NJ = DFF // 128
        chunks = [(0, 256), (256, 768), (768, 1280), (1280, 1792), (1792, 2304)]
        if l == self.last_layer:
            chunks = chunks[1:]
        with ExitStack() as st:
            fcw = self.sbt(st, "fcw", [128, 3, NJ]); fcb = self.sbt(st, "fcb", [128, NJ])
            for tap in range(3):
                P.dma(fcw[:, tap, :], I["ffn_conv_w"][l, tap].rearrange("(j p) -> p j", p=128), w=[fcw], allow_slow_non_contiguous=True)
            P.dma(fcb[:], I["ffn_conv_b"][l].rearrange("(j p) -> p j", p=128), w=[fcb], allow_slow_non_contiguous=True)
            wa = Rot([self.sbt(st, "wa", [128, 8, 128], BF16) for _ in range(3)])
            wb = Rot([self.sbt(st, "wb", [128, 8, 128], BF16) for _ in range(3)])
            aS = Rot([self.sbt(st, "aS", [128, NROWS]) for _ in range(2)])
            bS = Rot([self.sbt(st, "bS", [128, NROWS]) for _ in range(2)])
            cv = Rot([self.sbt(st, "cv", [128, NROWS]) for _ in range(2)])
            gS = Rot([self.sbt(st, "gS", [128, NROWS], BF16) for _ in range(2)])
            pa = Rot([self.pst(st, "pa", [128, 512]) for _ in range(4)])
            pb = Rot([self.pst(st, "pb", [128, 512]) for _ in range(4)])
            for a in aS.items + bS.items:
                P.op("pool", lambda e, a=a: e.memset(a[:], 0.0), w=[a])
            for j in range(NJ):
                w1 = wa.next(); w2 = wb.next(); a = aS.next(); b = bS.next(); c = cv.next(); g = gS.next()
                P.dma(w1[:], S["wb_up%d" % l][:, j * 128:(j + 1) * 128].rearrange("(k p) n -> p k n", p=128), w=[w1])
                P.dma(w2[:], S["wb_up%d" % l][:, DFF + j * 128:DFF + (j + 1) * 128].rearrange("(k p) n -> p k n", p=128), w=[w2])
                for (c0, c1) in chunks:
                    n = c1 - c0
                    o0 = c0 + 1 if c0 < 256 else c0 + 2
                    p1 = pa.next(); p2 = pb.next()

                    def mm(e, w=w1, p=p1, c0=c0, c1=c1, n=n):
                        for k in range(8):
                            i = e.matmul(p[:, 0:n], lhsT=w[:, k, :], rhs=uT[:, k, c0:c1], start=(k == 0), stop=(k == 7))
                        return i
                    P.op("pe", mm, r=[w1] + [(uT, t) for t in range(2 if l == self.last_layer else 0, NT)], w=[p1])

                    def mm2(e, w=w2, p=p2, c0=c0, c1=c1, n=n):
                        for k in range(8):
                            i = e.matmul(p[:, 0:n], lhsT=w[:, k, :], rhs=uT[:, k, c0:c1], start=(k == 0), stop=(k == 7))
                        return i
                    P.op("pe", mm2, r=[w2] + [(uT, t) for t in range(2 if l == self.last_layer else 0, NT)], w=[p2])
                    P.op("act", lambda e, p=p1, a=a, o0=o0, n=n: e.copy(out=a[:, o0:o0 + n], in_=p[:, 0:n]), r=[p1], w=[a])
                    P.op("dve", lambda e, p=p2, b=b, o0=o0, n=n: e.tensor_copy(out=b[:, o0:o0 + n], in_=p[:, 0:n]), r=[p2], w=[b])
                W = NROWS - 2
                P.op("dve", lambda e, a=a, c=c, j=j: e.tensor_scalar(out=c[:, 1:1 + W], in0=a[:, 0:W], scalar1=fcw[:, 0, j:j + 1], scalar2=fcb[:, j:j + 1], op0=ALU.mult, op1=ALU.add),
                     r=[a, fcw, fcb], w=[c])
                P.op("dve", lambda e, a=a, c=c, j=j: e.scalar_tensor_tensor(out=c[:, 1:1 + W], in0=a[:, 1:1 + W], scalar=fcw[:, 1, j:j + 1], in1=c[:, 1:1 + W], op0=ALU.mult, op1=ALU.add),
                     r=[a, fcw, c], w=[c])
                P.op("dve", lambda e, a=a, c=c, j=j: e.scalar_tensor_tensor(out=c[:, 1:1 + W], in0=a[:, 2:2 + W], scalar=fcw[:, 2, j:j + 1], in1=c[:, 1:1 + W], op0=ALU.mult, op1=ALU.add),
                     r=[a, fcw, c], w=[c])
                P.op("act", lambda e, c=c: e.activation(out=c[:, 1:1 + W], in_=c[:, 1:1 + W], func=AF.Silu), r=[c], w=[c])
                P.op("pool", lambda e, c=c, b=b, g=g: e.tensor_tensor(out=g[:, 1:1 + W], in0=c[:, 1:1 + W], in1=b[:, 1:1 + W], op=ALU.mult), r=[c, b], w=[g])
                if l != self.last_layer:
                    P.dma(S["gT"][j * 128:(j + 1) * 128, 0:256], g[:, 1:257], r=[g], eng="act")
                P.dma(S["gT"][j * 128:(j + 1) * 128, 256:TOK], g[:, 258:258 + SEQ], r=[g], eng="act")

    def ffn_down(self, s, l):
        P, S, I = self.P, self.S, self.I
        NJ = DFF // 128
        with ExitStack() as st:
            wd = self.sbt(st, "wd", [128, NJ, D], BF16)
            for j in range(NJ):
                P.dma(wd[:, j, :], S["wb_dn%d" % l][j * 128:(j + 1) * 128, :], w=[wd])
            g = self.bc(st, I["ln2_g"][l], D); b = self.bc(st, I["ln2_b"][l], D)
            m_s = self.bc(st, S["modv%d" % l][s, 5 * D:6 * D], D); m_c = self.bc(st, S["modv%d" % l][4, 5 * D:6 * D], D)
            gt = Rot([self.sbt(st, "gt", [128, NJ, 128], BF16) for _ in range(3)])
            tmp = self.sbt(st, "tmp", [128, D])
            stats = self.sbt(st, "stats", [128, 2, 6]); mv = self.sbt(st, "mv", [128, 2]); rstd = self.sbt(st, "rstd", [128, 1])
            py = Rot([[self.pst(st, "py", [128, 512]) for _ in range(2)] for _ in range(2)])
            for t in range(2 if l == self.last_layer else 0, NT):
                gg = gt.next(); y = py.next()
                for j0 in (0, 6, 12, 18):
                    j1 = min(NJ, j0 + 6)
                    P.dma(gg[:, j0:j1, :], S["gT"][j0 * 128:j1 * 128, t * 128:(t + 1) * 128].rearrange("(j p) n -> p j n", p=128), w=[gg])
                for h in range(2):
                    def mm(e, gg=gg, y=y, h=h):
                        for j in range(NJ):
                            i = e.matmul(y[h][:], lhsT=gg[:, j, :], rhs=wd[:, j, h * 512:(h + 1) * 512], start=(j == 0), stop=(j == NJ - 1))
                        return i
                    P.op("pe", mm, r=[gg, wd], w=[y[h]])
                self.resid_ln(t, y, m_c if t < 2 else m_s, g, b, tmp, stats, mv, rstd)
            P.barrier()


    def finish_heads(self, st_tiles, hacc, gate, gfunc, ng, nb, t, col, extra=None):
        P = self.P
        if t < 2 and self.cur_layer == self.last_layer:
            return
        s1, s2, gm, grs, sq, gout = st_tiles
        hv = hacc[:].rearrange("p (h f) -> p h f", f=64)
        gv = gout[:].rearrange("p (h f) -> p h f", f=64)
        bcl = lambda a: a.unsqueeze(2).to_broadcast([128, 4, 64])
        P.op("dve", lambda e: e.tensor_reduce(out=s1[:], in_=hv, axis=AX.X, op=ALU.add), r=[hacc], w=[s1])
        P.op("pool", lambda e: e.tensor_tensor(out=sq[:], in0=hacc[:], in1=hacc[:], op=ALU.mult), r=[hacc], w=[sq])
        P.op("dve", lambda e: e.tensor_reduce(out=s2[:], in_=sq[:].rearrange("p (h f) -> p h f", f=64), axis=AX.X, op=ALU.add), r=[sq], w=[s2])
        P.op("dve", lambda e: e.tensor_scalar_mul(out=gm[:], in0=s1[:], scalar1=1.0 / 64), r=[s1], w=[gm])
        P.op("dve", lambda e: e.tensor_tensor(out=s1[:], in0=gm[:], in1=gm[:], op=ALU.mult), r=[gm, s1], w=[s1])
        P.op("dve", lambda e: e.scalar_tensor_tensor(out=grs[:], in0=s2[:], scalar=1.0 / 64, in1=s1[:], op0=ALU.mult, op1=ALU.subtract), r=[s2, s1], w=[grs])
        P.op("act", lambda e: e.activation(out=grs[:], in_=grs[:], func=AF.Sqrt, bias=self.epsc[:, 0:1], scale=1.0), r=[grs, self.epsc], w=[grs])
        P.op("dve", lambda e: e.reciprocal(out=grs[:], in_=grs[:]), r=[grs], w=[grs])
        P.op("dve", lambda e: e.tensor_tensor(out=gv, in0=hv, in1=bcl(gm[:]), op=ALU.subtract), r=[hacc, gm], w=[gout])
        P.op("dve", lambda e: e.tensor_tensor(out=gv, in0=gv, in1=bcl(grs[:]), op=ALU.mult), r=[gout, grs], w=[gout])
        P.op("pool", lambda e: e.tensor_tensor(out=gout[:], in0=gout[:], in1=ng[:], op=ALU.mult), r=[gout, ng], w=[gout])
        P.op("pool", lambda e: e.tensor_tensor(out=gout[:], in0=gout[:], in1=nb[:], op=ALU.add), r=[gout, nb], w=[gout])
        if extra is not None:
            P.op("pool", lambda e: e.tensor_tensor(out=gout[:], in0=gout[:], in1=extra[:], op=ALU.add), r=[gout, extra], w=[gout])
        if gfunc is not None:
            P.op("act", lambda e: e.activation(out=gate, in_=gate, func=gfunc), r=[gate], w=[gate])
        if gate is not None:
            P.op("dve", lambda e: e.tensor_tensor(out=gout[:], in0=gout[:], in1=gate, op=ALU.mult), r=[gout, gate], w=[gout])
        P.dma(self.S["mix"][t * 128:(t + 1) * 128, col:col + 256], gout[:], r=[gout], eng="act")

    def head_tiles(self, st):
        return (self.sbt(st, "gs1", [128, 4]), self.sbt(st, "gs2", [128, 4]), self.sbt(st, "ggm", [128, 4]), self.sbt(st, "grs", [128, 4]),
                self.sbt(st, "gsq", [128, 256]), self.sbt(st, "gout", [128, 256]))

    @staticmethod
    def order(d):
        return list(range(NT)) if d == 0 else [1, 0] + list(range(NT - 1, 1, -1))


    def wrap_sin(self, h, tmp, L):
        P = self.P
        for _ in range(2):
            P.op("dve", lambda e: e.tensor_scalar(out=tmp[:], in0=h[:], scalar1=-math.pi, scalar2=2.0 * math.pi, op0=ALU.is_lt, op1=ALU.mult), r=[h], w=[tmp])
            P.op("dve", lambda e: e.tensor_tensor(out=h[:], in0=h[:], in1=tmp[:], op=ALU.add), r=[h, tmp], w=[h])
            P.op("dve", lambda e: e.tensor_scalar(out=tmp[:], in0=h[:], scalar1=math.pi, scalar2=-2.0 * math.pi, op0=ALU.is_gt, op1=ALU.mult), r=[h], w=[tmp])
            P.op("dve", lambda e: e.tensor_tensor(out=h[:], in0=h[:], in1=tmp[:], op=ALU.add), r=[h, tmp], w=[h])
        P.op("act", lambda e: e.activation(out=h[:], in_=h[:], func=AF.Sin), r=[h], w=[h])

    def stage_hyena(self):
        for l in self.layers:
            for kind, L in (("lat", SEQ), ("ctx", CTX)):
                if kind == "ctx" and l == self.last_layer:
                    continue
                self.stage_hyena_one(l, kind, L)

    def stage_hyena_one(self, l, kind, L):
        P, S, I = self.P, self.S, self.I
        if True:
            if True:
                nft = L // 128
                with ExitStack() as st:
                    w1 = self.sbt(st, "hw1", [33, 64]); w2 = self.sbt(st, "hw2", [64, 64]); w3 = self.sbt(st, "hw3", [64, 1024])
                    b1 = self.sbt(st, "hb1", [64, 1]); b2 = self.sbt(st, "hb2", [64, 1]); fr = self.sbt(st, "hfr", [64, 1])
                    fT = self.sbt(st, "hfT", [33, L]); h1 = self.sbt(st, "hh1", [64, L]); h2 = self.sbt(st, "hh2", [64, L]); tmp = self.sbt(st, "htmp", [64, L])
                    P.dma(w1[:], I["hyena_w1"][l], w=[w1]); P.dma(w2[:], I["hyena_w2"][l], w=[w2]); P.dma(w3[:], I["hyena_w3"][l], w=[w3])
                    P.dma(b1[:], I["hyena_b1"][l].rearrange("(p o) -> p o", o=1), w=[b1])
                    P.dma(b2[:], I["hyena_b2"][l].rearrange("(p o) -> p o", o=1), w=[b2])
                    P.dma(fr[:], I["hyena_freq"][l].rearrange("(p o) -> p o", o=1), w=[fr])
                    P.dma(fT[:], I["k_hfeat_" + kind], w=[fT])
                    pb = Rot([self.bank(st, "hps") for _ in range(4)])
                    for (w, src, bb, dst) in ((w1, fT, b1, h1), (w2, h1, b2, h2)):
                        for c0 in range(0, L, 512):
                            n = min(512, L - c0)
                            p = pb.next()
                            P.op("pe", lambda e, p=p, w=w, src=src, c0=c0, n=n: e.matmul(p[0:64, 0:n], lhsT=w[:], rhs=src[:, c0:c0 + n], start=True, stop=True), r=[w, src], w=[p])
                            P.op("dve", lambda e, p=p, dst=dst, bb=bb, c0=c0, n=n: e.tensor_scalar(out=dst[:, c0:c0 + n], in0=p[0:64, 0:n], scalar1=bb[:, 0:1], scalar2=fr[:, 0:1], op0=ALU.add, op1=ALU.mult),
                                 r=[p, bb, fr], w=[dst])
                        self.wrap_sin(dst, tmp, L)
                    Pm = self.sbt(st, "hPm", [128, nft, 512], BF16); Mm = self.sbt(st, "hMm", [128, nft, 512], BF16)
                    win = Rot([self.sbt(st, "hwin", [128, 256]) for _ in range(2)])
                    hf = self.sbt(st, "hhf", [128, 512]); hb = self.sbt(st, "hhb", [128, 512])
                    for tt in range(nft):
                        wn = win.next(); pA = pb.next(); pB = pb.next()
                        P.dma(wn[:], I["k_hwin_" + kind][tt * 128:(tt + 1) * 128, :], w=[wn])
                        P.op("pe", lambda e, pA=pA, tt=tt: e.matmul(pA[:], lhsT=h2[:, tt * 128:(tt + 1) * 128], rhs=w3[:, 0:512], start=True, stop=True), r=[h2, w3], w=[pA])
                        P.op("pe", lambda e, pB=pB, tt=tt: e.matmul(pB[:], lhsT=h2[:, tt * 128:(tt + 1) * 128], rhs=w3[:, 512:1024], start=True, stop=True), r=[h2, w3], w=[pB])
                        for o in range(2):
                            P.op("dve", lambda e, pA=pA, wn=wn, o=o: e.tensor_tensor(out=hf[:, o * 256:(o + 1) * 256], in0=pA[:, o * 256:(o + 1) * 256], in1=wn[:], op=ALU.mult), r=[pA, wn], w=[hf])
                            P.op("dve", lambda e, pB=pB, wn=wn, o=o: e.tensor_tensor(out=hb[:, o * 256:(o + 1) * 256], in0=pB[:, o * 256:(o + 1) * 256], in1=wn[:], op=ALU.mult), r=[pB, wn], w=[hb])
                        P.op("pool", lambda e, tt=tt: e.tensor_tensor(out=Pm[:, tt, :], in0=hf[:], in1=hb[:], op=ALU.add), r=[hf, hb], w=[Pm])
                        P.op("pool", lambda e, tt=tt: e.tensor_tensor(out=Mm[:, tt, :], in0=hf[:], in1=hb[:], op=ALU.subtract), r=[hf, hb], w=[Mm])
                    bas = Rot([self.sbt(st, "hbas", [128, 2, nft, 128], BF16) for _ in range(2)])
                    tab = Rot([self.sbt(st, "htab", [128, 3, 512]) for _ in range(2)])
                    for ft in range(nft):
                        ba = bas.next(); tb = tab.next(); pK = pb.next(); pI = pb.next()
                        P.dma(ba[:], I["k_hF_" + kind][ft], w=[ba])

                        def mk(e, ba=ba, p=pK, ri=0, src=Pm):
                            for ch in range(nft):
                                i = e.matmul(p[:], lhsT=ba[:, ri, ch, :], rhs=src[:, ch, :], start=(ch == 0), stop=(ch == nft - 1))
                            return i
                        P.op("pe", mk, r=[ba, Pm], w=[pK])
                        P.op("pe", lambda e, ba=ba, pI=pI, mk=mk: mk(e, ba=ba, p=pI, ri=1, src=Mm), r=[ba, Mm], w=[pI])
                        P.op("act", lambda e, tb=tb, pK=pK: e.copy(out=tb[:, 0, :], in_=pK[:]), r=[pK], w=[tb])
                        P.op("dve", lambda e, tb=tb, pK=pK: e.tensor_copy(out=tb[:, 2, :], in_=pK[:]), r=[pK], w=[tb])
                        P.op("act", lambda e, tb=tb, pI=pI: e.copy(out=tb[:, 1, :], in_=pI[:]), r=[pI], w=[tb])
                        if ft == 0:
                            pN = pb.next()
                            P.op("pe", lambda e, ba=ba, pN=pN, mk=mk: mk(e, ba=ba, p=pN, ri=1, src=Pm), r=[ba, Pm], w=[pN])
                            P.op("pool", lambda e, tb=tb: e.memset(tb[0:1, 1, :], 0.0), r=[tb], w=[tb])
                            P.op("dve", lambda e, tb=tb, pN=pN: e.tensor_copy(out=tb[0:1, 2, :], in_=pN[0:1, :]), r=[pN, tb], w=[tb])
                        P.dma(S["HT%d_%s" % (l, kind)][ft], tb[:], r=[tb], eng="act")
                    P.barrier()

    def mix_hyena(self, s, l):
        for kind, tiles in (("lat", list(range(2, NT))), ("ctx", [0, 1])):
            if kind == "ctx" and l == self.last_layer:
                continue
            self.mix_hyena_one(s, l, kind, tiles)

    def mix_hyena_one(self, s, l, kind, tiles):
        P, S, I = self.P, self.S, self.I
        C0 = 1040
        if True:
            nft = len(tiles)
            with ExitStack() as st:
                store = self.sbt(st, "hst", [128, nft, 768])
                Vb = self.sbt(st, "hVb", [128, nft, 256], BF16)
                with ExitStack() as st2:
                    cw = self.sbt(st2, "hcw", [128, 3, 768]); cb = self.bc(st2, I["hyena_conv_b"][l], 768)
                    for tap in range(3):
                        P.dma(cw[:, tap, :], I["hyena_conv_w"][l, tap].partition_broadcast(128), w=[cw])
                    q3 = Rot([self.sbt(st2, "hq3", [128, 3, 768]) for _ in range(3)])
                    tmp = self.sbt(st2, "hctmp", [128, 768])
                    for i, t in enumerate(tiles):
                        q = q3.next(); rb = rowbase(t)
                        for tap in range(3):
                            P.dma(q[:, tap, :], S["proj"][rb - 1 + tap:rb - 1 + tap + 128, C0:C0 + 768], w=[q])
                        sti = store[:, i, :]; K = (store, i)
                        P.op("dve", lambda e, q=q, sti=sti: e.tensor_tensor(out=sti, in0=q[:, 0, :], in1=cw[:, 0, :], op=ALU.mult), r=[q, cw], w=[K])
                        P.op("pool", lambda e, q=q: e.tensor_tensor(out=tmp[:], in0=q[:, 1, :], in1=cw[:, 1, :], op=ALU.mult), r=[q, cw], w=[tmp])
                        P.op("dve", lambda e, sti=sti: e.tensor_tensor(out=sti, in0=sti, in1=tmp[:], op=ALU.add), r=[K, tmp], w=[K])
                        P.op("pool", lambda e, q=q: e.tensor_tensor(out=tmp[:], in0=q[:, 2, :], in1=cw[:, 2, :], op=ALU.mult), r=[q, cw], w=[tmp])
                        P.op("dve", lambda e, sti=sti: e.tensor_tensor(out=sti, in0=sti, in1=tmp[:], op=ALU.add), r=[K, tmp], w=[K])
                        P.op("pool", lambda e, sti=sti: e.tensor_tensor(out=sti, in0=sti, in1=cb[:], op=ALU.add), r=[K, cb], w=[K])
                        P.op("act", lambda e, i=i: e.copy(out=Vb[:, i, :], in_=store[:, i, 0:256]), r=[K], w=[(Vb, i)])
                    P.barrier()
                ng = self.bc(st, I["out_norm_g"][l, 256:512], 256); nb = self.bc(st, I["out_norm_b"][l, 256:512], 256)
                dd = self.sbt(st, "hdd", [128, 2, 256])
                for o in range(2):
                    P.dma(dd[:, o, :], I["hyena_d"][l, o].partition_broadcast(128), w=[dd])
                Ys = self.sbt(st, "hYs", [128, nft, 2, 256], BF16)
                bas = Rot([self.sbt(st, "hbas", [128, 2, nft, 128], BF16) for _ in range(4)])
                tab = Rot([self.sbt(st, "htab", [128, 3, 256]) for _ in range(4)])
                t1 = self.sbt(st, "ht1", [128, 256]); t2 = self.sbt(st, "ht2", [128, 256]); t3 = self.sbt(st, "ht3", [128, 256]); t4 = self.sbt(st, "ht4", [128, 256])
                yy = Rot([self.sbt(st, "hyy", [128, 256]) for _ in range(2)])
                ht = self.head_tiles(st)
                pV = Rot([self.bank(st, "hpV") for _ in range(2)]); pY = Rot([self.bank(st, "hpY") for _ in range(2)])
                allV = [(Vb, i) for i in range(nft)]; allY = [(Ys, i) for i in range(nft)]
                for o in range(2):
                    for ft in range(nft):
                        ba = bas.next(); tb = tab.next(); p = pV.next()
                        P.dma(ba[:], I["k_hF_" + kind][ft], w=[ba])
                        P.dma(tb[:], S["HT%d_%s" % (l, kind)][ft][:, :, o * 256:(o + 1) * 256], w=[tb])

                        def fw(e, ba=ba, p=p):
                            for ri in range(2):
                                for ch in range(nft):
                                    i = e.matmul(p[:, ri * 256:(ri + 1) * 256], lhsT=ba[:, ri, ch, :], rhs=Vb[:, ch, :], start=(ch == 0), stop=(ch == nft - 1))
                            return i
                        P.op("pe", fw, r=[ba] + allV, w=[p])
                        P.op("dve", lambda e, p=p, tb=tb: e.tensor_tensor(out=t1[:], in0=p[:, 0:256], in1=tb[:, 0, :], op=ALU.mult), r=[p, tb], w=[t1])
                        P.op("dve", lambda e, p=p, tb=tb: e.tensor_tensor(out=t2[:], in0=p[:, 256:512], in1=tb[:, 1, :], op=ALU.mult), r=[p, tb], w=[t2])
                        P.op("pool", lambda e, ft=ft: e.tensor_tensor(out=Ys[:, ft, 0, :], in0=t1[:], in1=t2[:], op=ALU.subtract), r=[t1, t2], w=[(Ys, ft)])
                        P.op("dve", lambda e, p=p, tb=tb: e.tensor_tensor(out=t3[:], in0=p[:, 0:256], in1=tb[:, 1, :], op=ALU.mult), r=[p, tb], w=[t3])
                        P.op("dve", lambda e, p=p, tb=tb: e.tensor_tensor(out=t4[:], in0=p[:, 256:512], in1=tb[:, 2, :], op=ALU.mult), r=[p, tb], w=[t4])
                        P.op("pool", lambda e, ft=ft: e.tensor_tensor(out=Ys[:, ft, 1, :], in0=t3[:], in1=t4[:], op=ALU.add), r=[t3, t4], w=[(Ys, ft)])
                    for i, t in enumerate(tiles):
                        ba = bas.next(); p = pY.next(); K = (store, i)
                        P.dma(ba[:], I["k_hB_" + kind][i], w=[ba])

                        def iv(e, ba=ba, p=p):
                            n = 0
                            for fc in range(nft):
                                for ri in range(2):
                                    ins = e.matmul(p[:, 0:256], lhsT=ba[:, ri, fc, :], rhs=Ys[:, fc, ri, :], start=(n == 0), stop=(n == 2 * nft - 1))
                                    n += 1
                            return ins
                        P.op("pe", iv, r=[ba] + allY, w=[p])
                        P.op("pool", lambda e, i=i, o=o: e.tensor_tensor(out=t1[:], in0=store[:, i, 0:256], in1=dd[:, o, :], op=ALU.mult), r=[K, dd], w=[t1])
                        P.op("dve", lambda e, p=p: e.tensor_tensor(out=t1[:], in0=p[:, 0:256], in1=t1[:], op=ALU.add), r=[p, t1], w=[t1])
                        if o == 0:
                            P.op("dve", lambda e, i=i: e.tensor_tensor(out=store[:, i, 0:256], in0=store[:, i, 256:512], in1=t1[:], op=ALU.mult), r=[K, t1], w=[K])
                            P.op("act", lambda e, i=i: e.copy(out=Vb[:, i, :], in_=store[:, i, 0:256]), r=[K], w=[(Vb, i)])
                        else:
                            y_ = yy.next()
                            P.op("dve", lambda e, i=i, y_=y_: e.tensor_tensor(out=y_[:], in0=store[:, i, 512:768], in1=t1[:], op=ALU.mult), r=[K, t1], w=[y_])
                            self.finish_heads(ht, y_, None, None, ng, nb, t, 256)
                P.barrier()

    def TT(self, eng, out, a, b, op, r, w):
        self.P.op(eng, lambda e: e.tensor_tensor(out=out, in0=a, in1=b, op=op), r=r, w=w)

    def TS(self, eng, out, a, s1, s2, op0, op1, r, w):
        self.P.op(eng, lambda e: e.tensor_scalar(out=out, in0=a, scalar1=s1, scalar2=s2, op0=op0, op1=op1), r=r, w=w)

    def TSM(self, eng, out, a, s1, r, w):
        self.P.op(eng, lambda e: e.tensor_scalar_mul(out=out, in0=a, scalar1=s1), r=r, w=w)

    def STT(self, out, a, sc, b, op0, op1, r, w):
        self.P.op("dve", lambda e: e.scalar_tensor_tensor(out=out, in0=a, scalar=sc, in1=b, op0=op0, op1=op1), r=r, w=w)

    def ACT(self, out, a, func, r, w, bias=None, scale=1.0):
        if bias is None:
            self.P.op("act", lambda e: e.activation(out=out, in_=a, func=func, scale=scale), r=r, w=w)
        else:
            self.P.op("act", lambda e: e.activation(out=out, in_=a, func=func, bias=bias, scale=scale), r=r, w=w)

    def CP(self, eng, out, a, r, w):
        if eng == "act":
            self.P.op("act", lambda e: e.copy(out=out, in_=a), r=r, w=w)
        else:
            self.P.op(eng, lambda e: e.tensor_copy(out=out, in_=a), r=r, w=w)

    def MM(self, specs, r, w):
        specs = list(specs)

        def f(e):
            for (o, l_, r_, st_, sp_) in specs:
                i = e.matmul(o, lhsT=l_, rhs=r_, start=st_, stop=sp_)
            return i
        self.P.op("pe", f, r=r, w=w)

    def mix_rwkv(self, s, l):
        P, S, I = self.P, self.S, self.I
        C0 = 2832
        EM05 = math.exp(-0.5)
        with ExitStack() as st:
            ng = self.bc(st, I["out_norm_g"][l, 768:1024], 256); nb = self.bc(st, I["out_norm_b"][l, 768:1024], 256)
            mu = self.sbt(st, "rmu", [128, 2, 896]); w0 = self.sbt(st, "rw0", [128, 2, 256]); a0 = self.sbt(st, "ra0", [128, 2, 256])
            for j in range(2):
                P.dma(mu[:, j, :], I["rwkv_mu"][l, j].partition_broadcast(128), w=[mu])
                P.dma(w0[:, j, :], I["rwkv_w0"][l, j].partition_broadcast(128), w=[w0])
                P.dma(a0[:, j, :], I["rwkv_a0"][l, j].partition_broadcast(128), w=[a0])
            kks = self.bc(st, I["rwkv_kk"][l], 256); ka = self.bc(st, I["rwkv_ka"][l], 256)
            rk = self.bc(st, I["rwkv_rk"][l].rearrange("a b -> (a b)"), 256)
            Wl = self.sbt(st, "rWl", [128, 1280])
            P.op("pool", lambda e: e.memset(Wl[:], 0.0), w=[Wl])
            for j in range(2):
                P.dma(Wl[0:32, j * 256:(j + 1) * 256], I["rwkv_w2"][l, j], w=[Wl])
                P.dma(Wl[32:64, 512 + j * 256:512 + (j + 1) * 256], I["rwkv_a2"][l, j], w=[Wl])
            P.dma(Wl[64:128, 1024:1280], I["rwkv_g2"][l], w=[Wl])
            cmask = self.sbt(st, "rcm", [128, 2, 256])
            for d in range(2):
                self.CP("dve", cmask[:, d, 0:128], self.tri[:, 2 + d, :], [self.tri], [cmask])
                self.CP("dve", cmask[:, d, 128:256], self.tri[:, d, :], [self.tri], [cmask])
            eps12 = self.sbt(st, "reps", [128, 1])
            P.op("pool", lambda e: e.memset(eps12[:], 1e-12), w=[eps12])
            hacc = [self.sbt(st, "hacc", [128, 256]) for _ in range(NT)]
            ht = self.head_tiles(st)
            q3 = Rot([self.sbt(st, "rq3", [128, 3, 896]) for _ in range(1)])
            z = self.sbt(st, "rz", [128, 896])
            Lt = self.sbt(st, "rLt", [128, 128]); LT = self.sbt(st, "rLT", [128, 128])
            F = {n: self.sbt(st, "r" + n, [128, 256]) for n in ("logw", "a", "kk", "kd", "t1", "t2", "cum", "gam", "game", "ginv")}
            ss = self.sbt(st, "rss", [128, 4]); s4 = self.sbt(st, "rs4", [128, 4])
            X4 = Rot([self.sbt(st, "rX4", [128, 4, 256], BF16) for _ in range(3)])
            Vb = Rot([self.sbt(st, "rVb", [128, 256], BF16) for _ in range(3)])
            TTt = Rot([self.sbt(st, "rTT", [128, 8, 128], BF16) for _ in range(2)])
            ZP = Rot([self.sbt(st, "rZP", [128, 4, 2, 128], BF16) for _ in range(2)])
            for zp in ZP.items:
                P.op("pool", lambda e, zp=zp: e.memset(zp[:], 0.0), w=[zp])
            GC = [Rot([self.sbt(st, "rgc", [128, 1]) for _ in range(3)]) for _ in range(2)]
            FG = Rot([self.sbt(st, "rFg", [128, 256]) for _ in range(3)]); FB = Rot([self.sbt(st, "rFb", [128, 256]) for _ in range(3)])
            rwm = self.sbt(st, "rwm", [128, 2, 8, 128], BF16)
            P.dma(rwm[:], I["k_rwm"], w=[rwm])
            t4 = lambda nm_, n_: Rot([self.sbt(st, nm_, [128, 4, 128], BF16) for _ in range(n_)])
            RX = t4("rX", 3); RXT = t4("rXT", 3); RQ = t4("rQ", 3); RQT = t4("rQT", 3); RE = t4("rE", 2); RE2 = t4("rE2", 2)
            AdN4 = self.sbt(st, "rAdN", [128, 4, 128], BF16); AdT4 = self.sbt(st, "rAdT", [128, 4, 128], BF16)
            LoN4 = [self.sbt(st, "rLoN", [128, 4, 128], BF16) for _ in range(3)]; LoT4 = [self.sbt(st, "rLoT", [128, 4, 128], BF16) for _ in range(3)]
            RbTr = t4("rRb", 2); PTr = t4("rPT", 2)
            SKr = Rot([self.sbt(st, "rSK", [128, 2, 4, 128], BF16) for _ in range(2)])
            Wb = [self.sbt(st, "rWb", [128, 128], BF16) for _ in range(2)]
            Ub = [self.sbt(st, "rUb", [128, 128], BF16) for _ in range(2)]
            Hf = [self.sbt(st, "rHf", [128, 128]) for _ in range(2)]
            Hb = [self.sbt(st, "rHb", [128, 128], BF16) for _ in range(2)]
            htmp = self.sbt(st, "rht", [128, 128])
            pT = self.pst(st, "rpT", [128, 8, 128], BF16)
            pg = Rot([self.bank(st, "rpg") for _ in range(1)])
            pi = Rot([self.bank(st, "rpi") for _ in range(4)])
            pq = Rot([self.bank(st, "rpq") for _ in range(2)])
            for d in range(2):
                for pr in range(2):
                    P.op("pool", lambda e, pr=pr: e.memset(Hf[pr][:], 0.0), w=[Hf[pr]])
                    P.op("pool", lambda e, pr=pr: e.memset(Hb[pr][:], 0.0), w=[Hb[pr]])
                def prep(t):
                    q = q3.next(); rb = rowbase(t)
                    Fg = FG.next(); Fbon = FB.next(); gcol = [GC[0].next(), GC[1].next()]
                    for tap in range(3):
                        P.dma(q[:, tap, :], S["proj"][rb - 1 + tap:rb - 1 + tap + 128, C0:C0 + 896], w=[q])
                    self.TT("dve", q[:, 0, :], q[:, 0, :], q[:, 1, :], ALU.subtract, [q], [q])
                    self.TT("pool", q[:, 0, :], q[:, 0, :], mu[:, 0, :], ALU.mult, [q, mu], [q])
                    self.TT("pool", q[:, 2, :], q[:, 2, :], q[:, 1, :], ALU.subtract, [q], [q])
                    self.TT("pool", q[:, 2, :], q[:, 2, :], mu[:, 1, :], ALU.mult, [q, mu], [q])
                    self.TT("dve", z[:], q[:, 1, :], q[:, 0, :], ALU.add, [q], [z])
                    self.TT("dve", z[:], z[:], q[:, 2, :], ALU.add, [z, q], [z])
                    r_ = z[:, 0:256]; k_ = z[:, 256:512]; v_ = z[:, 512:768]
                    self.ACT(Lt[:, 0:32], z[:, 768:800], AF.Tanh, [z], [Lt])
                    self.CP("dve", Lt[:, 32:64], z[:, 800:832], [z], [Lt])
                    self.ACT(Lt[:, 64:128], z[:, 832:896], AF.Sigmoid, [z], [Lt])
                    p0 = pg.next()
                    P.op("pe", lambda e, p0=p0: e.transpose(out=p0[:, 0:128], in_=Lt[:], identity=self.identf[:]), r=[Lt, self.identf], w=[p0])
                    self.CP("dve", LT[:], p0[:, 0:128], [p0], [LT])
                    p1 = pg.next()
                    self.MM([(p1[:, 0:256], LT[:], Wl[:, d * 256:(d + 1) * 256], True, True),
                             (p1[:, 256:512], LT[:], Wl[:, 512 + d * 256:512 + (d + 1) * 256], True, True)], [LT, Wl], [p1])
                    self.TT("dve", F["logw"][:], p1[:, 0:256], w0[:, d, :], ALU.add, [p1, w0], [F["logw"]])
                    self.ACT(F["logw"][:], F["logw"][:], AF.Sigmoid, [F["logw"]], [F["logw"]])
                    self.TSM("dve", F["logw"][:], F["logw"][:], -EM05, [F["logw"]], [F["logw"]])
                    self.TT("dve", F["a"][:], p1[:, 256:512], a0[:, d, :], ALU.add, [p1, a0], [F["a"]])
                    self.ACT(F["a"][:], F["a"][:], AF.Sigmoid, [F["a"]], [F["a"]])
                    if d == 1:
                        p2 = pg.next()
                        self.MM([(p2[:, 0:256], LT[:], Wl[:, 1024:1280], True, True)], [LT, Wl], [p2])
                        self.CP("act", Fg[:], p2[:, 0:256], [p2], [Fg])
                        self.TT("pool", F["t1"][:], r_, k_, ALU.mult, [z], [F["t1"]])
                        self.TT("pool", F["t1"][:], F["t1"][:], rk[:], ALU.mult, [F["t1"], rk], [F["t1"]])
                        P.op("dve", lambda e: e.tensor_reduce(out=s4[:], in_=F["t1"][:].rearrange("p (h f) -> p h f", f=64), axis=AX.X, op=ALU.add), r=[F["t1"]], w=[s4])
                        for h in range(4):
                            self.TSM("pool", Fbon[:, h * 64:(h + 1) * 64], z[:, 512 + h * 64:512 + (h + 1) * 64], s4[:, h:h + 1], [z, s4], [Fbon])
                    self.TT("pool", F["kk"][:], k_, kks[:], ALU.mult, [z, kks], [F["kk"]])
                    self.TT("pool", F["t2"][:], F["kk"][:], F["kk"][:], ALU.mult, [F["kk"]], [F["t2"]])
                    P.op("dve", lambda e: e.tensor_reduce(out=ss[:], in_=F["t2"][:].rearrange("p (h f) -> p h f", f=64), axis=AX.X, op=ALU.add), r=[F["t2"]], w=[ss])
                    self.ACT(ss[:], ss[:], AF.Sqrt, [ss, eps12], [ss], bias=eps12[:, 0:1])
                    P.op("dve", lambda e: e.reciprocal(out=ss[:], in_=ss[:]), r=[ss], w=[ss])
                    for h in range(4):
                        self.TSM("pool", F["kk"][:, h * 64:(h + 1) * 64], F["kk"][:, h * 64:(h + 1) * 64], ss[:, h:h + 1], [F["kk"], ss], [F["kk"]])
                    self.STT(F["t2"][:], F["a"][:], -1.0, ka[:], ALU.add, ALU.mult, [F["a"], ka], [F["t2"]])
                    self.STT(F["kd"][:], F["t2"][:], 1.0, k_, ALU.add, ALU.mult, [F["t2"], z], [F["kd"]])
                    p3 = pg.next()
                    self.MM([(p3[:, 0:256], self.tri[:, d, :], F["logw"][:], True, True)], [self.tri, F["logw"]], [p3])
                    self.CP("act", F["cum"][:], p3[:, 0:256], [p3], [F["cum"]])
                    self.ACT(F["gam"][:], F["cum"][:], AF.Exp, [F["cum"]], [F["gam"]])
                    self.ACT(F["ginv"][:], F["cum"][:], AF.Exp, [F["cum"]], [F["ginv"]], scale=-1.0)
                    self.TT("dve", F["game"][:], F["cum"][:], F["logw"][:], ALU.subtract, [F["cum"], F["logw"]], [F["game"]])
                    self.ACT(F["game"][:], F["game"][:], AF.Exp, [F["game"]], [F["game"]])
                    x4 = X4.next(); vb = Vb.next()
                    self.STT(x4[:, 0, :], F["kk"][:], -1.0, F["game"][:], ALU.mult, ALU.mult, [F["kk"], F["game"]], [x4])
                    self.TT("pool", x4[:, 1, :], r_, F["gam"][:], ALU.mult, [z, F["gam"]], [x4])
                    self.TT("dve", F["t2"][:], F["kk"][:], F["a"][:], ALU.mult, [F["kk"], F["a"]], [F["t2"]])
                    self.TT("dve", x4[:, 2, :], F["t2"][:], F["ginv"][:], ALU.mult, [F["t2"], F["ginv"]], [x4])
                    self.TT("pool", x4[:, 3, :], F["kd"][:], F["ginv"][:], ALU.mult, [F["kd"], F["ginv"]], [x4])
                    self.CP("act", vb[:], v_, [z], [vb])
                    for pr in range(2):
                        p4 = pg.next()
                        self.MM([(p4[:, 0:1], F["logw"][:, pr * 128:(pr + 1) * 128], self.ones[:, 0:1], True, True)], [F["logw"], self.ones], [p4])
                        self.ACT(gcol[pr][:], p4[:, 0:1], AF.Exp, [p4], [gcol[pr]])
                    return dict(x4=x4, vb=vb, Fg=Fg, Fbon=Fbon, gcol=gcol)

                def inv(t, H):
                    x4 = H["x4"]
                    tt = TTt.next(); zp = ZP.next()
                    RbT = RbTr.next(); SK = SKr.next()
                    H["tt"] = tt; H["RbT"] = RbT; H["SK"] = SK
                    def tr(e, x4=x4):
                        for var in range(4):
                            for pr in range(2):
                                i = e.transpose(out=pT[:, var * 2 + pr, :], in_=x4[:, var, pr * 128:(pr + 1) * 128], identity=self.identb[:])
                        return i
                    P.op("pe", tr, r=[x4, self.identb], w=[pT])
                    self.CP("dve", tt[:], pT[:], [pT], [tt])
                    for var in range(2):
                        self.CP("dve", zp[0:64, 0:4:2, var, :], pT[0:64, var * 2:var * 2 + 2, :], [pT], [zp])
                        self.CP("dve", zp[64:128, 1:4:2, var, :], pT[64:128, var * 2:var * 2 + 2, :], [pT], [zp])
                    bv = lambda bk: bk[:].rearrange("p (a b) -> p a b", b=128)
                    bc4 = lambda ap_: ap_.unsqueeze(1).to_broadcast([128, 4, 128])
                    hc = lambda h: slice(h * 128, (h + 1) * 128)
                    pA = pi.next(); pN = pi.next()
                    self.MM([(pA[:, hc(h)], tt[:, 4 + h // 2, :], zp[:, h, 0, :], True, True) for h in range(4)], [tt, zp], [pA])
                    self.MM([(pN[:, hc(h)], zp[:, h, 0, :], tt[:, 4 + h // 2, :], True, True) for h in range(4)], [tt, zp], [pN])
                    self.TT("dve", AdT4[:], bv(pA), bc4(rwm[:, d, 0, :]), ALU.mult, [pA, rwm], [AdT4])
                    self.TT("dve", AdN4[:], bv(pN), bc4(rwm[:, d, 1, :]), ALU.mult, [pN, rwm], [AdN4])
                    for li in range(3):
                        self.TT("dve", LoT4[li][:], bv(pA), bc4(rwm[:, d, 2 + li, :]), ALU.mult, [pA, rwm], [LoT4[li]])
                        self.TT("dve", LoN4[li][:], bv(pN), bc4(rwm[:, d, 5 + li, :]), ALU.mult, [pN, rwm], [LoN4[li]])
                    pB = pi.next()
                    self.MM([(pB[:, hc(h)], tt[:, 4 + h // 2, :], zp[:, h, 1, :], True, True) for h in range(4)], [tt, zp], [pB])
                    self.TT("dve", RbT[:], bv(pB), bc4(cmask[:, d, 128:256]), ALU.mult, [pB, cmask], [RbT])
                    pC = pi.next()
                    self.MM([(pC[:, hc(h)], tt[:, 6 + h // 2, :], zp[:, h, 0, :], True, True) for h in range(4)], [tt, zp], [pC])
                    self.TT("dve", SK[:, 0], bv(pC), bc4(cmask[:, d, 0:128]), ALU.mult, [pC, cmask], [SK])
                    pD = pi.next()
                    self.MM([(pD[:, hc(h)], tt[:, 6 + h // 2, :], zp[:, h, 1, :], True, True) for h in range(4)], [tt, zp], [pD])
                    self.TT("dve", SK[:, 1], bv(pD), bc4(cmask[:, d, 128:256]), ALU.mult, [pD, cmask], [SK])
                    X = AdN4; XT = AdT4
                    Q = RQ.next(); QT = RQT.next()
                    self.TT("dve", Q[:], AdN4[:], bc4(self.identb[:]), ALU.add, [AdN4, self.identb], [Q])
                    self.TT("dve", QT[:], AdT4[:], bc4(self.identb[:]), ALU.add, [AdT4, self.identb], [QT])
                    for k in range(4):
                        if k < 3:
                            pXT = pi.next(); pX = pi.next()
                            self.MM([(pXT[:, hc(h)], X[:, h, :], XT[:, h, :], True, True) for h in range(4)], [X, XT], [pXT])
                            self.MM([(pX[:, hc(h)], XT[:, h, :], X[:, h, :], True, True) for h in range(4)], [X, XT], [pX])
                        if k >= 1:
                            pQT = pi.next(); pQ = pi.next()
                            self.MM([(pQT[:, hc(h)], X[:, h, :], QT[:, h, :], True, True) for h in range(4)], [X, QT], [pQT])
                            self.MM([(pQ[:, hc(h)], XT[:, h, :], Q[:, h, :], True, True) for h in range(4)], [XT, Q], [pQ])
                        if k < 3:
                            nXT = RXT.next(); nX = RX.next()
                            self.CP("act", nXT[:], bv(pXT), [pXT], [nXT])
                            self.CP("act", nX[:], bv(pX), [pX], [nX])
                        if k >= 1:
                            nQT = RQT.next(); nQ = RQ.next()
                            self.TT("dve", nQT[:], bv(pQT), QT[:], ALU.add, [pQT, QT], [nQT])
                            self.TT("dve", nQ[:], bv(pQ), Q[:], ALU.add, [pQ, Q], [nQ])
                            Q = nQ; QT = nQT
                        if k < 3:
                            X = nX; XT = nXT
                    for li in range(3):
                        last = li == 2
                        pE2 = pi.next()
                        self.MM([(pE2[:, hc(h)], LoN4[li][:, h, :], QT[:, h, :], True, True) for h in range(4)], [LoN4[li], QT], [pE2])
                        e2 = RE2.next()
                        self.CP("act", e2[:], bv(pE2), [pE2], [e2])
                        if not last:
                            pE = pi.next()
                            self.MM([(pE[:, hc(h)], LoT4[li][:, h, :], Q[:, h, :], True, True) for h in range(4)], [LoT4[li], Q], [pE])
                            e1 = RE.next()
                            self.CP("act", e1[:], bv(pE), [pE], [e1])
                        pD2 = pi.next()
                        self.MM([(pD2[:, hc(h)], Q[:, h, :], e2[:, h, :], True, True) for h in range(4)], [Q, e2], [pD2])
                        nQT = PTr.next() if last else RQT.next()
                        self.TT("dve", nQT[:], bv(pD2), QT[:], ALU.add, [pD2, QT], [nQT])
                        if not last:
                            pDD = pi.next()
                            self.MM([(pDD[:, hc(h)], QT[:, h, :], e1[:, h, :], True, True) for h in range(4)], [QT, e1], [pDD])
                            nQ = RQ.next()
                            self.TT("dve", nQ[:], bv(pDD), Q[:], ALU.add, [pDD, Q], [nQ])
                            Q = nQ
                        QT = nQT
                    H["QT"] = QT

                def seq(t, H):
                    x4 = H["x4"]; vb = H["vb"]; tt = H["tt"]; Fg = H["Fg"]; Fbon = H["Fbon"]; gcol = H["gcol"]
                    RbT = H["RbT"]; SK = H["SK"]; QT = H["QT"]
                    for pr in range(2):
                        pqx = pq.next()
                        hs = [pr * 2, pr * 2 + 1]
                        sp = [(pqx[:, 0:128], tt[:, 0 + pr, :], Hb[pr][:], True, False)]
                        for hh, h in enumerate(hs):
                            sp.append((pqx[:, hh * 64:(hh + 1) * 64], SK[:, 0, h, :], vb[:, h * 64:(h + 1) * 64], False, hh == 1))
                        self.MM(sp, [tt, Hb[pr], vb, SK], [pqx])
                        self.CP("act", Wb[pr][:], pqx[:, 0:128], [pqx], [Wb[pr]])
                        sp = []
                        for hh, h in enumerate(hs):
                            sp.append((pqx[:, 128 + hh * 64:128 + (hh + 1) * 64], QT[:, h, :], Wb[pr][:, hh * 64:(hh + 1) * 64], True, True))
                        self.MM(sp, [Wb[pr], QT], [pqx])
                        self.CP("dve", Ub[pr][:], pqx[:, 128:256], [pqx], [Ub[pr]])
                        sp = [(pqx[:, 256:384], tt[:, 2 + pr, :], Hb[pr][:], True, False)]
                        for hh, h in enumerate(hs):
                            cs = slice(256 + hh * 64, 256 + (hh + 1) * 64)
                            sp.append((pqx[:, cs], RbT[:, h, :], Ub[pr][:, hh * 64:(hh + 1) * 64], False, False))
                            sp.append((pqx[:, cs], SK[:, 1, h, :], vb[:, h * 64:(h + 1) * 64], False, hh == 1))
                        self.MM(sp, [tt, Hb[pr], Ub[pr], vb, RbT, SK], [pqx])
                        dst = hacc[t][:, pr * 128:(pr + 1) * 128]
                        if d == 0:
                            self.CP("act", dst, pqx[:, 256:384], [pqx], [hacc[t]])
                        else:
                            self.TT("dve", dst, pqx[:, 256:384], dst, ALU.add, [pqx, hacc[t]], [hacc[t]])
                        self.MM([(pqx[:, 384:512], x4[:, 2, pr * 128:(pr + 1) * 128], Ub[pr][:], True, False),
                                 (pqx[:, 384:512], x4[:, 3, pr * 128:(pr + 1) * 128], vb[:, pr * 128:(pr + 1) * 128], False, True)],
                                [x4, Ub[pr], vb], [pqx])
                        for hh in range(2):
                            blk = slice(hh * 64, (hh + 1) * 64); cb_ = slice(384 + hh * 64, 384 + (hh + 1) * 64)
                            self.TT("dve", htmp[blk, blk], pqx[blk, cb_], Hf[pr][blk, blk], ALU.add, [pqx, Hf[pr]], [htmp])
                            self.TSM("dve", Hf[pr][blk, blk], htmp[blk, blk], gcol[pr][blk, 0:1], [htmp, gcol[pr]], [Hf[pr]])
                        self.CP("act", Hb[pr][:], Hf[pr][:], [Hf[pr]], [Hb[pr]])
                    if d == 1:
                        self.finish_heads(ht, hacc[t], Fg[:], None, ng, nb, t, 768, extra=Fbon)
                order_ = self.order(d)
                n_ = len(order_)
                Hs = {}
                for step in range(n_ + 2):
                    if step < n_:
                        Hs[step] = prep(order_[step])
                    if 0 <= step - 1 < n_:
                        inv(order_[step - 1], Hs[step - 1])
                    if 0 <= step - 2 < n_:
                        seq(order_[step - 2], Hs.pop(step - 2))
            P.barrier()

    def mix_mlstm(self, s, l):
        P, S, I = self.P, self.S, self.I
        with ExitStack() as st:
            ng = self.bc(st, I["out_norm_g"][l, 0:256], 256); nb = self.bc(st, I["out_norm_b"][l, 0:256], 256)
            cw = self.sbt(st, "cw", [128, 3, 512]); cb = self.bc(st, I["mlstm_conv_b"][l], 512)
            for tap in range(3):
                P.dma(cw[:, tap, :], I["mlstm_conv_w"][l, tap].partition_broadcast(128), w=[cw])
            gb = self.bc(st, I["mlstm_gate_b"][l].rearrange("a b -> (a b)"), 16)
            tri4 = self.sbt(st, "tri4", [128, 4, 4, 128])
            P.dma(tri4[:], I["k_tri4"], w=[tri4])
            hacc = [self.sbt(st, "hacc", [128, 256]) for _ in range(NT)]
            ht = self.head_tiles(st)
            q3 = Rot([self.sbt(st, "q3", [128, 3, 512]) for _ in range(3)])
            rest = Rot([self.sbt(st, "rest", [128, 528]) for _ in range(3)])
            acc = self.sbt(st, "acc", [128, 512]); acc2 = self.sbt(st, "acc2", [128, 512])
            qkb = Rot([self.sbt(st, "qkb", [128, 512], BF16) for _ in range(2)])
            va = Rot([self.sbt(st, "va", [128, 4, 65], BF16) for _ in range(2)])
            TT = Rot([self.sbt(st, "TT", [128, 4, 128], BF16) for _ in range(2)])
            AM = Rot([self.sbt(st, "AM", [128, 4, 128], BF16) for _ in range(2)])
            TQ = Rot([self.sbt(st, "TQ", [128, 4, 128], BF16) for _ in range(2)])
            for tq_ in TQ.items:
                P.op("pool", lambda e, tq_=tq_: e.memset(tq_[:], 0.0), w=[tq_])
            G = self.sbt(st, "G", [128, 16]); lf = self.sbt(st, "lf", [128, 4])
            EE = Rot([self.sbt(st, "ee", [128, 4]) for _ in range(2)]); SC8 = Rot([self.sbt(st, "sc8", [128, 8]) for _ in range(2)]); ZS = Rot([self.sbt(st, "zs", [128, 4]) for _ in range(2)])
            Z = self.sbt(st, "Z", [128, 4]); rZ = self.sbt(st, "rZ", [128, 4]); wk = self.sbt(st, "wk", [128, 4]); wi = self.sbt(st, "wi", [128, 4])
            thr = self.sbt(st, "thr", [128, 4]); Em = self.sbt(st, "Em", [128, 4]); dm = self.sbt(st, "dm", [128, 4]); hsc = self.sbt(st, "hsc", [128, 256])
            CN = [self.sbt(st, "CN", [128, 130]) for _ in range(2)]
            CNw = [self.sbt(st, "CNw", [128, 130]) for _ in range(2)]
            CNb = [Rot([self.sbt(st, "CNb", [128, 130], BF16) for _ in range(2)]) for _ in range(2)]
            ptt = Rot([self.pst(st, "ptt", [128, 8, 128], BF16) for _ in range(1)])
            psc = Rot([self.bank(st, "psc") for _ in range(2)])
            pO = Rot([self.bank(st, "pO") for _ in range(2)])
            pSt = Rot([self.bank(st, "pSt") for _ in range(1)])
            pg = Rot([self.bank(st, "pg") for _ in range(2)])
            for d in range(2):
                for pr in range(2):
                    P.op("pool", lambda e, pr=pr: e.memset(CN[pr][:], 0.0), w=[CN[pr]])
                    P.op("pool", lambda e, pr=pr: e.memset(CNw[pr][:], 0.0), w=[CNw[pr]])
                P.op("pool", lambda e: e.memset(Em[:], 1.0), w=[Em])
                def prep(t):
                    q = q3.next(); rs = rest.next(); qk = qkb.next()
                    sc8 = SC8.next(); ee = EE.next(); zs = ZS.next()
                    rb = rowbase(t)
                    for tap in range(3):
                        P.dma(q[:, tap, :], S["proj"][rb - 1 + tap:rb - 1 + tap + 128, 0:512], w=[q])
                    P.dma(rs[:], S["proj"][rb:rb + 128, 512:1040], w=[rs])
                    P.op("dve", lambda e, q=q: e.tensor_tensor(out=acc[:], in0=q[:, 0, :], in1=cw[:, 0, :], op=ALU.mult), r=[q, cw], w=[acc])
                    P.op("pool", lambda e, q=q: e.tensor_tensor(out=acc2[:], in0=q[:, 1, :], in1=cw[:, 1, :], op=ALU.mult), r=[q, cw], w=[acc2])
                    P.op("dve", lambda e: e.tensor_tensor(out=acc[:], in0=acc[:], in1=acc2[:], op=ALU.add), r=[acc, acc2], w=[acc])
                    P.op("pool", lambda e, q=q: e.tensor_tensor(out=acc2[:], in0=q[:, 2, :], in1=cw[:, 2, :], op=ALU.mult), r=[q, cw], w=[acc2])
                    P.op("dve", lambda e: e.tensor_tensor(out=acc[:], in0=acc[:], in1=acc2[:], op=ALU.add), r=[acc, acc2], w=[acc])
                    P.op("dve", lambda e: e.tensor_tensor(out=acc[:], in0=acc[:], in1=cb[:], op=ALU.add), r=[acc, cb], w=[acc])
                    P.op("act", lambda e: e.activation(out=acc[:], in_=acc[:], func=AF.Silu), r=[acc], w=[acc])
                    P.op("act", lambda e, qk=qk: e.mul(out=qk[:, 0:256], in_=acc[:, 0:256], mul=0.125), r=[acc], w=[qk])
                    P.op("dve", lambda e, qk=qk: e.tensor_copy(out=qk[:, 256:512], in_=acc[:, 256:512]), r=[acc], w=[qk])
                    P.op("dve", lambda e, rs=rs: e.tensor_tensor(out=G[:], in0=rs[:, 512:528], in1=gb[:], op=ALU.add), r=[rs, gb], w=[G])
                    igs = G[:, d * 8:d * 8 + 4]; fps = G[:, d * 8 + 4:d * 8 + 8]
                    P.op("act", lambda e, fps=fps: e.activation(out=lf[:], in_=fps, func=AF.Exp, scale=-1.0), r=[G], w=[lf])
                    P.op("act", lambda e: e.activation(out=lf[:], in_=lf[:], func=AF.Ln, bias=self.ones[:, 0:1], scale=1.0), r=[lf, self.ones], w=[lf])
                    P.op("dve", lambda e: e.tensor_scalar_mul(out=lf[:], in0=lf[:], scalar1=-1.0), r=[lf], w=[lf])
                    g1 = pg.next()

                    def gm(e, g1=g1, d=d):
                        e.matmul(g1[:, 0:4], lhsT=self.tri[:, d, :], rhs=lf[:], start=True, stop=True)
                        return e.matmul(g1[:, 4:8], lhsT=self.ones[:], rhs=lf[:], start=True, stop=True)
                    P.op("pe", gm, r=[lf, self.tri, self.ones], w=[g1])
                    P.op("dve", lambda e, g1=g1: e.tensor_copy(out=sc8[:], in_=g1[:, 0:8]), r=[g1], w=[sc8])
                    P.op("dve", lambda e, igs=igs: e.tensor_tensor(out=ee[:], in0=igs, in1=sc8[:, 0:4], op=ALU.subtract), r=[G, sc8], w=[ee])
                    P.op("act", lambda e: e.activation(out=ee[:], in_=ee[:], func=AF.Exp), r=[ee], w=[ee])
                    g2 = pg.next()
                    P.op("pe", lambda e, g2=g2: e.matmul(g2[:, 0:4], lhsT=self.ones[:], rhs=ee[:], start=True, stop=True), r=[ee, self.ones], w=[g2])
                    P.op("act", lambda e, g2=g2: e.copy(out=zs[:], in_=g2[:, 0:4]), r=[g2], w=[zs])
                    return dict(rs=rs, qk=qk, sc8=sc8, ee=ee, zs=zs)

                def chain(t, H):
                    rs = H["rs"]; qk = H["qk"]; sc8 = H["sc8"]; ee = H["ee"]; zs = H["zs"]
                    v = va.next(); tt = TT.next(); am = AM.next()
                    P.op("dve", lambda e: e.tensor_tensor(out=Z[:], in0=zs[:], in1=Em[:], op=ALU.add), r=[zs, Em], w=[Z])
                    P.op("dve", lambda e: e.reciprocal(out=rZ[:], in_=Z[:]), r=[Z], w=[rZ])
                    P.op("dve", lambda e: e.tensor_tensor(out=wk[:], in0=ee[:], in1=rZ[:], op=ALU.mult), r=[ee, rZ], w=[wk])
                    P.op("dve", lambda e: e.tensor_tensor(out=wi[:], in0=Em[:], in1=rZ[:], op=ALU.mult), r=[Em, rZ], w=[wi])
                    P.op("act", lambda e: e.activation(out=thr[:], in_=sc8[:, 0:4], func=AF.Exp, scale=-1.0), r=[sc8], w=[thr])
                    P.op("dve", lambda e: e.tensor_tensor(out=thr[:], in0=thr[:], in1=rZ[:], op=ALU.mult), r=[thr, rZ], w=[thr])
                    P.op("act", lambda e: e.activation(out=Em[:], in_=sc8[:, 4:8], func=AF.Exp), r=[sc8, wi], w=[Em])
                    P.op("dve", lambda e: e.tensor_tensor(out=Em[:], in0=Em[:], in1=Z[:], op=ALU.mult), r=[Em, Z], w=[Em])
                    for h in range(4):
                        P.op("pool", lambda e, rs=rs, v=v, h=h: e.tensor_scalar_mul(out=v[:, h, 0:64], in0=rs[:, h * 64:(h + 1) * 64], scalar1=wk[:, h:h + 1]), r=[rs, wk], w=[v])
                    P.op("pool", lambda e, v=v: e.tensor_copy(out=v[:, :, 64], in_=wk[:]), r=[wk], w=[v])
                    ptb = ptt.next(); pt = ptb[:, 0:4, :]

                    def tr(e, qk=qk, pt=pt):
                        for k in range(4):
                            i = e.transpose(out=pt[:, k, :], in_=qk[:, k * 128:(k + 1) * 128], identity=self.identb[:])
                        return i
                    P.op("pe", tr, r=[qk, self.identb], w=[ptb])
                    P.op("dve", lambda e, pt=pt, tt=tt: e.tensor_copy(out=tt[:], in_=pt), r=[ptb], w=[tt])
                    tq = TQ.next()
                    P.op("dve", lambda e, pt=pt, tq=tq: e.tensor_copy(out=tq[0:64, 0:4:2, :], in_=pt[0:64, 0:2, :]), r=[ptb], w=[tq])
                    P.op("dve", lambda e, pt=pt, tq=tq: e.tensor_copy(out=tq[64:128, 1:4:2, :], in_=pt[64:128, 0:2, :]), r=[ptb], w=[tq])
                    psb = psc.next(); ps = psb[:].rearrange("p (a b) -> p a b", b=128)

                    def sc(e, tt=tt, ps=ps, tq=tq):
                        for h in range(4):
                            i = e.matmul(ps[:, h, :], lhsT=tt[:, 2 + h // 2, :], rhs=tq[:, h, :], start=True, stop=True)
                        return i
                    P.op("pe", sc, r=[tt, tq], w=[psb])
                    P.op("dve", lambda e, ps=ps, am=am, d=d: e.tensor_tensor(out=am[:], in0=ps, in1=tri4[:, d], op=ALU.mult), r=[psb, tri4], w=[am])
                    po = pO.next()
                    for pr in range(2):
                        pS = pSt.next(); cb_ = CNb[pr].next()
                        c0 = pr * 130
                        for hh in range(2):
                            h = pr * 2 + hh
                            blk = slice(hh * 64, (hh + 1) * 64); cblk = slice(hh * 65, (hh + 1) * 65)
                            P.op("dve", lambda e, pr=pr, blk=blk, cblk=cblk, h=h: e.tensor_scalar_mul(out=CNw[pr][blk, cblk], in0=CN[pr][blk, cblk], scalar1=wi[blk, h:h + 1]),
                                 r=[CN[pr], wi], w=[CNw[pr]])
                        P.op("act", lambda e, pr=pr, cb_=cb_: e.copy(out=cb_[:], in_=CNw[pr][:]), r=[CNw[pr]], w=[cb_])

                        def mo(e, tt=tt, am=am, v=v, po=po, pr=pr, cb_=cb_, c0=c0):
                            e.matmul(po[:, c0:c0 + 130], lhsT=tt[:, pr, :], rhs=cb_[:], start=True, stop=False)
                            for hh in range(2):
                                h = pr * 2 + hh
                                i = e.matmul(po[:, c0 + hh * 65:c0 + (hh + 1) * 65], lhsT=am[:, h, :], rhs=v[:, h, :], start=False, stop=(hh == 1))
                            return i
                        P.op("pe", mo, r=[tt, am, v, cb_], w=[(po, pr)])
                        P.op("pe", lambda e, qk=qk, v=v, pS=pS, pr=pr: e.matmul(pS[:, 0:130], lhsT=qk[:, 256 + pr * 128:256 + (pr + 1) * 128],
                                                                                 rhs=v[:, pr * 2:pr * 2 + 2, :].rearrange("p a b -> p (a b)"), start=True, stop=True), r=[qk, v], w=[pS])
                        for hh in range(2):
                            blk = slice(hh * 64, (hh + 1) * 64); cblk = slice(hh * 65, (hh + 1) * 65)
                            P.op("dve", lambda e, pS=pS, pr=pr, blk=blk, cblk=cblk: e.tensor_tensor(out=CN[pr][blk, cblk], in0=CNw[pr][blk, cblk], in1=pS[blk, cblk], op=ALU.add),
                                 r=[pS, CNw[pr]], w=[CN[pr]])
                    pov = po[:, 0:260].rearrange("p (a b) -> p a b", b=65)
                    pok = [(po, 0), (po, 1)]
                    P.op("act", lambda e, pov=pov: e.activation(out=dm[:], in_=pov[:, :, 64], func=AF.Abs), r=pok, w=[dm])
                    P.op("dve", lambda e: e.tensor_tensor(out=dm[:], in0=dm[:], in1=thr[:], op=ALU.max), r=[dm, thr], w=[dm])
                    P.op("dve", lambda e: e.reciprocal(out=dm[:], in_=dm[:]), r=[dm], w=[dm])
                    dmb = dm[:].unsqueeze(2).to_broadcast([128, 4, 64])
                    hv = hacc[t][:].rearrange("p (h f) -> p h f", f=64)
                    if d == 0:
                        P.op("dve", lambda e, pov=pov, hv=hv, dmb=dmb: e.tensor_tensor(out=hv, in0=pov[:, :, 0:64], in1=dmb, op=ALU.mult), r=pok + [dm], w=[hacc[t]])
                    else:
                        P.op("dve", lambda e, pov=pov, dmb=dmb: e.tensor_tensor(out=hsc[:].rearrange("p (h f) -> p h f", f=64), in0=pov[:, :, 0:64], in1=dmb, op=ALU.mult), r=pok + [dm], w=[hsc])
                        P.op("pool", lambda e, t=t: e.tensor_tensor(out=hacc[t][:], in0=hacc[t][:], in1=hsc[:], op=ALU.add), r=[hacc[t], hsc], w=[hacc[t]])
                    if d == 1:
                        gate = rs[:, 256:512]
                        self.finish_heads(ht, hacc[t], gate, AF.Sigmoid, ng, nb, t, 0)
                order_ = self.order(d)
                hn_ = prep(order_[0])
                for i_, t in enumerate(order_):
                    hc_ = hn_
                    if i_ + 1 < len(order_):
                        hn_ = prep(order_[i_ + 1])
                    chain(t, hc_)
            P.barrier()

    def mix_retention(self, s, l):
        P, S, I = self.P, self.S, self.I
        C0 = 1808
        with ExitStack() as st:
            ng = self.bc(st, I["out_norm_g"][l, 512:768], 256); nb = self.bc(st, I["out_norm_b"][l, 512:768], 256)
            mask = self.sbt(st, "rtmask", [128, 2, 4, 128]); dec = self.sbt(st, "rtdec", [128, 24])
            P.dma(mask[:], I["k_rtmask"], w=[mask]); P.dma(dec[:], I["k_rtdec"], w=[dec])
            hacc = [self.sbt(st, "hacc", [128, 256]) for _ in range(NT)]
            ht = self.head_tiles(st)
            raw = Rot([self.sbt(st, "raw", [128, 1024]) for _ in range(3)])
            rope = Rot([self.sbt(st, "rope", [128, 2, 256]) for _ in range(3)])
            rt1 = self.sbt(st, "rt1", [128, 256]); rt2 = self.sbt(st, "rt2", [128, 256])
            qkb = Rot([self.sbt(st, "qkb", [128, 512], BF16) for _ in range(2)])
            vb = Rot([self.sbt(st, "vb", [128, 256], BF16) for _ in range(2)])
            vd = Rot([self.sbt(st, "vd", [128, 256], BF16) for _ in range(2)])
            TT = Rot([self.sbt(st, "TT", [128, 4, 128], BF16) for _ in range(2)])
            AM = Rot([self.sbt(st, "AM", [128, 4, 128], BF16) for _ in range(2)])
            TQ = Rot([self.sbt(st, "TQ", [128, 4, 128], BF16) for _ in range(2)])
            for tq_ in TQ.items:
                P.op("pool", lambda e, tq_=tq_: e.memset(tq_[:], 0.0), w=[tq_])
            o1 = Rot([self.sbt(st, "o1", [128, 128]) for _ in range(2)])
            Sf = [self.sbt(st, "Sf", [128, 128]) for _ in range(2)]
            Sb = [Rot([self.sbt(st, "Sb", [128, 128], BF16) for _ in range(2)]) for _ in range(2)]
            ptt = Rot([self.pst(st, "ptt", [128, 8, 128], BF16) for _ in range(1)])
            psc = Rot([self.bank(st, "psc") for _ in range(2)])
            pO1 = Rot([self.bank(st, "pO1") for _ in range(2)])
            pO2 = Rot([self.bank(st, "pO2") for _ in range(2)])
            pSt = Rot([self.bank(st, "pSt") for _ in range(1)])
            for d in range(2):
                sbc = []
                for pr in range(2):
                    P.op("pool", lambda e, pr=pr: e.memset(Sf[pr][:], 0.0), w=[Sf[pr]])
                    b0 = Sb[pr].next()
                    P.op("pool", lambda e, b0=b0: e.memset(b0[:], 0.0), w=[b0])
                    sbc.append(b0)
                def prep(t):
                    rw = raw.next(); qk = qkb.next(); v = vb.next(); vdd = vd.next()
                    P.dma(rw[:], S["proj"][rowbase(t):rowbase(t) + 128, C0:C0 + 1024], w=[rw])
                    if t >= 2:
                        rp = rope.next()
                        P.dma(rp[:], I["k_rope"][(t - 2) * 128:(t - 1) * 128], w=[rp])
                        for qi in range(2):
                            src = rw[:, qi * 256:(qi + 1) * 256]
                            sv = src.rearrange("p (a two f) -> p a two f", two=2, f=16)
                            snv = rp[:, 1, :].rearrange("p (a two f) -> p a two f", two=2, f=16)
                            t1v = rt1[:].rearrange("p (a two f) -> p a two f", two=2, f=16)
                            P.op("pool", lambda e, sv=sv, snv=snv, t1v=t1v: e.tensor_tensor(out=t1v[:, :, 0, :], in0=sv[:, :, 1, :], in1=snv[:, :, 0, :], op=ALU.mult), r=[rw, rp], w=[rt1])
                            P.op("pool", lambda e, sv=sv, snv=snv, t1v=t1v: e.tensor_tensor(out=t1v[:, :, 1, :], in0=sv[:, :, 0, :], in1=snv[:, :, 1, :], op=ALU.mult), r=[rw, rp], w=[rt1])
                            P.op("dve", lambda e, src=src, rp=rp: e.tensor_tensor(out=rt2[:], in0=src, in1=rp[:, 0, :], op=ALU.mult), r=[rw, rp], w=[rt2])
                            P.op("dve", lambda e: e.tensor_tensor(out=rt2[:], in0=rt2[:], in1=rt1[:], op=ALU.add), r=[rt1, rt2], w=[rt2])
                            P.op("act", lambda e, qi=qi, qk=qk: e.mul(out=qk[:, qi * 256:(qi + 1) * 256], in_=rt2[:], mul=(0.125 if qi == 0 else 1.0)), r=[rt2], w=[qk])
                    else:
                        P.op("act", lambda e, rw=rw, qk=qk: e.mul(out=qk[:, 0:256], in_=rw[:, 0:256], mul=0.125), r=[rw], w=[qk])
                        P.op("act", lambda e, rw=rw, qk=qk: e.copy(out=qk[:, 256:512], in_=rw[:, 256:512]), r=[rw], w=[qk])
                    P.op("dve", lambda e, rw=rw, v=v: e.tensor_copy(out=v[:], in_=rw[:, 512:768]), r=[rw], w=[v])
                    for h in range(4):
                        P.op("pool", lambda e, rw=rw, vdd=vdd, h=h, d=d: e.tensor_scalar_mul(out=vdd[:, h * 64:(h + 1) * 64], in0=rw[:, 512 + h * 64:512 + (h + 1) * 64],
                                                                                       scalar1=dec[:, 8 + d * 4 + h:8 + d * 4 + h + 1]), r=[rw, dec], w=[vdd])
                    return dict(rw=rw, qk=qk, v=v, vdd=vdd)

                def chain(t, H):
                    rw = H["rw"]; qk = H["qk"]; v = H["v"]; vdd = H["vdd"]
                    tt = TT.next(); am = AM.next()
                    ptb = ptt.next(); pt = ptb[:, 0:4, :]

                    def tr(e, qk=qk, pt=pt):
                        for k in range(4):
                            i = e.transpose(out=pt[:, k, :], in_=qk[:, k * 128:(k + 1) * 128], identity=self.identb[:])
                        return i
                    P.op("pe", tr, r=[qk, self.identb], w=[ptb])
                    P.op("dve", lambda e, pt=pt, tt=tt: e.tensor_copy(out=tt[:], in_=pt), r=[ptb], w=[tt])
                    tq = TQ.next()
                    P.op("dve", lambda e, pt=pt, tq=tq: e.tensor_copy(out=tq[0:64, 0:4:2, :], in_=pt[0:64, 0:2, :]), r=[ptb], w=[tq])
                    P.op("dve", lambda e, pt=pt, tq=tq: e.tensor_copy(out=tq[64:128, 1:4:2, :], in_=pt[64:128, 0:2, :]), r=[ptb], w=[tq])
                    psb = psc.next(); ps = psb[:].rearrange("p (a b) -> p a b", b=128)

                    def sc(e, tt=tt, ps=ps, tq=tq):
                        for h in range(4):
                            i = e.matmul(ps[:, h, :], lhsT=tt[:, 2 + h // 2, :], rhs=tq[:, h, :], start=True, stop=True)
                        return i
                    P.op("pe", sc, r=[tt, tq], w=[psb])
                    P.op("dve", lambda e, ps=ps, am=am, d=d: e.tensor_tensor(out=am[:], in0=ps, in1=mask[:, d], op=ALU.mult), r=[psb, mask], w=[am])
                    for pr in range(2):
                        p1 = pO1.next(); p2 = pO2.next(); pS = pSt.next(); oo = o1.next()
                        sb_old = sbc[pr]

                        def m1(e, am=am, v=v, p1=p1, pr=pr):
                            for hh in range(2):
                                h = pr * 2 + hh
                                i = e.matmul(p1[:, hh * 64:(hh + 1) * 64], lhsT=am[:, h, :], rhs=v[:, h * 64:(h + 1) * 64], start=True, stop=True)
                            return i
                        P.op("pe", m1, r=[am, v], w=[p1])
                        P.op("pe", lambda e, tt=tt, p2=p2, pr=pr, sb_old=sb_old: e.matmul(p2[:, 0:128], lhsT=tt[:, pr, :], rhs=sb_old[:], start=True, stop=True), r=[tt, sb_old], w=[p2])
                        P.op("pe", lambda e, qk=qk, vdd=vdd, pS=pS, pr=pr: e.matmul(pS[:, 0:128], lhsT=qk[:, 256 + pr * 128:256 + (pr + 1) * 128], rhs=vdd[:, pr * 128:(pr + 1) * 128], start=True, stop=True),
                             r=[qk, vdd], w=[pS])
                        P.op("act", lambda e, p1=p1, oo=oo: e.copy(out=oo[:], in_=p1[:, 0:128]), r=[p1], w=[oo])
                        for hh in range(2):
                            h = pr * 2 + hh
                            dst = hacc[t][:, h * 64:(h + 1) * 64]
                            P.op("dve", lambda e, p2=p2, oo=oo, hh=hh, h=h, d=d, dst=dst: e.scalar_tensor_tensor(out=(oo[:, hh * 64:(hh + 1) * 64] if d == 1 else dst), in0=p2[:, hh * 64:(hh + 1) * 64],
                                                                                                             scalar=dec[:, d * 4 + h:d * 4 + h + 1], in1=oo[:, hh * 64:(hh + 1) * 64], op0=ALU.mult, op1=ALU.add),
                                 r=[p2, oo, dec], w=[oo if d == 1 else hacc[t]])
                            if d == 1:
                                P.op("pool", lambda e, oo=oo, hh=hh, dst=dst: e.tensor_tensor(out=dst, in0=dst, in1=oo[:, hh * 64:(hh + 1) * 64], op=ALU.add), r=[oo, hacc[t]], w=[hacc[t]])
                            blk = slice(hh * 64, (hh + 1) * 64)
                            P.op("dve", lambda e, pS=pS, pr=pr, blk=blk, h=h, d=d: e.scalar_tensor_tensor(out=Sf[pr][blk, blk], in0=Sf[pr][blk, blk], scalar=dec[blk, 16 + d * 4 + h:16 + d * 4 + h + 1],
                                                                                                     in1=pS[blk, blk], op0=ALU.mult, op1=ALU.add), r=[pS, Sf[pr], dec], w=[Sf[pr]])
                        nb_ = Sb[pr].next()
                        P.op("act", lambda e, nb_=nb_, pr=pr: e.copy(out=nb_[:], in_=Sf[pr][:]), r=[Sf[pr]], w=[nb_])
                        sbc[pr] = nb_
                    if d == 1:
                        gate = rw[:, 768:1024]
                        self.finish_heads(ht, hacc[t], gate, AF.Silu, ng, nb, t, 512)
                for t in self.order(d):
                    chain(t, prep(t))
            P.barrier()

    def ln_mod(self, s, l, uT, kind):
        P = self.P
        with ExitStack() as st:
            stats = self.sbt(st, "stats", [128, 2, 6])
            mv = self.sbt(st, "mv", [128, 2])
            rstd = self.sbt(st, "rstd", [128, 1])
            utmp = self.sbt(st, "utmp", [128, 8, 128])
            xn = Rot([self.sbt(st, "xn", [128, D], BF16) for _ in range(2)])
            pt = Rot([self.pst(st, "ptr", [128, 8, 128], BF16) for _ in range(2)])
            for t in range(2 if (kind == 1 and l == self.last_layer) else 0, NT):
                r = 4 if t < 2 else s
                x = self.xres[t]
                a = xn.next(); p = pt.next()
                self.ln_stats(x, stats, mv, rstd)
                P.op("dve", lambda e, a=a, x=x: e.tensor_scalar(out=a[:], in0=x[:], scalar1=mv[:, 0:1], scalar2=rstd[:, 0:1],
                                                                op0=ALU.subtract, op1=ALU.mult), r=[x, mv, rstd], w=[a])

                def tr(e, a=a, p=p):
                    for k in range(8):
                        i = e.transpose(out=p[:, k, :], in_=a[:, k * 128:(k + 1) * 128], identity=self.identb[:])
                    return i
                P.op("pe", tr, r=[a, self.identb], w=[p])
                sc = self.modp[:, l, r, 2 * kind + 1, :].unsqueeze(2).to_broadcast([128, 8, 128])
                sh = self.modp[:, l, r, 2 * kind, :].unsqueeze(2).to_broadcast([128, 8, 128])
                dst = uT[:, :, t * 128:(t + 1) * 128]
                P.op("dve", lambda e, p=p, sc=sc: e.tensor_tensor(out=utmp[:], in0=p[:], in1=sc, op=ALU.mult), r=[p, self.modp], w=[utmp])
                P.op("dve", lambda e, sh=sh, dst=dst: e.tensor_tensor(out=dst, in0=utmp[:], in1=sh, op=ALU.add), r=[utmp, self.modp], w=[(uT, t)])
            P.barrier()

    def ln_stats(self, x, stats, mv, rstd, width=D):
        P = self.P
        ng = width // 512
        for g in range(ng):
            P.op("dve", lambda e, g=g: e.bn_stats(out=stats[:, g, :], in_=x[:, g * 512:(g + 1) * 512]), r=[x], w=[stats])
        P.op("dve", lambda e: e.bn_aggr(out=mv[:], in_=stats[:, 0:ng, :].rearrange("p g s -> p (g s)")), r=[stats], w=[mv])
        P.op("act", lambda e: e.activation(out=rstd[:], in_=mv[:, 1:2], func=AF.Sqrt, bias=self.epsc[:, 0:1], scale=1.0), r=[mv, self.epsc], w=[rstd])
        P.op("dve", lambda e: e.reciprocal(out=rstd[:], in_=rstd[:]), r=[rstd], w=[rstd])

    def in_proj(self, l, uT):
        P, S = self.P, self.S
        with ExitStack() as st:
            wch = Rot([self.sbt(st, "wch", [128, 8, 512], BF16) for _ in range(3)])
            stg = Rot([self.sbt(st, "stg", [128, 512]) for _ in range(4)])
            ps = Rot([self.pst(st, "pps", [128, 512]) for _ in range(4)])
            ev = Rot(["act", "act", "act", "dve"])
            for g in range(8):
                c0 = g * 512
                nco = min(512, DPROJ - c0)
                w = wch.next()
                P.dma(w[:, :, 0:nco], S["wb_in%d" % l][:, c0:c0 + nco].rearrange("(k p) n -> p k n", p=128), w=[w])
                for t in range(NT):
                    p = ps.next(); sg = stg.next()

                    def mm(e, p=p, w=w, t=t, nco=nco):
                        for k in range(8):
                            i = e.matmul(p[:, 0:nco], lhsT=uT[:, k, t * 128:(t + 1) * 128], rhs=w[:, k, 0:nco], start=(k == 0), stop=(k == 7))
                        return i
                    P.op("pe", mm, r=[w, (uT, t)], w=[p])
                    eve = ev.next()
                    if eve == "act":
                        P.op("act", lambda e, p=p, sg=sg, nco=nco: e.copy(out=sg[:, 0:nco], in_=p[:, 0:nco]), r=[p], w=[sg])
                    else:
                        P.op("dve", lambda e, p=p, sg=sg, nco=nco: e.tensor_copy(out=sg[:, 0:nco], in_=p[:, 0:nco]), r=[p], w=[sg])
                    P.dma(S["proj"][rowbase(t):rowbase(t) + 128, c0:c0 + nco], sg[:, 0:nco], r=[sg], eng="act")
            P.barrier()


_CACHE = {}


def kernel(**inputs):
    consts = make_consts()
    n_cores = 8
    if "nc" not in _CACHE:
        _CACHE["nc"] = Builder(n_seq=4).build(consts)
    nc = _CACHE["nc"]
    in_maps = []
    for cidx in range(n_cores):
        m = {}
        for k, v in inputs.items():
            v = np.asarray(v)
            if k in ("x", "c", "ctx"):
                m[k] = np.ascontiguousarray(v[cidx * 4:(cidx + 1) * 4])
            else:
                m[k] = np.ascontiguousarray(v)
        for k, v in consts.items():
            m["k_" + k] = v
        in_maps.append(m)
    res = run_bass_kernel_spmd(nc, in_maps, core_ids=list(range(n_cores)))
    return np.concatenate([r["out"] for r in res.results], axis=0).astype(np.float32)
```
